# Optimizing a Trainium2 kernel written in Bass

```python
import jax, jax.numpy as jnp
from jax import lax
import numpy as np

D_MODEL = 1024
BATCH = 2
SEQ = 16384
DEPTH = 4

CONV_DIM = 512
CONV_WIDTH = 31
SC_DIM = 512
SC_WIDTH = 3
MLA_HEADS = 8
QK_NOPE_DIM = 64
QK_ROPE_DIM = 32
V_HEAD_DIM = 64
Q_LORA_RANK = 256
KV_LORA_RANK = 128
ROPE_THETA = 10000.0
ATTN_BLOCK = 128
POOL_WINDOWS = (2, 4, 8, 16)
POOL_GROUPS = 4
POOL_DIM = 512
POOL_GROUP_DIM = POOL_DIM // POOL_GROUPS
N_BRANCHES = 4
D_FF = 2816
FFN_CONV_WIDTH = 3
LN_EPS = 1e-5
RMS_EPS = 1e-6
DEEPNORM_ALPHA = (2.0 * DEPTH) ** 0.25
DEEPNORM_BETA = (8.0 * DEPTH) ** -0.25

OFF_CONV_A = CONV_DIM
OFF_CONV_B = OFF_CONV_A + CONV_DIM
OFF_SC_B = OFF_CONV_B + SC_DIM
OFF_SC_C = OFF_SC_B + SC_DIM
OFF_SC_X = OFF_SC_C + SC_DIM
OFF_Q_LAT = OFF_SC_X + Q_LORA_RANK
OFF_KV_LAT = OFF_Q_LAT + KV_LORA_RANK
OFF_K_ROPE = OFF_KV_LAT + QK_ROPE_DIM
OFF_POOL = OFF_K_ROPE + POOL_DIM
IN_COLS = OFF_POOL + N_BRANCHES * D_MODEL
IN_OFFSETS = (OFF_CONV_A, OFF_CONV_B, OFF_SC_B, OFF_SC_C, OFF_SC_X, OFF_Q_LAT, OFF_KV_LAT, OFF_K_ROPE, OFF_POOL)

kernel_name = 'hybrid_gated_conv_mla_pool_deepnorm'


def layer_norm(x, g, b):
    xf = x.astype(jnp.float32)
    mu = jnp.mean(xf, axis=-1, keepdims=True)
    var = jnp.mean(jnp.square(xf - mu), axis=-1, keepdims=True)
    y = (xf - mu) * lax.rsqrt(var + LN_EPS)
    return (y * g.astype(jnp.float32) + b.astype(jnp.float32)).astype(x.dtype)


def rms_norm(x, g):
    xf = x.astype(jnp.float32)
    y = xf * lax.rsqrt(jnp.mean(jnp.square(xf), axis=-1, keepdims=True) + RMS_EPS)
    return (y * g.astype(jnp.float32)).astype(x.dtype)


def causal_dwconv(x, w):
    k, c = w.shape
    return lax.conv_general_dilated(
        x, w[:, None, :].astype(x.dtype), window_strides=(1,), padding=[(k - 1, 0)],
        dimension_numbers=('NWC', 'WIO', 'NWC'), feature_group_count=c)


def rope_tables(positions):
    inv = 1.0 / (ROPE_THETA ** (jnp.arange(0, QK_ROPE_DIM, 2, dtype=jnp.float32) / QK_ROPE_DIM))
    ang = positions.astype(jnp.float32)[..., None] * inv
    return jnp.cos(ang), jnp.sin(ang)


def apply_rope(x, cos, sin):
    half = QK_ROPE_DIM // 2
    xf = x.astype(jnp.float32)
    x1, x2 = xf[..., :half], xf[..., half:]
    return jnp.concatenate([x1 * cos - x2 * sin, x2 * cos + x1 * sin], axis=-1).astype(x.dtype)


def mla_attention(q_nope, q_rope, k_nope, k_rope, v):
    b, s, h, _ = q_nope.shape
    nb = s // ATTN_BLOCK
    scale = (QK_NOPE_DIM + QK_ROPE_DIM) ** -0.5
    key_idx = jnp.arange(s)

    def to_blocks(t):
        return jnp.moveaxis(t.reshape(b, nb, ATTN_BLOCK, *t.shape[2:]), 1, 0)

    def one_block(args):
        qn, qr, blk = args
        sc = (jnp.einsum('bqhd,bkhd->bhqk', qn, k_nope, preferred_element_type=jnp.float32)
              + jnp.einsum('bqhd,bkd->bhqk', qr, k_rope, preferred_element_type=jnp.float32))
        q_idx = blk * ATTN_BLOCK + jnp.arange(ATTN_BLOCK)
        mask = key_idx[None, :] <= q_idx[:, None]
        p = jax.nn.softmax(jnp.where(mask, sc * scale, -jnp.inf), axis=-1)
        return jnp.einsum('bhqk,bkhd->bqhd', p.astype(v.dtype), v)

    out = lax.map(one_block, (to_blocks(q_nope), to_blocks(q_rope), jnp.arange(nb)))
    return jnp.moveaxis(out, 0, 1).reshape(b, s, h * V_HEAD_DIM)


def multiscale_pool(u):
    s = u.shape[1]
    uf = u.astype(jnp.float32)
    cs = jnp.cumsum(uf, axis=1)
    t = jnp.arange(s)
    means = []
    for g, w in enumerate(POOL_WINDOWS):
        cg = cs[..., g * POOL_GROUP_DIM:(g + 1) * POOL_GROUP_DIM]
        prev = jnp.pad(cg, ((0, 0), (w, 0), (0, 0)))[:, :s]
        cnt = jnp.minimum(t + 1, w).astype(jnp.float32)[None, :, None]
        means.append((cg - prev) / cnt)
    return (jnp.concatenate(means, axis=-1) - uf).astype(u.dtype)


def setup_inputs(seed: int = 0) -> dict:
    key = jax.random.key(seed)
    ks = iter(jax.random.split(key, 40))
    L, D = DEPTH, D_MODEL
    beta = DEEPNORM_BETA

    def nrm(shape, scale):
        return scale * jax.random.normal(next(ks), shape, jnp.float32)

    x = nrm((BATCH, SEQ, D), 1.0)
    c = nrm((BATCH, D), 1.0)
    offsets = jax.random.randint(next(ks), (BATCH, 1), 0, 4096, dtype=jnp.int32)
    positions = offsets + jnp.arange(SEQ, dtype=jnp.int32)[None, :]
    return {
        'x': x,
        'c': c,
        'positions': positions,
        'w_ada': nrm((L, D, 6 * D), 0.1 * D ** -0.5),
        'b_ada': nrm((L, 6 * D), 0.02),
        'w_in': nrm((L, D, IN_COLS), D ** -0.5),
        'b_in': nrm((L, IN_COLS), 0.02),
        'conv_dw': nrm((L, CONV_WIDTH, CONV_DIM), CONV_WIDTH ** -0.5),
        'conv_ln_g': 1.0 + nrm((L, CONV_DIM), 0.1),
        'conv_ln_b': nrm((L, CONV_DIM), 0.02),
        'w_conv_out': nrm((L, CONV_DIM, D), beta * CONV_DIM ** -0.5),
        'sc_dw': nrm((L, SC_WIDTH, SC_DIM), SC_WIDTH ** -0.5),
        'w_sc_out': nrm((L, SC_DIM, D), beta * SC_DIM ** -0.5),
        'q_norm_g': 1.0 + nrm((L, Q_LORA_RANK), 0.1),
        'w_uq': nrm((L, Q_LORA_RANK, MLA_HEADS * (QK_NOPE_DIM + QK_ROPE_DIM)), Q_LORA_RANK ** -0.5),
        'kv_norm_g': 1.0 + nrm((L, KV_LORA_RANK), 0.1),
        'w_ukv': nrm((L, KV_LORA_RANK, MLA_HEADS * (QK_NOPE_DIM + V_HEAD_DIM)), KV_LORA_RANK ** -0.5),
        'w_mla_out': nrm((L, MLA_HEADS * V_HEAD_DIM, D), beta * (MLA_HEADS * V_HEAD_DIM) ** -0.5),
        'w_pool': nrm((L, POOL_GROUPS, POOL_GROUP_DIM, POOL_GROUP_DIM), POOL_GROUP_DIM ** -0.5),
        'pool_scale': 1.0 + nrm((L, POOL_DIM), 0.1),
        'w_pool_out': nrm((L, POOL_DIM, D), beta * POOL_DIM ** -0.5),
        'w_o': nrm((L, D, D), beta * D ** -0.5),
        'ln1_g': 1.0 + nrm((L, D), 0.1),
        'ln1_b': nrm((L, D), 0.02),
        'w_up': nrm((L, D, 2 * D_FF), D ** -0.5),
        'ffn_dw': nrm((L, FFN_CONV_WIDTH, 2 * D_FF), FFN_CONV_WIDTH ** -0.5),
        'w_down': nrm((L, D_FF, D), beta * D_FF ** -0.5),
        'ln2_g': 1.0 + nrm((L, D), 0.1),
        'ln2_b': nrm((L, D), 0.02),
    }


def reference(x, c, positions, w_ada, b_ada, w_in, b_in, conv_dw, conv_ln_g, conv_ln_b, w_conv_out,
              sc_dw, w_sc_out, q_norm_g, w_uq, kv_norm_g, w_ukv, w_mla_out, w_pool, pool_scale,
              w_pool_out, w_o, ln1_g, ln1_b, w_up, ffn_dw, w_down, ln2_g, ln2_b):
    b, s, d = x.shape
    cos, sin = rope_tables(positions)
    c_act = jax.nn.silu(c)
    for l in range(DEPTH):
        mod = (c_act @ w_ada[l] + b_ada[l])[:, None, :]
        sh1, sc1, g1, sh2, sc2, g2 = jnp.split(mod, 6, axis=-1)

        h = x * (1.0 + sc1) + sh1
        proj = h @ w_in[l] + b_in[l]
        conv_a, conv_b, sc_bg, sc_cg, sc_x, q_lat, kv_lat, k_rope, pool_u, gates = jnp.split(
            proj, IN_OFFSETS, axis=-1)

        ya = conv_a * jax.nn.sigmoid(conv_b)
        ya = causal_dwconv(ya, conv_dw[l])
        ya = jax.nn.silu(layer_norm(ya, conv_ln_g[l], conv_ln_b[l]))
        ya = ya @ w_conv_out[l]

        yb = (sc_bg * causal_dwconv(sc_cg * sc_x, sc_dw[l])) @ w_sc_out[l]

        q = (rms_norm(q_lat, q_norm_g[l]) @ w_uq[l]).reshape(b, s, MLA_HEADS, QK_NOPE_DIM + QK_ROPE_DIM)
        q_nope = q[..., :QK_NOPE_DIM]
        q_rope = apply_rope(q[..., QK_NOPE_DIM:], cos[:, :, None, :], sin[:, :, None, :])
        kv = (rms_norm(kv_lat, kv_norm_g[l]) @ w_ukv[l]).reshape(b, s, MLA_HEADS, QK_NOPE_DIM + V_HEAD_DIM)
        k_nope, v = kv[..., :QK_NOPE_DIM], kv[..., QK_NOPE_DIM:]
        k_rope_r = apply_rope(k_rope, cos, sin)
        yc = mla_attention(q_nope, q_rope, k_nope, k_rope_r, v) @ w_mla_out[l]

        pd = multiscale_pool(pool_u).reshape(b, s, POOL_GROUPS, POOL_GROUP_DIM)
        yd = jnp.einsum('bsgc,gcd->bsgd', pd, w_pool[l]).reshape(b, s, POOL_DIM) * pool_scale[l]
        yd = yd @ w_pool_out[l]

        gt = jax.nn.sigmoid(gates.astype(jnp.float32)).astype(x.dtype).reshape(b, s, N_BRANCHES, d)
        merged = gt[:, :, 0] * ya + gt[:, :, 1] * yb + gt[:, :, 2] * yc + gt[:, :, 3] * yd
        mix = merged @ w_o[l]
        x = layer_norm(DEEPNORM_ALPHA * x + (1.0 + g1) * mix, ln1_g[l], ln1_b[l])

        h = x * (1.0 + sc2) + sh2
        up = causal_dwconv(h @ w_up[l], ffn_dw[l])
        val, gate = jnp.split(up, 2, axis=-1)
        ffn = (jax.nn.silu(gate) * val) @ w_down[l]
        x = layer_norm(DEEPNORM_ALPHA * x + (1.0 + g2) * ffn, ln2_g[l], ln2_b[l])
    return x
```

```python
import math
from contextlib import ExitStack
import numpy as np
import concourse.bass as bass
import concourse.mybir as mybir
from concourse.bass_utils import run_bass_kernel_spmd

F32 = mybir.dt.float32
BF16 = mybir.dt.bfloat16
I32 = mybir.dt.int32
AF = mybir.ActivationFunctionType
ALU = mybir.AluOpType

D = 1024
KD = 8
DFF = 2816
NUP = 44
NFF = 22
NH = 8
LN_EPS = 1e-5
RMS_EPS = 1e-6
NCH_IN = 61
HALO = 32
SCALE = 96.0 ** -0.5

VEC = {}
_o = 0
for _n, _w in [("b_ada", 48), ("b_in", NCH_IN), ("conv_dw", 4 * 31), ("cln_g", 4), ("cln_b", 4), ("sc_dw", 12),
               ("qn_g", 2), ("kvn_g", 1), ("pool_scale", 4), ("ln1_g", 8), ("ln1_b", 8), ("ln2_g", 8),
               ("ln2_b", 8), ("ffn_dw", NUP * 3)]:
    VEC[_n] = _o
    _o += _w
NV = _o


class Buf:
    __slots__ = ("name", "last_w", "readers")

    def __init__(self, name):
        self.name = name
        self.last_w = None
        self.readers = []


class Op:
    __slots__ = ("eng", "fn", "deps", "dma", "needs_inc", "sem", "val", "prev_same_sem")

    def __init__(self, eng, fn, deps, dma):
        self.eng, self.fn, self.deps, self.dma = eng, fn, deps, dma
        self.needs_inc = False
        self.sem = None
        self.val = 0
        self.prev_same_sem = None


ENGS = ("pe", "act", "dve", "pool", "sp")
EPOCH = 4000
NDMASEM = 12


class Sched:
    def __init__(self, same_engine_sync=False):
        self.ops = []
        self.same = same_engine_sync
        self.pending_bar = {}

    def add(self, eng, fn, reads=(), writes=(), dma=False):
        idx = len(self.ops)
        deps = set()
        for b in reads:
            if b.last_w is not None:
                deps.add(b.last_w)
        for b in writes:
            if b.last_w is not None:
                deps.add(b.last_w)
            deps.update(b.readers)
        for b in reads:
            b.readers.append(idx)
        for b in writes:
            b.last_w = idx
            b.readers = []
        keep = []
        for d in deps:
            od = self.ops[d]
            if (not od.dma) and od.eng == eng and not dma:
                if eng == "pe" or not self.same:
                    continue
            keep.append(d)
        if eng in self.pending_bar:
            keep = sorted(set(keep) | set(self.pending_bar.pop(eng)))
        self.ops.append(Op(eng, fn, sorted(keep), dma))
        return idx

    def barrier(self):
        deps = []
        for e in ENGS:
            comp = [i for i in range(len(self.ops) - 1, -1, -1) if self.ops[i].eng == e and not self.ops[i].dma][:1]
            deps += comp
            dm = [i for i in range(len(self.ops) - 1, max(-1, len(self.ops) - 4000), -1)
                  if self.ops[i].eng == e and self.ops[i].dma][:NDMASEM]
            deps += dm
        self.pending_bar = {e: sorted(set(deps)) for e in ENGS}

    def emit(self, nc, es):
        ops = self.ops
        for op in ops:
            for d in op.deps:
                ops[d].needs_inc = True
        cnt = {e: 0 for e in ENGS}
        dcnt = {e: 0 for e in ENGS}
        sems = {}

        def getsem(key):
            if key not in sems:
                sems[key] = es.enter_context(nc.semaphore("s_%s_%s_%d" % key))
            return sems[key]

        last_on_dsem = {}
        for i, op in enumerate(ops):
            if op.dma:
                k = dcnt[op.eng]
                dcnt[op.eng] += 1
                key = ("d", op.eng, k % NDMASEM)
                op.sem = key
                op.val = 16 * (k // NDMASEM + 1)
                op.prev_same_sem = last_on_dsem.get(key)
                last_on_dsem[key] = i
            elif op.needs_inc:
                c = cnt[op.eng]
                cnt[op.eng] += 1
                op.sem = ("c", op.eng, c // EPOCH)
                op.val = c % EPOCH + 1
        for op in ops:
            if op.sem is not None:
                getsem(op.sem)
        import os as _os
        if _os.environ.get("KSTATS"):
            print("KSTATS ops", {e: sum(1 for o in ops if o.eng == e) for e in ENGS}, "incs", cnt, "dmas", dcnt,
                  "nsems", len(sems), flush=True)
        block = es.enter_context(nc.Block())
        streams = {e: [i for i, op in enumerate(ops) if op.eng == e] for e in ENGS}

        def run(eng_name, eng):
            known = {}
            for i in streams[eng_name]:
                op = ops[i]
                waits = [(ops[d].sem, ops[d].val) for d in op.deps]
                if op.dma and op.prev_same_sem is not None:
                    p = ops[op.prev_same_sem]
                    waits.append((p.sem, p.val))
                for key, val in waits:
                    if known.get(key, 0) < val:
                        eng.wait_ge(sems[key], val)
                        known[key] = val
                ins = op.fn(eng)
                if op.dma:
                    ins.then_inc(sems[op.sem], 16)
                elif op.needs_inc:
                    ins.then_inc(sems[op.sem], 1)
            for key, s in sems.items():
                if key[0] == "d" and key[1] == eng_name:
                    li = last_on_dsem[key]
                    if known.get(key, 0) < ops[li].val:
                        eng.wait_ge(s, ops[li].val)

        @block.tensor
        def _(e):
            run("pe", e)

        @block.scalar
        def _(e):
            run("act", e)

        @block.vector
        def _(e):
            run("dve", e)

        @block.gpsimd
        def _(e):
            run("pool", e)

        @block.sync
        def _(e):
            run("sp", e)


class Arena:
    def __init__(self, nc, es, nbytes, name="arena"):
        self.t = es.enter_context(nc.sbuf_tensor(name, [128, nbytes // 4], F32))
        self.off = 0
        self.cap = nbytes
        self.peak = 0

    def reset(self):
        self.off = 0

    def alloc(self, shape, dt):
        free = 1
        for s in shape[1:]:
            free *= s
        esz = 4 if dt in (F32, I32) else 2
        nb = (free * esz + 31) // 32 * 32
        assert self.off + nb <= self.cap, ("arena overflow", self.off, nb, self.cap)
        ap = self.t[:, self.off // 4:(self.off + nb) // 4]
        if dt != F32:
            ap = ap.bitcast(dt)
        ap = ap[:, 0:free]
        self.off += nb
        self.peak = max(self.peak, self.off)
        if len(shape) == 3:
            ap = ap.rearrange("p (a b) -> p a b", b=shape[2])
        elif len(shape) == 4:
            ap = ap.rearrange("p (a b c) -> p a b c", b=shape[2], c=shape[3])
        return ap


class Rot:
    def __init__(self, nc, es, name, shape, dt, n):
        self.t = [es.alloc(shape, dt) for i in range(n)]
        self.b = [Buf("%s%d" % (name, i)) for i in range(n)]
        self.i = 0

    def next(self):
        k = self.i % len(self.t)
        self.i += 1
        return self.t[k], self.b[k]


def build_program(CPS, TPC, DEPTH, n_cores, alpha_depth=None):
    T = 512
    NT = TPC // T
    NKC = CPS
    NK = NKC * TPC
    NKB = NK // 128
    groups = [list(range(g * CPS, (g + 1) * CPS)) for g in range(n_cores // CPS)]
    nc = bass.Bass("TRN2", target_bir_lowering=False)
    S = Sched()
    es = ExitStack()

    def din(name, shape, dt=F32):
        return nc.dram_tensor(name, list(shape), dt, kind="ExternalInput").ap()

    def dint(name, shape, dt=F32):
        if DEBUG and name in DEBUG_NAMES:
            return nc.dram_tensor(name, list(shape), dt, kind="ExternalOutput").ap()
        return nc.dram_tensor(name, list(shape), dt, kind="Internal").ap()

    xT_in = din("xT", [D, TPC])
    pos_in = din("posr", [128, TPC], I32)
    cT_in = din("cT", [128, KD])
    cst_in = din("cst", [128, 16])
    rcnt_in = din("rcnt", [128, 4 * 16])
    tri_in = din("tri", [128, 128])
    shift_in = din("shift", [128, 64])
    RT = wrows(n_cores)
    RSH = RT // n_cores
    vecs_in = din("vecs", [DEPTH, 128, NV])
    wfull_in = din("wfull", [DEPTH, RT, WCOLS])
    wfull = [wfull_in[i] for i in range(DEPTH)]
    bwsrc = [Buf("wsrc%d" % i) for i in range(DEPTH)]
    bwfull = [Buf("wfull%d" % i) for i in range(DEPTH)]
    curl = [0]

    def Wl(name):
        off, r, c = WOFF[name]
        return wfull[curl[0]].rearrange("r c -> (r c)")[off:off + r * c].rearrange("(r c) -> r c", c=c)

    def gather_weights(i):
        pass

    out_T = nc.dram_tensor("outT", [D, TPC], F32, kind="ExternalOutput").ap()

    xA = dint("xA", [D, TPC]); bxA = Buf("xA")
    xM = dint("xM", [D, TPC]); bxM = Buf("xM")
    tail_src = dint("tail_src", [D, HALO]); btail_src = Buf("tail_src")
    tail_all = dint("tail_all", [CPS * D, HALO]); btail_all = Buf("tail_all")
    kv_src = dint("kv_src", [NT * 160, T], BF16); bkv_src = [Buf("kv_src%d" % i) for i in range(NT)]
    kv_all = dint("kv_all", [NT * CPS * 160, T], BF16); bkv_all = [Buf("kv_all%d" % i) for i in range(NT)]
    mg_d = dint("mg_d", [D, TPC]); bmg = Buf("mg_d")
    g2_d = dint("g2_d", [D, TPC], BF16); bg2 = Buf("g2_d")
    q_d = dint("q_d", [NH * 96, TPC], BF16); bq = Buf("q_d")
    at_d = dint("at_d", [NH * 64, TPC], BF16); bat = Buf("at_d")
    cc_d = dint("cc_d", [32, TPC]); bcc = Buf("cc_d")
    ss_d = dint("ss_d", [32, TPC]); bss = Buf("ss_d")

    def sb(name, shape, dt=F32):
        return es.enter_context(nc.sbuf_tensor("sb_" + name, list(shape), dt))

    cst = sb("cst", [128, 16]); bcst = Buf("cst")
    rcnt = sb("rcnt", [128, 4, 16])
    tri = sb("tri", [128, 128], BF16)
    shiftm = sb("shiftm", [128, 64])
    epsc = sb("epsc", [128, 4]); bconst = Buf("const")
    ones = sb("ones", [128, 3, 128], BF16)
    ones1k = sb("ones1k", [128, 128], BF16)
    cact = sb("cact", [128, KD], BF16); bcact = Buf("cact")
    mod = sb("mod", [128, 48]); bmod = Buf("mod")
    vecs = sb("vecs", [128, NV]); bvecs = Buf("vecs")
    ps = es.enter_context(nc.psum_tensor("ps", [128, 8, 512], F32))
    bps = [Buf("ps%d" % i) for i in range(8)]
    psi = [0]

    def nps():
        k = psi[0] % 8
        psi[0] += 1
        return k

    AR = Arena(nc, es, 168 * 1024)
    ST = Rot(nc, Arena(nc, es, 8 * 1024, "arena_st"), "st", [128, T], F32, 4)
    rc2 = sb("rc2", [128, 64]); brc2 = Buf("rc2")
    uph = sb("uph", [128, NUP, 2]); buph = Buf("uph")
    tailsb = sb("tailsb", [128, CPS, KD, HALO]); btailsb = Buf("tailsb")
    xhalo = sb("xhalo", [128, KD, HALO]); bxhalo = Buf("xhalo")
    wukv = sb("wukv", [128, 1024], BF16); bwukv = Buf("wukv")
    ALPHA = (2.0 * (alpha_depth or DEPTH)) ** 0.25
    XT = HT = WS = F1 = B1 = F4 = B4 = B8 = F8 = tabs = upb = PW = qh = PT = None
    bufA = bufB = bufP = merged = ffin = kvnT = kT = vaug = None
    bbufA = [Buf("bufA%d" % j) for j in range(4)]
    bbufB = [Buf("bufB%d" % j) for j in range(4)]
    bbufP = [Buf("bufP%d" % j) for j in range(4)]
    bmerged = [Buf("mg%d" % m) for m in range(KD)]
    bffin = [Buf("ffin%d" % i) for i in range(NFF)]
    bkvnT = [Buf("kvnT%d" % r) for r in range(NKC)]
    bkTn = Buf("kTn"); bkTr = Buf("kTr"); bvv = Buf("vaug_v"); bvo = Buf("vaug_o")

    P = lambda a: a

    def dma(q, out, in_, reads, writes):
        S.add(q, lambda e, o=out, i=in_: e.dma_start(out=o, in_=i), reads, writes, dma=True)

    def act(out, in_, func, reads, writes, scale=None, bias=None):
        kw = {}
        if scale is not None:
            kw["scale"] = scale
        if bias is not None:
            kw["bias"] = bias
        S.add("act", lambda e, o=out, i=in_, f=func, kw=kw: e.activation(out=o, in_=i, func=f, **kw), reads, writes)

    def tt(out, in0, in1, op, reads, writes, eng="dve"):
        S.add(eng, lambda e, o=out, a=in0, b=in1, p=op: e.tensor_tensor(out=o, in0=a, in1=b, op=p), reads, writes)

    def ts(out, in0, s1, op0, reads, writes, s2=None, op1=None, eng="dve"):
        if op1 is None:
            S.add(eng, lambda e, o=out, a=in0, s=s1, p=op0: e.tensor_scalar(out=o, in0=a, scalar1=s, scalar2=None, op0=p),
                  reads, writes)
        else:
            S.add(eng, lambda e, o=out, a=in0, s=s1, p=op0, s_2=s2, p1=op1: e.tensor_scalar(
                out=o, in0=a, scalar1=s, scalar2=s_2, op0=p, op1=p1), reads, writes)

    def stt(out, in0, scalar, in1, op0, op1, reads, writes):
        S.add("dve", lambda e, o=out, a=in0, s=scalar, b=in1, p0=op0, p1=op1: e.scalar_tensor_tensor(
            out=o, in0=a, scalar=s, in1=b, op0=p0, op1=p1), reads, writes)

    def mm(out, pairs, reads, writes):
        def fn(e, out=out, pairs=pairs):
            n = len(pairs)
            ins = None
            for i, (l, r) in enumerate(pairs):
                ins = e.matmul(out, lhsT=l, rhs=r, start=(i == 0), stop=(i == n - 1))
            return ins
        S.add("pe", fn, reads, writes)

    def vcol(name, i=0, n=1):
        o = VEC[name] + i
        return vecs[:, o:o + n]

    def wload(src_ap, kparts, ncols, reads=()):
        t, b = WS.next()
        dma("pool", t[:, 0:kparts, 0:ncols], src_ap.rearrange("(k p) c -> p k c", p=128),
            list(reads) + [bwfull[curl[0]]], [b])
        return t, b

    S.add("sp", lambda e: e.dma_start(out=cst[:], in_=cst_in), [], [bcst], dma=True)
    S.add("sp", lambda e: e.dma_start(out=rcnt[:].rearrange("p a b -> p (a b)"), in_=rcnt_in), [], [bconst], dma=True)
    S.add("pool", lambda e: e.dma_start(out=tri[:], in_=tri_in), [], [bconst], dma=True)
    S.add("sp", lambda e: e.dma_start(out=shiftm[:], in_=shift_in), [], [bconst], dma=True)
    for col, v in enumerate([LN_EPS, RMS_EPS, 0.0, 1.0]):
        S.add("dve", lambda e, c=col, v=v: e.memset(epsc[:, c:c + 1], v), [], [bconst])
    for k, v in enumerate([1.0 / 128, 1.0 / 256, 1.0 / 512]):
        S.add("dve", lambda e, k=k, v=v: e.memset(ones[:, k, :], v), [], [bconst])
    S.add("dve", lambda e: e.memset(ones1k[:], 1.0 / 1024), [], [bconst])
    for g_, w_ in enumerate((2, 4, 8, 16)):
        for t_ in range(w_ - 1):
            A_ = 1.0 / (t_ + 1)
            B_ = 1.0 / w_ - A_
            ts(rc2[:, g_ * 16 + t_:g_ * 16 + t_ + 1], cst[:, 2:3], B_, ALU.mult, [bcst], [brc2], s2=A_, op1=ALU.add)
    eps_ln = epsc[:, 0:1]
    eps_rms = epsc[:, 1:2]

    ctmp = sb("ctmp", [128, KD])
    S.add("sp", lambda e: e.dma_start(out=ctmp[:], in_=cT_in), [], [bcact], dma=True)
    act(cact[:], ctmp[:], AF.Silu, [bcact], [bcact])

    R = slice(64, 96)
    TWO_PI = 2.0 * math.pi
    PI_HI = 6.28125
    PI_LO = TWO_PI - PI_HI
    PI_SAFE = 3.14159
    AR.reset()
    posi = AR.alloc([128, T], I32)
    bro = Buf("ropetmp")
    ang = AR.alloc([128, T], F32); kf = AR.alloc([128, T], F32); ki = AR.alloc([128, T], I32)
    rr = AR.alloc([128, T], F32); mk = AR.alloc([128, T], F32); sn = AR.alloc([128, T], F32)
    for n in range(NT):
        cs = slice(n * T, (n + 1) * T)
        dma("sp", posi[R, :], pos_in[64:96, cs], [], [bro])
        S.add("dve", lambda e: e.tensor_copy(out=ang[R, :], in_=posi[R, :]), [bro], [bro])
        ts(ang[R, :], ang[R, :], cst[R, 0:1], ALU.mult, [bro, bcst], [bro])
        ts(kf[R, :], ang[R, :], 1.0 / TWO_PI, ALU.mult, [bro], [bro])
        S.add("dve", lambda e: e.tensor_copy(out=ki[R, :], in_=kf[R, :]), [bro], [bro])
        S.add("dve", lambda e: e.tensor_copy(out=kf[R, :], in_=ki[R, :]), [bro], [bro])
        stt(rr[R, :], kf[R, :], -PI_HI, ang[R, :], ALU.mult, ALU.add, [bro], [bro])
        stt(rr[R, :], kf[R, :], -PI_LO, rr[R, :], ALU.mult, ALU.add, [bro], [bro])

        def wrap(dst, src):
            ts(mk[R, :], src, math.pi, ALU.is_gt, [bro], [bro])
            stt(dst, mk[R, :], -TWO_PI, src, ALU.mult, ALU.add, [bro], [bro])
            ts(mk[R, :], dst, -math.pi, ALU.is_lt, [bro], [bro])
            stt(dst, mk[R, :], TWO_PI, dst, ALU.mult, ALU.add, [bro], [bro])
            ts(dst, dst, -PI_SAFE, ALU.max, [bro], [bro], s2=PI_SAFE, op1=ALU.min)
        wrap(rr[R, :], rr[R, :])
        act(sn[R, :], rr[R, :], AF.Sin, [bro], [bro])
        ts(sn[R, :], sn[R, :], cst[R, 1:2], ALU.mult, [bro, bcst], [bro])
        dma("sp", ss_d[:, cs], sn[R, :], [bro], [bss])
        ts(ang[R, :], rr[R, :], math.pi / 2, ALU.add, [bro], [bro])
        wrap(ang[R, :], ang[R, :])
        act(kf[R, :], ang[R, :], AF.Sin, [bro], [bro])
        dma("sp", cc_d[:, cs], kf[R, :], [bro], [bcc])

    for k in range(KD):
        dma("sp", xA[k * 128:(k + 1) * 128, :], xT_in[k * 128:(k + 1) * 128, :], [], [bxA])

    def exchange_tail(xsrc, bxsrc):
        dma("pool", tail_src, xsrc[:, TPC - HALO:TPC], [bxsrc], [btail_src])
        S.add("pool", lambda e: e.collective_compute("AllGather", ALU.bypass, replica_groups=groups,
                                                     ins=[tail_src], outs=[tail_all]),
              [btail_src], [btail_all])
        dma("pool", tailsb[:].rearrange("p r k h -> p (r k) h"),
            tail_all.rearrange("(rk p) h -> p rk h", p=128), [btail_all], [btailsb])
        xh = xhalo[:].rearrange("p k h -> p (k h)")
        for r in range(CPS):
            src = tailsb[:, r].rearrange("p k h -> p (k h)")
            if r == 0:
                ts(xh, src, cst[:, 6:7], ALU.mult, [btailsb, bcst], [bxhalo])
            else:
                stt(xh, src, cst[:, 6 + r:7 + r], xh, ALU.mult, ALU.add, [btailsb, bcst], [bxhalo])

    def modulate(xt, bxt, N, off):
        ht, bht = HT.next()
        for k in range(KD):
            act(ht[:, k, 0:N], xt[:, k, 0:N], AF.Identity, [bxt, bmod], [bht],
                scale=mod[:, off + 8 + k:off + 9 + k], bias=mod[:, off + k:off + k + 1])
        return ht, bht

    def proj_group(wsrc, c0, ncols, ht, bht, N, kparts=KD):
        wt, bw = wload(wsrc[:, c0:c0 + ncols], kparts, ncols)
        banks = []
        nchunks = (ncols + 127) // 128
        for ci in range(nchunks):
            mcols = min(128, ncols - ci * 128)
            b = nps()
            mm(ps[0:mcols, b, 0:N], [(wt[:, k, ci * 128:ci * 128 + mcols], ht[:, k, 0:N]) for k in range(kparts)],
               [bw, bht], [bps[b]])
            banks.append(b)
        return banks

    def stats(src_bf, bsrc, sq_bf, bsq, nchunk, ones_ap, N):
        bm = nps()
        mm(ps[:, bm, 0:N], [(ones_ap, src_bf[:, j, 0:N]) for j in range(nchunk)], [bsrc, bconst], [bps[bm]])
        be = nps()
        mm(ps[:, be, 0:N], [(ones_ap, sq_bf[:, j, 0:N]) for j in range(nchunk)], [bsq, bconst], [bps[be]])
        return bm, be

    def ln_finish(bm, be, N, eps_ap):
        mean, bmean = ST.next()
        act(mean[:, 0:N], ps[:, bm, 0:N], AF.Copy, [bps[bm]], [bmean])
        var, bvar = ST.next()
        tt(var[:, 0:N], mean[:, 0:N], mean[:, 0:N], ALU.mult, [bmean], [bvar])
        tt(var[:, 0:N], ps[:, be, 0:N], var[:, 0:N], ALU.subtract, [bps[be], bvar], [bvar])
        ts(var[:, 0:N], var[:, 0:N], 0.0, ALU.max, [bvar], [bvar])
        act(var[:, 0:N], var[:, 0:N], AF.Sqrt, [bvar, bconst], [bvar], bias=eps_ap)
        S.add("dve", lambda e, v=var, N=N: e.reciprocal(out=v[:, 0:N], in_=v[:, 0:N]), [bvar], [bvar])
        return mean, bmean, var, bvar

    def layer_norm_full(xin, bxin, gname, bname, N, dst_ap_fn, bdst):
        xb, bxb = B8.next()
        sq, bsq = B8.next()
        for k in range(KD):
            act(xb[:, k, 0:N], xin[:, k, 0:N], AF.Copy, [bxin], [bxb])
            act(sq[:, k, 0:N], xin[:, k, 0:N], AF.Square, [bxin], [bsq])
        bm, be = stats(xb, bxb, sq, bsq, KD, ones1k[:], N)
        mean, bmean, rstd, brstd = ln_finish(bm, be, N, eps_ln)
        for k in range(KD):
            d, bd = F1.next()
            tt(d[:, 0:N], xin[:, k, 0:N], mean[:, 0:N], ALU.subtract, [bxin, bmean], [bd])
            tt(d[:, 0:N], d[:, 0:N], rstd[:, 0:N], ALU.mult, [bd, brstd], [bd])
            act(dst_ap_fn(k), d[:, 0:N], AF.Identity, [bd, bvecs], [bdst], scale=vcol(gname, k), bias=vcol(bname, k))

    for l in range(DEPTH):
        S.barrier()
        AR.reset()
        XT = Rot(nc, AR, "xt", [128, KD, T], F32, 1)
        HT = Rot(nc, AR, "ht", [128, KD, T], BF16, 1)
        WS = Rot(nc, AR, "ws", [128, KD, 512], BF16, 3)
        F1 = Rot(nc, AR, "f1", [128, T], F32, 6)
        B1 = Rot(nc, AR, "b1", [128, T], BF16, 3)
        F4 = Rot(nc, AR, "f4", [128, 4, T], F32, 2)
        B4 = Rot(nc, AR, "b4", [128, 4, T], BF16, 5)
        B8 = Rot(nc, AR, "b8", [128, KD, T], BF16, 1)
        tabs = Rot(nc, AR, "tabs", [128, 2, T], F32, 1)
        PW = Rot(nc, AR, "pw", [128, 16 + T], F32, 3)
        bufA = AR.alloc([128, 4, 30 + T], F32)
        bufB = AR.alloc([128, 4, 2 + T], F32)
        bufP = AR.alloc([128, 4, 16 + T], F32)
        merged = AR.alloc([128, KD, T], F32)
        curl[0] = l
        if l == 0:
            gather_weights(0)
        if l + 1 < DEPTH:
            gather_weights(l + 1)
        dma("sp", vecs[:], vecs_in[l], [], [bvecs])
        for g6 in range(6):
            for half in range(2):
                wt, bw = wload(Wl("w_ada")[:, g6 * D + half * 512: g6 * D + half * 512 + 512], KD, 512)
                for ci in range(4):
                    mcol = g6 * 8 + half * 4 + ci
                    b = nps()
                    mm(ps[:, b, 0:1], [(wt[:, k, ci * 128:(ci + 1) * 128], cact[:, k:k + 1]) for k in range(KD)],
                       [bw, bcact], [bps[b]])
                    act(mod[:, mcol:mcol + 1], ps[:, b, 0:1], AF.Identity, [bps[b], bvecs], [bmod],
                        bias=vcol("b_ada", mcol))
        for g6 in (1, 2, 4, 5):
            ts(mod[:, g6 * 8:(g6 + 1) * 8], mod[:, g6 * 8:(g6 + 1) * 8], 1.0, ALU.add, [bmod], [bmod])

        exchange_tail(xA, bxA)

        wi = Wl("w_in")

        def phase1_tile(n):
            halo = n < 0
            N = HALO if halo else T
            if halo:
                xt, bxt = xhalo, bxhalo
            else:
                xt, bxt = XT.next()
                dma("sp", xt[:], xA[:, n * T:(n + 1) * T].rearrange("(k p) t -> p k t", p=128), [bxA], [bxt])
            ht, bht = modulate(xt, bxt, N, 0)

            bA = proj_group(wi, 0, 512, ht, bht, N)
            bB = proj_group(wi, 512, 512, ht, bht, N)
            for j in range(4):
                sg, bsg = F1.next()
                act(sg[:, 0:N], ps[:, bB[j], 0:N], AF.Sigmoid, [bps[bB[j]], bvecs], [bsg], bias=vcol("b_in", 4 + j))
                if halo:
                    tmp, btmp = F1.next()
                    stt(tmp[:, 0:N], ps[:, bA[j], 0:N], vcol("b_in", j), sg[:, 0:N], ALU.add, ALU.mult,
                        [bps[bA[j]], bsg, bvecs], [btmp])
                    ts(bufA[:, j, 0:30], tmp[:, 2:32], cst[:, 2:3], ALU.mult, [btmp, bcst], [bbufA[j]])
                else:
                    stt(bufA[:, j, 30:30 + T], ps[:, bA[j], 0:T], vcol("b_in", j), sg[:, 0:T], ALU.add, ALU.mult,
                        [bps[bA[j]], bsg, bvecs], [bbufA[j]])
            bC = proj_group(wi, 12 * 128, 512, ht, bht, N)
            bX = proj_group(wi, 16 * 128, 512, ht, bht, N)
            for j in range(4):
                cs_, bcs = F1.next()
                act(cs_[:, 0:N], ps[:, bC[j], 0:N], AF.Identity, [bps[bC[j]], bvecs], [bcs], bias=vcol("b_in", 12 + j))
                if halo:
                    tmp, btmp = F1.next()
                    stt(tmp[:, 0:N], ps[:, bX[j], 0:N], vcol("b_in", 16 + j), cs_[:, 0:N], ALU.add, ALU.mult,
                        [bps[bX[j]], bcs, bvecs], [btmp])
                    ts(bufB[:, j, 0:2], tmp[:, 30:32], cst[:, 2:3], ALU.mult, [btmp, bcst], [bbufB[j]])
                else:
                    stt(bufB[:, j, 2:2 + T], ps[:, bX[j], 0:T], vcol("b_in", 16 + j), cs_[:, 0:T], ALU.add, ALU.mult,
                        [bps[bX[j]], bcs, bvecs], [bbufB[j]])
            bP = proj_group(wi, 25 * 128, 512, ht, bht, N)
            for j in range(4):
                if halo:
                    tmp, btmp = F1.next()
                    act(tmp[:, 0:N], ps[:, bP[j], 0:N], AF.Identity, [bps[bP[j]], bvecs], [btmp], bias=vcol("b_in", 25 + j))
                    ts(bufP[:, j, 0:16], tmp[:, 16:32], cst[:, 2:3], ALU.mult, [btmp, bcst], [bbufP[j]])
                else:
                    act(bufP[:, j, 16:16 + T], ps[:, bP[j], 0:T], AF.Identity, [bps[bP[j]], bvecs], [bbufP[j]],
                        bias=vcol("b_in", 25 + j))
            if halo:
                return

            accA, baccA = F4.next()
            for j in range(4):
                for k in range(31):
                    wk = vcol("conv_dw", j * 31 + k)
                    if k == 0:
                        ts(accA[:, j, :], bufA[:, j, 0:T], wk, ALU.mult, [bbufA[j], bvecs], [baccA])
                    else:
                        stt(accA[:, j, :], bufA[:, j, k:k + T], wk, accA[:, j, :], ALU.mult, ALU.add,
                            [bbufA[j], bvecs], [baccA])
                ts(bufA[:, j, 0:30], bufA[:, j, T:T + 30], 1.0, ALU.mult, [bbufA[j]], [bbufA[j]])
            ab, bab = B4.next()
            sq, bsq = B4.next()
            for j in range(4):
                act(ab[:, j, :], accA[:, j, :], AF.Copy, [baccA], [bab])
                act(sq[:, j, :], accA[:, j, :], AF.Square, [baccA], [bsq])
            bm, be = stats(ab, bab, sq, bsq, 4, ones[:, 2, :], T)
            mean, bmean, rstd, brstd = ln_finish(bm, be, T, eps_ln)
            yA, byA = B4.next()
            for j in range(4):
                d, bd = F1.next()
                tt(d[:], accA[:, j, :], mean[:], ALU.subtract, [baccA, bmean], [bd])
                tt(d[:], d[:], rstd[:], ALU.mult, [bd, brstd], [bd])
                act(yA[:, j, :], d[:], AF.Silu, [bd, bvecs], [byA], scale=vcol("cln_g", j), bias=vcol("cln_b", j))

            bG = proj_group(wi, 8 * 128, 512, ht, bht, T)
            yB, byB = B4.next()
            for j in range(4):
                acc, bacc = F1.next()
                for k in range(3):
                    wk = vcol("sc_dw", j * 3 + k)
                    if k == 0:
                        ts(acc[:], bufB[:, j, 0:T], wk, ALU.mult, [bbufB[j], bvecs], [bacc])
                    else:
                        stt(acc[:], bufB[:, j, k:k + T], wk, acc[:], ALU.mult, ALU.add, [bbufB[j], bvecs], [bacc])
                ts(bufB[:, j, 0:2], bufB[:, j, T:T + 2], 1.0, ALU.mult, [bbufB[j]], [bbufB[j]])
                stt(yB[:, j, :], ps[:, bG[j], :], vcol("b_in", 8 + j), acc[:], ALU.add, ALU.mult,
                    [bps[bG[j]], bacc, bvecs], [byB])

            pd, bpd = B4.next()
            for g in range(4):
                w = 2 << g
                cur = bufP[:, g, :]
                bcur = bbufP[g]
                span = 16 + T
                sh = 1
                srcs = (cur, bcur)
                while sh < w:
                    tmpP, btmpP = PW.next()
                    tt(tmpP[:, sh:span], srcs[0][:, sh:span], srcs[0][:, 0:span - sh], ALU.add, [srcs[1]], [btmpP])
                    srcs = (tmpP, btmpP)
                    sh *= 2
                o, bo = F1.next()
                stt(o[:], srcs[0][:, 16:16 + T], 1.0 / w, bufP[:, g, 16:16 + T], ALU.mult, ALU.subtract,
                    [srcs[1], bbufP[g]], [bo])
                if n == 0:
                    for t_ in range(w - 1):
                        stt(o[:, t_:t_ + 1], srcs[0][:, 16 + t_:17 + t_], rc2[:, g * 16 + t_:g * 16 + t_ + 1],
                            bufP[:, g, 16 + t_:17 + t_], ALU.mult, ALU.subtract, [srcs[1], bbufP[g], brc2], [bo])
                act(pd[:, g, :], o[:], AF.Copy, [bo], [bpd])
                ts(bufP[:, g, 0:16], bufP[:, g, T:T + 16], 1.0, ALU.mult, [bbufP[g]], [bbufP[g]])
            wpt, bwp = WS.next()
            dma("pool", wpt[:, 0:4, 0:128], Wl("w_pool").rearrange("(g c) d -> c g d", c=128), [bwfull[l]], [bwp])
            yD, byD = B4.next()
            for g in range(4):
                b = nps()
                mm(ps[:, b, :], [(wpt[:, g, 0:128], pd[:, g, :])], [bwp, bpd], [bps[b]])
                act(yD[:, g, :], ps[:, b, :], AF.Copy, [bps[b], bvecs], [byD], scale=vcol("pool_scale", g))

            for bi, (wname, ysrc, bysrc) in enumerate([("w_conv_out", yA, byA), ("w_sc_out", yB, byB),
                                                      ("w_pool_out", yD, byD)]):
                gate_b = (0, 1, 3)[bi]
                for half in range(2):
                    bo_ = proj_group(Wl(wname), half * 512, 512, ysrc, bysrc, T, kparts=4)
                    bg_ = proj_group(wi, (29 + gate_b * 8 + half * 4) * 128, 512, ht, bht, T)
                    for ci in range(4):
                        m = half * 4 + ci
                        gt, bgt = F1.next()
                        act(gt[:], ps[:, bg_[ci], :], AF.Sigmoid, [bps[bg_[ci]], bvecs], [bgt],
                            bias=vcol("b_in", 29 + gate_b * 8 + m))
                        if bi == 0:
                            tt(merged[:, m, :], gt[:], ps[:, bo_[ci], :], ALU.mult, [bgt, bps[bo_[ci]]], [bmerged[m]])
                        else:
                            tt(gt[:], gt[:], ps[:, bo_[ci], :], ALU.mult, [bgt, bps[bo_[ci]]], [bgt])
                            tt(merged[:, m, :], merged[:, m, :], gt[:], ALU.add, [bgt, bmerged[m]], [bmerged[m]],
                               eng="pool")
            dma("sp", mg_d[:, n * T:(n + 1) * T].rearrange("(k p) t -> p k t", p=128), merged[:], bmerged, [bmg])
            g2t, bg2t = B8.next()
            for half in range(2):
                bg_ = proj_group(wi, (29 + 2 * 8 + half * 4) * 128, 512, ht, bht, T)
                for ci in range(4):
                    m = half * 4 + ci
                    act(g2t[:, m, :], ps[:, bg_[ci], :], AF.Sigmoid, [bps[bg_[ci]], bvecs], [bg2t],
                        bias=vcol("b_in", 29 + 16 + m))
            dma("sp", g2_d[:, n * T:(n + 1) * T].rearrange("(k p) t -> p k t", p=128), g2t[:], [bg2t], [bg2])

            tb, btb = tabs.next()
            dma("sp", tb[64:96, 0, :], cc_d[:, n * T:(n + 1) * T], [bcc], [btb])
            dma("sp", tb[64:96, 1, :], ss_d[:, n * T:(n + 1) * T], [bss], [btb])
            bQ = proj_group(wi, 20 * 128, 512, ht, bht, T)
            bK2 = proj_group(wi, 24 * 128, 128, ht, bht, T)
            ql, bql = F4.next()
            qsq, bqsq = B4.next()
            for j in range(2):
                act(ql[:, j, :], ps[:, bQ[j], :], AF.Identity, [bps[bQ[j]], bvecs], [bql], bias=vcol("b_in", 20 + j))
                act(qsq[:, j, :], ql[:, j, :], AF.Square, [bql], [bqsq])
            act(ql[:, 2, :], ps[:, bQ[2], :], AF.Identity, [bps[bQ[2]], bvecs], [bql], bias=vcol("b_in", 22))
            act(qsq[:, 2, :], ql[:, 2, :], AF.Square, [bql], [bqsq])
            bqs = nps()
            mm(ps[:, bqs, :], [(ones[:, 1, :], qsq[:, j, :]) for j in range(2)], [bqsq, bconst], [bps[bqs]])
            bks = nps()
            mm(ps[:, bks, :], [(ones[:, 0, :], qsq[:, 2, :])], [bqsq, bconst], [bps[bks]])
            qn, bqn = B4.next()
            for (bb, js, gname) in ((bqs, (0, 1), "qn_g"), (bks, (2,), "kvn_g")):
                rs, brs = F1.next()
                act(rs[:], ps[:, bb, :], AF.Sqrt, [bps[bb], bconst], [brs], bias=eps_rms)
                S.add("dve", lambda e, v=rs: e.reciprocal(out=v[:], in_=v[:]), [brs], [brs])
                for j in js:
                    stt(qn[:, j, :], ql[:, j, :], vcol(gname, j if gname == "qn_g" else 0), rs[:], ALU.mult, ALU.mult,
                        [bql, brs, bvecs], [bqn])
            dma("sp", kv_src[n * 160:n * 160 + 128, :], qn[:, 2, :], [bqn], [bkv_src[n]])
            kr, bkr = F1.next()
            kr2, bkr2 = F1.next()
            stt(kr[R, :], ps[R, bQ[3], :], vecs[R, VEC["b_in"] + 23:VEC["b_in"] + 24], tb[R, 0, :], ALU.add, ALU.mult,
                [bps[bQ[3]], btb, bvecs], [bkr])
            stt(kr2[R, :], ps[R, bK2[0], :], vecs[R, VEC["b_in"] + 24:VEC["b_in"] + 25], tb[R, 1, :], ALU.add, ALU.mult,
                [bps[bK2[0]], btb, bvecs], [bkr2])
            krb, bkrb = B1.next()
            tt(krb[R, :], kr[R, :], kr2[R, :], ALU.add, [bkr, bkr2], [bkrb])
            dma("sp", kv_src[n * 160 + 128:(n + 1) * 160, :], krb[R, :], [bkrb], [bkv_src[n]])
            S.add("pool", lambda e, n=n: e.collective_compute(
                "AllGather", ALU.bypass, replica_groups=groups, ins=[kv_src[n * 160:(n + 1) * 160, :]],
                outs=[kv_all[n * CPS * 160:(n + 1) * CPS * 160, :]]), [bkv_src[n]], [bkv_all[n]])
            wqs = []
            for s4 in range(4):
                wq_, bwq_ = WS.next()
                dma("pool", wq_[:, 0:2, 0:512], Wl("w_uq")[:, s4 * 512:(s4 + 1) * 512].rearrange(
                    "(k p) c -> p k c", p=128), [bwfull[l]], [bwq_])
                outs4 = []
                for h4 in range(4):
                    b = nps()
                    mm(ps[0:96, b, :], [(wq_[:, k, h4 * 128:h4 * 128 + 96], qn[:, k, :]) for k in range(2)],
                       [bwq_, bqn], [bps[b]])
                    outs4.append(b)
                wqs.append(outs4)
                if s4 % 2 == 0:
                    continue
                for h4 in range(4):
                    h = (s4 // 2) * 4 + h4
                    outs = [wqs[s4 - 1][h4], wqs[s4][h4]]
                    qo, bqo = B1.next()
                    act(qo[0:64, :], ps[0:64, outs[0], :], AF.Copy, [bps[outs[0]]], [bqo])
                    t1, bt1 = F1.next()
                    t2, bt2 = F1.next()
                    tt(t1[R, :], ps[R, outs[0], :], tb[R, 0, :], ALU.mult, [bps[outs[0]], btb], [bt1])
                    tt(t2[R, :], ps[R, outs[1], :], tb[R, 1, :], ALU.mult, [bps[outs[1]], btb], [bt2])
                    tt(qo[R, :], t1[R, :], t2[R, :], ALU.add, [bt1, bt2], [bqo])
                    dma("sp", q_d[h * 96:(h + 1) * 96, n * T:(n + 1) * T], qo[0:96, :], [bqo], [bq])

        phase1_tile(-1)
        for n in range(NT):
            phase1_tile(n)

        S.barrier()
        AR.reset()
        kvnT = AR.alloc([128, NK], BF16)
        kT = AR.alloc([128, NK], BF16)
        vaug = AR.alloc([128, NKB, 128], BF16)
        qh = Rot(nc, AR, "qh", [128, TPC], BF16, 2)
        PT = Rot(nc, AR, "pt", [128, 2, T], BF16, 3)
        F1 = Rot(nc, AR, "f1", [128, T], F32, 6)
        B1 = Rot(nc, AR, "b1", [128, T], BF16, 3)
        for r in range(NKC):
            kb0 = r * (TPC // 128)
            kb1 = (r + 1) * (TPC // 128)
            S.add("pool", lambda e, kb0=kb0, kb1=kb1: e.memset(vaug[:, kb0:kb1, 64:128], 1.0), [], [bvo])
            if r < NKC - 1:
                S.add("pool", lambda e, kb0=kb0, kb1=kb1, r=r: e.tensor_scalar(
                    out=vaug[:, kb0:kb1, 64:128], in0=vaug[:, kb0:kb1, 64:128], scalar1=cst[:, 3 + r:4 + r],
                    scalar2=None, op0=ALU.mult), [bcst, bvo], [bvo])
        for r in range(NKC):
            for n in range(NT):
                cs = slice(r * TPC + n * T, r * TPC + (n + 1) * T)
                if r < NKC - 1:
                    r0 = (n * CPS + r) * 160
                    dma("sp", kvnT[:, cs], kv_all[r0:r0 + 128, :], [bkv_all[n]], [bkvnT[r]])
                    dma("sp", kT[64:96, cs], kv_all[r0 + 128:r0 + 160, :], [bkv_all[n]], [bkTr])
                else:
                    dma("sp", kvnT[:, cs], kv_src[n * 160:n * 160 + 128, :], [bkv_src[n]], [bkvnT[r]])
                    dma("sp", kT[64:96, cs], kv_src[n * 160 + 128:(n + 1) * 160, :], [bkv_src[n]], [bkTr])
        dma("pool", wukv[:], Wl("w_ukv"), [bwfull[l]], [bwukv])
        OB = (4, 5)
        oi = 0
        for h in range(NH):
            for kb5 in range(NK // 512):
                b = 6 + (kb5 % 2)
                mm(ps[0:64, b, :], [(wukv[:, h * 128:h * 128 + 64], kvnT[:, kb5 * 512:(kb5 + 1) * 512])],
                   [bwukv] + bkvnT, [bps[b]])
                act(kT[0:64, kb5 * 512:(kb5 + 1) * 512], ps[0:64, b, :], AF.Copy, [bps[b]], [bkTn])
            for kb8 in range(NKB // 8):
                b = 6 + (kb8 % 2)
                for i8 in range(8):
                    kb = kb8 * 8 + i8
                    mm(ps[:, b, i8 * 64:(i8 + 1) * 64], [(kvnT[:, kb * 128:(kb + 1) * 128],
                                                          wukv[:, h * 128 + 64:(h + 1) * 128])],
                       [bwukv] + bkvnT, [bps[b]])
                r = (kb8 * 8 * 128) // TPC
                src = ps[:, b, :].rearrange("p (a d) -> p a d", d=64)
                if r < NKC - 1:
                    act(vaug[:, kb8 * 8:(kb8 + 1) * 8, 0:64], src, AF.Copy, [bps[b], bcst], [bvv],
                        scale=cst[:, 3 + r:4 + r])
                else:
                    act(vaug[:, kb8 * 8:(kb8 + 1) * 8, 0:64], src, AF.Copy, [bps[b]], [bvv])
            qt, bqt = qh.next()
            dma("sp", qt[0:96, :], q_d[h * 96:(h + 1) * 96, :], [bq], [bqt])
            for n in range(NT):
                kbl = [(kb, 0, False) for kb in range((NKC - 1) * (TPC // 128))]
                own0 = (NKC - 1) * (TPC // 128)
                for kbo in range(4 * n + 4):
                    i = kbo - 4 * n
                    if i < 0:
                        kbl.append((own0 + kbo, 0, False))
                    else:
                        kbl.append((own0 + kbo, i * 128, True))
                ob = OB[oi % 2]
                oi += 1
                nblk = len(kbl)
                idx = 0
                sbank = 0
                while idx < nblk:
                    pair = kbl[idx:idx + 2]
                    sb0 = (sbank % 2) * 2
                    sbank += 1
                    pt, bpt = PT.next()
                    for e_, (kb, c0, dg) in enumerate(pair):
                        mm(ps[:, sb0 + e_, c0:T], [(kT[0:96, kb * 128:(kb + 1) * 128], qt[0:96, n * T + c0:(n + 1) * T])],
                           [bkTn, bkTr, bqt], [bps[sb0 + e_]])
                    if len(pair) == 2 and pair[0][1] == 0 and pair[1][1] == 0:
                        act(pt[:, 0:2, :], ps[:, sb0:sb0 + 2, :], AF.Exp, [bps[sb0], bps[sb0 + 1]], [bpt], scale=SCALE)
                    else:
                        for e_, (kb, c0, dg) in enumerate(pair):
                            act(pt[:, e_, c0:T], ps[:, sb0 + e_, c0:T], AF.Exp, [bps[sb0 + e_]], [bpt], scale=SCALE)
                    for e_, (kb, c0, dg) in enumerate(pair):
                        if dg:
                            tt(pt[:, e_, c0:c0 + 128], pt[:, e_, c0:c0 + 128], tri[:], ALU.mult, [bpt, bconst], [bpt],
                               eng="pool")
                    for e_, (kb, c0, dg) in enumerate(pair):
                        gi = idx + e_
                        S.add("pe", lambda e, ob=ob, kb=kb, c0=c0, pt=pt, e_=e_, st=(gi == 0), sp_=(gi == nblk - 1):
                              e.matmul(ps[:, ob, c0:T], lhsT=vaug[:, kb, :], rhs=pt[:, e_, c0:T], start=st, stop=sp_),
                              [bvv, bvo, bpt], [bps[ob]])
                    idx += 2
                osb, bosb = F1.next()
                act(osb[:], ps[:, ob, :], AF.Copy, [bps[ob]], [bosb])
                bd_ = 6 + (oi % 2)
                mm(ps[0:64, bd_, :], [(shiftm[:], osb[:])], [bosb, bconst], [bps[bd_]])
                rd, brd = F1.next()
                S.add("dve", lambda e, rd=rd, bd_=bd_: e.reciprocal(out=rd[0:64, :], in_=ps[0:64, bd_, :]),
                      [bps[bd_]], [brd])
                ao, bao = B1.next()
                tt(ao[0:64, :], osb[0:64, :], rd[0:64, :], ALU.mult, [bosb, brd], [bao])
                dma("sp", at_d[h * 64:(h + 1) * 64, n * T:(n + 1) * T], ao[0:64, :], [bao], [bat])

        S.barrier()
        AR.reset()
        XT = Rot(nc, AR, "xt", [128, KD, T], F32, 2)
        F8 = Rot(nc, AR, "f8", [128, KD, T], F32, 2)
        F1 = Rot(nc, AR, "f1", [128, T], F32, 6)
        B4 = Rot(nc, AR, "b4", [128, 4, T], BF16, 2)
        B8 = Rot(nc, AR, "b8", [128, KD, T], BF16, 4)
        WS = Rot(nc, AR, "ws", [128, KD, 512], BF16, 3)
        for n in range(NT):
            cs = slice(n * T, (n + 1) * T)
            at, bat_t = B4.next()
            dma("sp", at[:], at_d[:, cs].rearrange("(k p) t -> p k t", p=128), [bat], [bat_t])
            mgt, bmgt = F8.next()
            dma("sp", mgt[:], mg_d[:, cs].rearrange("(k p) t -> p k t", p=128), [bmg], [bmgt])
            g2t, bg2t = B8.next()
            dma("sp", g2t[:], g2_d[:, cs].rearrange("(k p) t -> p k t", p=128), [bg2], [bg2t])
            xt, bxt = XT.next()
            dma("sp", xt[:], xA[:, cs].rearrange("(k p) t -> p k t", p=128), [bxA], [bxt])
            mb, bmb = B8.next()
            for half in range(2):
                bo_ = proj_group(Wl("w_mla_out"), half * 512, 512, at, bat_t, T, kparts=4)
                for ci in range(4):
                    m = half * 4 + ci
                    tmp, btmp = F1.next()
                    tt(tmp[:], g2t[:, m, :], ps[:, bo_[ci], :], ALU.mult, [bg2t, bps[bo_[ci]]], [btmp])
                    tt(mb[:, m, :], tmp[:], mgt[:, m, :], ALU.add, [btmp, bmgt], [bmb], eng="pool")
            x1, bx1 = F8.next()
            for half in range(2):
                bo_ = proj_group(Wl("w_o"), half * 512, 512, mb, bmb, T)
                for ci in range(4):
                    m = half * 4 + ci
                    tmp, btmp = F1.next()
                    act(tmp[:], ps[:, bo_[ci], :], AF.Copy, [bps[bo_[ci]], bmod], [btmp], scale=mod[:, 16 + m:17 + m])
                    stt(x1[:, m, :], xt[:, m, :], ALPHA, tmp[:], ALU.mult, ALU.add, [bxt, btmp], [bx1])
            xo, bxo = XT.next()
            layer_norm_full(x1, bx1, "ln1_g", "ln1_b", T, lambda k, xo=xo: xo[:, k, :], bxo)
            dma("sp", xM[:, cs].rearrange("(k p) t -> p k t", p=128), xo[:], [bxo], [bxM])

        exchange_tail(xM, bxM)

        def ffn_tile(n):
            halo = n < 0
            N = HALO if halo else T
            if halo:
                xt, bxt = xhalo, bxhalo
            else:
                xt, bxt = XT.next()
                dma("sp", xt[:], xM[:, n * T:(n + 1) * T].rearrange("(k p) t -> p k t", p=128), [bxM], [bxt])
            ht, bht = modulate(xt, bxt, N, 24)
            vals = [None, None]
            for g in range(NUP // 4):
                banks = proj_group(Wl("w_up"), g * 512, 512, ht, bht, N)
                for ci in range(4):
                    c = g * 4 + ci
                    if halo:
                        ts(uph[:, c, :], ps[:, banks[ci], 30:32], cst[:, 2:3], ALU.mult, [bps[banks[ci]], bcst], [buph])
                        continue
                    ub, bub = upb.next()
                    act(ub[:, 2:2 + T], ps[:, banks[ci], :], AF.Copy, [bps[banks[ci]]], [bub])
                    ts(ub[:, 0:2], uph[:, c, :], 1.0, ALU.mult, [buph], [bub], eng="pool")
                    ts(uph[:, c, :], ub[:, T:T + 2], 1.0, ALU.mult, [bub], [buph], eng="pool")
                    acc, bacc = F1.next()
                    for k in range(3):
                        wk = vcol("ffn_dw", c * 3 + k)
                        if k == 0:
                            ts(acc[:], ub[:, 0:T], wk, ALU.mult, [bub, bvecs], [bacc])
                        else:
                            stt(acc[:], ub[:, k:k + T], wk, acc[:], ALU.mult, ALU.add, [bub, bvecs], [bacc])
                    if ci < 2:
                        vals[ci] = (acc, bacc)
                    else:
                        i = g * 2 + ci - 2
                        sg, bsg = F1.next()
                        act(sg[:], acc[:], AF.Silu, [bacc], [bsg])
                        tt(ffin[:, i, :], sg[:], vals[ci - 2][0][:], ALU.mult, [bsg, vals[ci - 2][1]], [bffin[i]])
            if halo:
                return
            x2, bx2 = F8.next()
            for half in range(2):
                pb = []
                for ci in range(4):
                    pb.append(nps())
                kgs = [(0, 8), (8, 8), (16, 6)]
                wts = []
                for (k0, kn) in kgs:
                    wt, bw = wload(Wl("w_down")[k0 * 128:(k0 + kn) * 128, half * 512:(half + 1) * 512], kn, 512)
                    wts.append((wt, bw, k0, kn))
                for ci in range(4):
                    pairs = []
                    rd_ = []
                    for (wt, bw, k0, kn) in wts:
                        for k in range(kn):
                            pairs.append((wt[:, k, ci * 128:(ci + 1) * 128], ffin[:, k0 + k, :]))
                        rd_.append(bw)
                    mm(ps[:, pb[ci], :], pairs, rd_ + bffin, [bps[pb[ci]]])
                    m = half * 4 + ci
                    tmp, btmp = F1.next()
                    act(tmp[:], ps[:, pb[ci], :], AF.Copy, [bps[pb[ci]], bmod], [btmp], scale=mod[:, 40 + m:41 + m])
                    stt(x2[:, m, :], xt[:, m, :], ALPHA, tmp[:], ALU.mult, ALU.add, [bxt, btmp], [bx2])
            xo, bxo = XT.next()
            layer_norm_full(x2, bx2, "ln2_g", "ln2_b", T, lambda k, xo=xo: xo[:, k, :], bxo)
            if l == DEPTH - 1:
                dma("sp", out_T[:, n * T:(n + 1) * T].rearrange("(k p) t -> p k t", p=128), xo[:], [bxo], [Buf("o")])
            else:
                dma("sp", xA[:, n * T:(n + 1) * T].rearrange("(k p) t -> p k t", p=128), xo[:], [bxo], [bxA])

        S.barrier()
        AR.reset()
        XT = Rot(nc, AR, "xt", [128, KD, T], F32, 2)
        HT = Rot(nc, AR, "ht", [128, KD, T], BF16, 1)
        WS = Rot(nc, AR, "ws", [128, KD, 512], BF16, 4)
        upb = Rot(nc, AR, "upb", [128, 2 + T], F32, 4)
        F1 = Rot(nc, AR, "f1", [128, T], F32, 8)
        F8 = Rot(nc, AR, "f8", [128, KD, T], F32, 1)
        B8 = Rot(nc, AR, "b8", [128, KD, T], BF16, 2)
        ffin = AR.alloc([128, NFF, T], BF16)
        ffn_tile(-1)
        for n in range(NT):
            ffn_tile(n)

    S.emit(nc, es)
    es.close()
    return nc


DEPTH_FULL = 4
WCOLS = 2048
WLAY = [("w_ada", D, 6 * D), ("w_in", D, NCH_IN * 128), ("w_conv_out", 512, D), ("w_sc_out", 512, D),
        ("w_uq", 256, 2048), ("w_ukv", 128, 1024), ("w_mla_out", 512, D), ("w_pool", 512, 128),
        ("w_pool_out", 512, D), ("w_o", D, D), ("w_up", D, 2 * DFF), ("w_down", DFF, D)]
WOFF = {}
_o = 0
for _n, _r, _c in WLAY:
    WOFF[_n] = (_o, _r, _c)
    _o += _r * _c
WTOT = _o


WCH = 256


def wrows(n_cores):
    per = n_cores * WCH * WCOLS
    return (WTOT + per - 1) // per * n_cores * WCH
DEBUG = False
DEBUG_NAMES = ("xM", "mg_d", "g2_d", "q_d", "at_d", "cc_d", "ss_d", "kv_dbg")
LAST = {}


def _pt(v, nchunk):
    return np.ascontiguousarray(np.asarray(v, np.float32).reshape(nchunk, 128).T)


def prep_weights(inp, DEPTH, n_cores):
    f = lambda a: np.ascontiguousarray(np.asarray(a, dtype=np.float32))
    w_in = f(inp["w_in"]); b_in = f(inp["b_in"])
    idx = np.zeros(NCH_IN * 128, np.int64)
    valid = np.zeros(NCH_IN * 128, bool)

    def put(chunk, cols, off=0):
        idx[chunk * 128 + off: chunk * 128 + off + len(cols)] = cols
        valid[chunk * 128 + off: chunk * 128 + off + len(cols)] = True
    put(0, np.arange(0, 512)); put(4, np.arange(512, 1024)); put(8, np.arange(1024, 1536))
    put(12, np.arange(1536, 2048)); put(16, np.arange(2048, 2560)); put(20, np.arange(2560, 2816))
    put(22, np.arange(2816, 2944))
    kr = np.arange(2944, 2976)
    put(23, kr, 64)
    put(24, np.concatenate([kr[16:], kr[:16]]), 64)
    put(25, np.arange(2976, 3488))
    put(29, np.arange(3488, 7584))
    w_inP = np.where(valid[None, None, :], w_in[:, :, idx], 0.0).astype(np.float32)
    b_inP = np.where(valid[None, :], b_in[:, idx], 0.0).astype(np.float32)
    w_uq = f(inp["w_uq"])
    hid = np.arange(768).reshape(NH, 96)
    sw = np.concatenate([hid[:, :64], hid[:, 80:96], hid[:, 64:80]], axis=1).reshape(-1)
    w_uq_sw = w_uq[:, :, sw]
    w_uqP = np.zeros((w_uq.shape[0], 256, 2048), np.float32)
    for h in range(NH):
        for v, src_ in enumerate((w_uq, w_uq_sw)):
            s4 = (h // 4) * 2 + v
            c0 = s4 * 512 + (h % 4) * 128
            w_uqP[:, :, c0:c0 + 96] = src_[:, :, h * 96:(h + 1) * 96]
    upidx = []
    for g in range(NUP // 4):
        for ci in range(4):
            ch = (2 * g + ci) if ci < 2 else (NFF + 2 * g + ci - 2)
            upidx.append(np.arange(ch * 128, (ch + 1) * 128))
    upidx = np.concatenate(upidx)
    w_upP = f(inp["w_up"])[:, :, upidx]
    ffn_dwP = f(inp["ffn_dw"])[:, :, upidx]
    vecs = np.zeros((DEPTH, 128, NV), np.float32)

    def setv(name, l, arr):
        vecs[l, :, VEC[name]:VEC[name] + arr.shape[1]] = arr
    for l in range(DEPTH):
        setv("b_ada", l, _pt(inp["b_ada"][l], 48))
        setv("b_in", l, _pt(b_inP[l], NCH_IN))
        setv("conv_dw", l, f(inp["conv_dw"][l]).T.reshape(4, 128, 31).transpose(1, 0, 2).reshape(128, 124))
        setv("cln_g", l, _pt(inp["conv_ln_g"][l], 4)); setv("cln_b", l, _pt(inp["conv_ln_b"][l], 4))
        setv("sc_dw", l, f(inp["sc_dw"][l]).T.reshape(4, 128, 3).transpose(1, 0, 2).reshape(128, 12))
        setv("qn_g", l, _pt(inp["q_norm_g"][l], 2)); setv("kvn_g", l, _pt(inp["kv_norm_g"][l], 1))
        setv("pool_scale", l, _pt(inp["pool_scale"][l], 4))
        setv("ln1_g", l, _pt(inp["ln1_g"][l], 8)); setv("ln1_b", l, _pt(inp["ln1_b"][l], 8))
        setv("ln2_g", l, _pt(inp["ln2_g"][l], 8)); setv("ln2_b", l, _pt(inp["ln2_b"][l], 8))
        setv("ffn_dw", l, ffn_dwP[l].T.reshape(NUP, 128, 3).transpose(1, 0, 2).reshape(128, NUP * 3))
    Wd = {"w_ada": f(inp["w_ada"]), "w_in": w_inP, "w_conv_out": f(inp["w_conv_out"]),
          "w_sc_out": f(inp["w_sc_out"]), "w_uq": w_uqP, "w_ukv": f(inp["w_ukv"]), "w_mla_out": f(inp["w_mla_out"]),
          "w_pool": f(inp["w_pool"]), "w_pool_out": f(inp["w_pool_out"]), "w_o": f(inp["w_o"]),
          "w_up": w_upP, "w_down": f(inp["w_down"])}
    RT = wrows(n_cores)
    blob = np.zeros((DEPTH, RT * WCOLS), np.float32)
    for n_, r_, c_ in WLAY:
        o_ = WOFF[n_][0]
        blob[:, o_:o_ + r_ * c_] = Wd[n_][:DEPTH].reshape(DEPTH, r_ * c_)
    return vecs, blob.reshape(DEPTH, RT, WCOLS)


def run_model(inp, NB, CPS, TPC, DEPTH, layers_per_launch=None):
    n_cores = NB * CPS
    lpl = layers_per_launch or DEPTH
    x = np.asarray(inp["x"], np.float32)
    c = np.asarray(inp["c"], np.float32)
    pos = np.asarray(inp["positions"], np.int32)
    vecs_h, blob_full = prep_weights(inp, DEPTH, n_cores)
    inv = (1.0 / (10000.0 ** (np.arange(0, 32, 2, dtype=np.float32) / np.float32(32)))).astype(np.float32)
    tri = (np.arange(128)[:, None] <= np.arange(128)[None, :]).astype(np.float32)
    shift = np.zeros((128, 64), np.float32)
    shift[64 + np.arange(64), np.arange(64)] = 1.0
    base_maps = []
    for core in range(n_cores):
        b, j = core // CPS, core % CPS
        sl = slice(j * TPC, (j + 1) * TPC)
        cst = np.zeros((128, 16), np.float32)
        cst[64:80, 0] = inv; cst[80:96, 0] = inv
        cst[64:80, 1] = -1.0; cst[80:96, 1] = 1.0
        cst[:, 2] = 1.0 if j > 0 else 0.0
        for r in range(3):
            cst[:, 3 + r] = 1.0 if r < j else 0.0
        for r in range(CPS):
            cst[:, 6 + r] = 1.0 if r == j - 1 else 0.0
        rc = np.zeros((128, 4, 16), np.float32)
        m = {"xT": np.ascontiguousarray(x[b, sl, :].T),
             "posr": np.ascontiguousarray(np.broadcast_to(pos[b, sl][None, :], (128, TPC))).astype(np.int32),
             "cT": _pt(c[b], KD), "cst": cst, "rcnt": rc.reshape(128, 64), "tri": tri, "shift": shift}
        base_maps.append(m)
    nc = build_program(CPS, TPC, lpl, n_cores, alpha_depth=DEPTH)
    res = None
    for l0 in range(0, DEPTH, lpl):
        in_maps = []
        for core in range(n_cores):
            m = dict(base_maps[core])
            if res is not None:
                m["xT"] = np.ascontiguousarray(res.results[core]["outT"])
            m["vecs"] = np.ascontiguousarray(vecs_h[l0:l0 + lpl])
            m["wfull"] = blob_full[l0:l0 + lpl]
            in_maps.append(m)
        res = run_bass_kernel_spmd(nc, in_maps, core_ids=list(range(n_cores)))
    if DEBUG:
        LAST["res"] = res.results
    out = np.zeros((NB, CPS * TPC, D), np.float32)
    for core in range(n_cores):
        b, j = core // CPS, core % CPS
        out[b, j * TPC:(j + 1) * TPC, :] = res.results[core]["outT"].T
    return out


LAYERS_PER_LAUNCH = 4


def kernel(**inputs):
    return run_model(inputs, NB=2, CPS=4, TPC=4096, DEPTH=4, layers_per_launch=LAYERS_PER_LAUNCH)
```

```python
import math
from contextlib import ExitStack
import numpy as np
import concourse.bass as bass
import concourse.mybir as mybir
from concourse.bass_utils import run_bass_kernel_spmd

F32 = mybir.dt.float32
BF16 = mybir.dt.bfloat16
I32 = mybir.dt.int32
AF = mybir.ActivationFunctionType
ALU = mybir.AluOpType

D = 1024
KD = 8
DFF = 2816
NUP = 44
NFF = 22
NH = 8
LN_EPS = 1e-5
RMS_EPS = 1e-6
NCH_IN = 61
HALO = 32
SCALE = 96.0 ** -0.5

VEC = {}
_o = 0
for _n, _w in [("b_ada", 48), ("b_in", NCH_IN), ("conv_dw", 4 * 31), ("cln_g", 4), ("cln_b", 4), ("sc_dw", 12),
               ("qn_g", 2), ("kvn_g", 1), ("pool_scale", 4), ("ln1_g", 8), ("ln1_b", 8), ("ln2_g", 8),
               ("ln2_b", 8), ("ffn_dw", NUP * 3)]:
    VEC[_n] = _o
    _o += _w
NV = _o


class Buf:
    __slots__ = ("name", "last_w", "readers")

    def __init__(self, name):
        self.name = name
        self.last_w = None
        self.readers = []


class Op:
    __slots__ = ("eng", "fn", "deps", "dma", "needs_inc", "sem", "val", "prev_same_sem")

    def __init__(self, eng, fn, deps, dma):
        self.eng, self.fn, self.deps, self.dma = eng, fn, deps, dma
        self.needs_inc = False
        self.sem = None
        self.val = 0
        self.prev_same_sem = None


ENGS = ("pe", "act", "dve", "pool", "sp")
EPOCH = 4000
NDMASEM = 12


class Sched:
    def __init__(self, same_engine_sync=False):
        self.ops = []
        self.same = same_engine_sync
        self.pending_bar = {}

    def add(self, eng, fn, reads=(), writes=(), dma=False):
        idx = len(self.ops)
        deps = set()
        for b in reads:
            if b.last_w is not None:
                deps.add(b.last_w)
        for b in writes:
            if b.last_w is not None:
                deps.add(b.last_w)
            deps.update(b.readers)
        for b in reads:
            b.readers.append(idx)
        for b in writes:
            b.last_w = idx
            b.readers = []
        keep = []
        for d in deps:
            od = self.ops[d]
            if (not od.dma) and od.eng == eng and not dma:
                if eng == "pe" or not self.same:
                    continue
            keep.append(d)
        if eng in self.pending_bar:
            keep = sorted(set(keep) | set(self.pending_bar.pop(eng)))
        self.ops.append(Op(eng, fn, sorted(keep), dma))
        return idx

    def barrier(self):
        deps = []
        for e in ENGS:
            comp = [i for i in range(len(self.ops) - 1, -1, -1) if self.ops[i].eng == e and not self.ops[i].dma][:1]
            deps += comp
            dm = [i for i in range(len(self.ops) - 1, max(-1, len(self.ops) - 4000), -1)
                  if self.ops[i].eng == e and self.ops[i].dma][:NDMASEM]
            deps += dm
        self.pending_bar = {e: sorted(set(deps)) for e in ENGS}

    def emit(self, nc, es):
        ops = self.ops
        for op in ops:
            for d in op.deps:
                ops[d].needs_inc = True
        cnt = {e: 0 for e in ENGS}
        dcnt = {e: 0 for e in ENGS}
        sems = {}

        def getsem(key):
            if key not in sems:
                sems[key] = es.enter_context(nc.semaphore("s_%s_%s_%d" % key))
            return sems[key]

        last_on_dsem = {}
        for i, op in enumerate(ops):
            if op.dma:
                k = dcnt[op.eng]
                dcnt[op.eng] += 1
                key = ("d", op.eng, k % NDMASEM)
                op.sem = key
                op.val = 16 * (k // NDMASEM + 1)
                op.prev_same_sem = last_on_dsem.get(key)
                last_on_dsem[key] = i
            elif op.needs_inc:
                c = cnt[op.eng]
                cnt[op.eng] += 1
                op.sem = ("c", op.eng, c // EPOCH)
                op.val = c % EPOCH + 1
        for op in ops:
            if op.sem is not None:
                getsem(op.sem)
        import os as _os
        if _os.environ.get("KSTATS"):
            print("KSTATS ops", {e: sum(1 for o in ops if o.eng == e) for e in ENGS}, "incs", cnt, "dmas", dcnt,
                  "nsems", len(sems), flush=True)
        block = es.enter_context(nc.Block())
        streams = {e: [i for i, op in enumerate(ops) if op.eng == e] for e in ENGS}

        def run(eng_name, eng):
            known = {}
            for i in streams[eng_name]:
                op = ops[i]
                waits = [(ops[d].sem, ops[d].val) for d in op.deps]
                if op.dma and op.prev_same_sem is not None:
                    p = ops[op.prev_same_sem]
                    waits.append((p.sem, p.val))
                for key, val in waits:
                    if known.get(key, 0) < val:
                        eng.wait_ge(sems[key], val)
                        known[key] = val
                ins = op.fn(eng)
                if op.dma:
                    ins.then_inc(sems[op.sem], 16)
                elif op.needs_inc:
                    ins.then_inc(sems[op.sem], 1)
            for key, s in sems.items():
                if key[0] == "d" and key[1] == eng_name:
                    li = last_on_dsem[key]
                    if known.get(key, 0) < ops[li].val:
                        eng.wait_ge(s, ops[li].val)

        @block.tensor
        def _(e):
            run("pe", e)

        @block.scalar
        def _(e):
            run("act", e)

        @block.vector
        def _(e):
            run("dve", e)

        @block.gpsimd
        def _(e):
            run("pool", e)

        @block.sync
        def _(e):
            run("sp", e)


class Arena:
    def __init__(self, nc, es, nbytes, name="arena"):
        self.t = es.enter_context(nc.sbuf_tensor(name, [128, nbytes // 4], F32))
        self.off = 0
        self.cap = nbytes
        self.peak = 0

    def reset(self):
        self.off = 0

    def alloc(self, shape, dt):
        free = 1
        for s in shape[1:]:
            free *= s
        esz = 4 if dt in (F32, I32) else 2
        nb = (free * esz + 31) // 32 * 32
        assert self.off + nb <= self.cap, ("arena overflow", self.off, nb, self.cap)
        ap = self.t[:, self.off // 4:(self.off + nb) // 4]
        if dt != F32:
            ap = ap.bitcast(dt)
        ap = ap[:, 0:free]
        self.off += nb
        self.peak = max(self.peak, self.off)
        if len(shape) == 3:
            ap = ap.rearrange("p (a b) -> p a b", b=shape[2])
        elif len(shape) == 4:
            ap = ap.rearrange("p (a b c) -> p a b c", b=shape[2], c=shape[3])
        return ap


class Rot:
    def __init__(self, nc, es, name, shape, dt, n):
        self.t = [es.alloc(shape, dt) for i in range(n)]
        self.b = [Buf("%s%d" % (name, i)) for i in range(n)]
        self.i = 0

    def next(self):
        k = self.i % len(self.t)
        self.i += 1
        return self.t[k], self.b[k]


def build_program(CPS, TPC, DEPTH, n_cores, alpha_depth=None):
    T = 512
    NT = TPC // T
    NKC = CPS
    NK = NKC * TPC
    NKB = NK // 128
    groups = [list(range(g * CPS, (g + 1) * CPS)) for g in range(n_cores // CPS)]
    nc = bass.Bass("TRN2", target_bir_lowering=False)
    S = Sched()
    es = ExitStack()

    def din(name, shape, dt=F32):
        return nc.dram_tensor(name, list(shape), dt, kind="ExternalInput").ap()

    def dint(name, shape, dt=F32):
        if DEBUG and name in DEBUG_NAMES:
            return nc.dram_tensor(name, list(shape), dt, kind="ExternalOutput").ap()
        return nc.dram_tensor(name, list(shape), dt, kind="Internal").ap()

    xT_in = din("xT", [D, TPC])
    pos_in = din("posr", [128, TPC], I32)
    cT_in = din("cT", [128, KD])
    cst_in = din("cst", [128, 16])
    rcnt_in = din("rcnt", [128, 4 * 16])
    tri_in = din("tri", [128, 128])
    shift_in = din("shift", [128, 64])
    RT = wrows(n_cores)
    RSH = RT // n_cores
    vecs_in = din("vecs", [DEPTH, 128, NV])
    wfull_in = din("wfull", [DEPTH, RT, WCOLS])
    wfull = [wfull_in[i] for i in range(DEPTH)]
    bwsrc = [Buf("wsrc%d" % i) for i in range(DEPTH)]
    bwfull = [Buf("wfull%d" % i) for i in range(DEPTH)]
    curl = [0]

    def Wl(name):
        off, r, c = WOFF[name]
        return wfull[curl[0]].rearrange("r c -> (r c)")[off:off + r * c].rearrange("(r c) -> r c", c=c)

    def gather_weights(i):
        pass

    out_T = nc.dram_tensor("outT", [D, TPC], F32, kind="ExternalOutput").ap()

    xA = dint("xA", [D, TPC]); bxA = Buf("xA")
    xM = dint("xM", [D, TPC]); bxM = Buf("xM")
    tail_src = dint("tail_src", [D, HALO]); btail_src = Buf("tail_src")
    tail_all = dint("tail_all", [CPS * D, HALO]); btail_all = Buf("tail_all")
    kv_src = dint("kv_src", [NT * 160, T], BF16); bkv_src = [Buf("kv_src%d" % i) for i in range(NT)]
    kv_all = dint("kv_all", [NT * CPS * 160, T], BF16); bkv_all = [Buf("kv_all%d" % i) for i in range(NT)]
    mg_d = dint("mg_d", [D, TPC]); bmg = Buf("mg_d")
    g2_d = dint("g2_d", [D, TPC], BF16); bg2 = Buf("g2_d")
    q_d = dint("q_d", [NH * 96, TPC], BF16); bq = Buf("q_d")
    at_d = dint("at_d", [NH * 64, TPC], BF16); bat = Buf("at_d")
    cc_d = dint("cc_d", [32, TPC]); bcc = Buf("cc_d")
    ss_d = dint("ss_d", [32, TPC]); bss = Buf("ss_d")

    def sb(name, shape, dt=F32):
        return es.enter_context(nc.sbuf_tensor("sb_" + name, list(shape), dt))

    cst = sb("cst", [128, 16]); bcst = Buf("cst")
    rcnt = sb("rcnt", [128, 4, 16])
    tri = sb("tri", [128, 128], BF16)
    shiftm = sb("shiftm", [128, 64])
    epsc = sb("epsc", [128, 4]); bconst = Buf("const")
    ones = sb("ones", [128, 3, 128], BF16)
    ones1k = sb("ones1k", [128, 128], BF16)
    cact = sb("cact", [128, KD], BF16); bcact = Buf("cact")
    mod = sb("mod", [128, 48]); bmod = Buf("mod")
    vecs = sb("vecs", [128, NV]); bvecs = Buf("vecs")
    ps = es.enter_context(nc.psum_tensor("ps", [128, 8, 512], F32))
    bps = [Buf("ps%d" % i) for i in range(8)]
    psi = [0]

    def nps():
        k = psi[0] % 8
        psi[0] += 1
        return k

    AR = Arena(nc, es, 168 * 1024)
    ST = Rot(nc, Arena(nc, es, 8 * 1024, "arena_st"), "st", [128, T], F32, 4)
    rc2 = sb("rc2", [128, 64]); brc2 = Buf("rc2")
    uph = sb("uph", [128, NUP, 2]); buph = Buf("uph")
    tailsb = sb("tailsb", [128, CPS, KD, HALO]); btailsb = Buf("tailsb")
    xhalo = sb("xhalo", [128, KD, HALO]); bxhalo = Buf("xhalo")
    wukv = sb("wukv", [128, 1024], BF16); bwukv = Buf("wukv")
    ALPHA = (2.0 * (alpha_depth or DEPTH)) ** 0.25
    XT = HT = WS = F1 = B1 = F4 = B4 = B8 = F8 = tabs = upb = PW = qh = PT = None
    bufA = bufB = bufP = merged = ffin = kvnT = kT = vaug = None
    bbufA = [Buf("bufA%d" % j) for j in range(4)]
    bbufB = [Buf("bufB%d" % j) for j in range(4)]
    bbufP = [Buf("bufP%d" % j) for j in range(4)]
    bmerged = [Buf("mg%d" % m) for m in range(KD)]
    bffin = [Buf("ffin%d" % i) for i in range(NFF)]
    bkvnT = [Buf("kvnT%d" % r) for r in range(NKC)]
    bkTn = Buf("kTn"); bkTr = Buf("kTr"); bvv = Buf("vaug_v"); bvo = Buf("vaug_o")

    P = lambda a: a

    def dma(q, out, in_, reads, writes):
        S.add(q, lambda e, o=out, i=in_: e.dma_start(out=o, in_=i), reads, writes, dma=True)

    def act(out, in_, func, reads, writes, scale=None, bias=None):
        kw = {}
        if scale is not None:
            kw["scale"] = scale
        if bias is not None:
            kw["bias"] = bias
        S.add("act", lambda e, o=out, i=in_, f=func, kw=kw: e.activation(out=o, in_=i, func=f, **kw), reads, writes)

    def tt(out, in0, in1, op, reads, writes, eng="dve"):
        S.add(eng, lambda e, o=out, a=in0, b=in1, p=op: e.tensor_tensor(out=o, in0=a, in1=b, op=p), reads, writes)

    def ts(out, in0, s1, op0, reads, writes, s2=None, op1=None, eng="dve"):
        if op1 is None:
            S.add(eng, lambda e, o=out, a=in0, s=s1, p=op0: e.tensor_scalar(out=o, in0=a, scalar1=s, scalar2=None, op0=p),
                  reads, writes)
        else:
            S.add(eng, lambda e, o=out, a=in0, s=s1, p=op0, s_2=s2, p1=op1: e.tensor_scalar(
                out=o, in0=a, scalar1=s, scalar2=s_2, op0=p, op1=p1), reads, writes)

    def stt(out, in0, scalar, in1, op0, op1, reads, writes):
        S.add("dve", lambda e, o=out, a=in0, s=scalar, b=in1, p0=op0, p1=op1: e.scalar_tensor_tensor(
            out=o, in0=a, scalar=s, in1=b, op0=p0, op1=p1), reads, writes)

    def mm(out, pairs, reads, writes):
        def fn(e, out=out, pairs=pairs):
            n = len(pairs)
            ins = None
            for i, (l, r) in enumerate(pairs):
                ins = e.matmul(out, lhsT=l, rhs=r, start=(i == 0), stop=(i == n - 1))
            return ins
        S.add("pe", fn, reads, writes)

    def vcol(name, i=0, n=1):
        o = VEC[name] + i
        return vecs[:, o:o + n]

    def wload(src_ap, kparts, ncols, reads=()):
        t, b = WS.next()
        dma("pool", t[:, 0:kparts, 0:ncols], src_ap.rearrange("(k p) c -> p k c", p=128),
            list(reads) + [bwfull[curl[0]]], [b])
        return t, b

    S.add("sp", lambda e: e.dma_start(out=cst[:], in_=cst_in), [], [bcst], dma=True)
    S.add("sp", lambda e: e.dma_start(out=rcnt[:].rearrange("p a b -> p (a b)"), in_=rcnt_in), [], [bconst], dma=True)
    S.add("pool", lambda e: e.dma_start(out=tri[:], in_=tri_in), [], [bconst], dma=True)
    S.add("sp", lambda e: e.dma_start(out=shiftm[:], in_=shift_in), [], [bconst], dma=True)
    for col, v in enumerate([LN_EPS, RMS_EPS, 0.0, 1.0]):
        S.add("dve", lambda e, c=col, v=v: e.memset(epsc[:, c:c + 1], v), [], [bconst])
    for k, v in enumerate([1.0 / 128, 1.0 / 256, 1.0 / 512]):
        S.add("dve", lambda e, k=k, v=v: e.memset(ones[:, k, :], v), [], [bconst])
    S.add("dve", lambda e: e.memset(ones1k[:], 1.0 / 1024), [], [bconst])
    for g_, w_ in enumerate((2, 4, 8, 16)):
        for t_ in range(w_ - 1):
            A_ = 1.0 / (t_ + 1)
            B_ = 1.0 / w_ - A_
            ts(rc2[:, g_ * 16 + t_:g_ * 16 + t_ + 1], cst[:, 2:3], B_, ALU.mult, [bcst], [brc2], s2=A_, op1=ALU.add)
    eps_ln = epsc[:, 0:1]
    eps_rms = epsc[:, 1:2]

    ctmp = sb("ctmp", [128, KD])
    S.add("sp", lambda e: e.dma_start(out=ctmp[:], in_=cT_in), [], [bcact], dma=True)
    act(cact[:], ctmp[:], AF.Silu, [bcact], [bcact])

    R = slice(64, 96)
    TWO_PI = 2.0 * math.pi
    PI_HI = 6.28125
    PI_LO = TWO_PI - PI_HI
    PI_SAFE = 3.14159
    AR.reset()
    posi = AR.alloc([128, T], I32)
    bro = Buf("ropetmp")
    ang = AR.alloc([128, T], F32); kf = AR.alloc([128, T], F32); ki = AR.alloc([128, T], I32)
    rr = AR.alloc([128, T], F32); mk = AR.alloc([128, T], F32); sn = AR.alloc([128, T], F32)
    for n in range(NT):
        cs = slice(n * T, (n + 1) * T)
        dma("sp", posi[R, :], pos_in[64:96, cs], [], [bro])
        S.add("dve", lambda e: e.tensor_copy(out=ang[R, :], in_=posi[R, :]), [bro], [bro])
        ts(ang[R, :], ang[R, :], cst[R, 0:1], ALU.mult, [bro, bcst], [bro])
        ts(kf[R, :], ang[R, :], 1.0 / TWO_PI, ALU.mult, [bro], [bro])
        S.add("dve", lambda e: e.tensor_copy(out=ki[R, :], in_=kf[R, :]), [bro], [bro])
        S.add("dve", lambda e: e.tensor_copy(out=kf[R, :], in_=ki[R, :]), [bro], [bro])
        stt(rr[R, :], kf[R, :], -PI_HI, ang[R, :], ALU.mult, ALU.add, [bro], [bro])
        stt(rr[R, :], kf[R, :], -PI_LO, rr[R, :], ALU.mult, ALU.add, [bro], [bro])

        def wrap(dst, src):
            ts(mk[R, :], src, math.pi, ALU.is_gt, [bro], [bro])
            stt(dst, mk[R, :], -TWO_PI, src, ALU.mult, ALU.add, [bro], [bro])
            ts(mk[R, :], dst, -math.pi, ALU.is_lt, [bro], [bro])
            stt(dst, mk[R, :], TWO_PI, dst, ALU.mult, ALU.add, [bro], [bro])
            ts(dst, dst, -PI_SAFE, ALU.max, [bro], [bro], s2=PI_SAFE, op1=ALU.min)
        wrap(rr[R, :], rr[R, :])
        act(sn[R, :], rr[R, :], AF.Sin, [bro], [bro])
        ts(sn[R, :], sn[R, :], cst[R, 1:2], ALU.mult, [bro, bcst], [bro])
        dma("sp", ss_d[:, cs], sn[R, :], [bro], [bss])
        ts(ang[R, :], rr[R, :], math.pi / 2, ALU.add, [bro], [bro])
        wrap(ang[R, :], ang[R, :])
        act(kf[R, :], ang[R, :], AF.Sin, [bro], [bro])
        dma("sp", cc_d[:, cs], kf[R, :], [bro], [bcc])

    for k in range(KD):
        dma("sp", xA[k * 128:(k + 1) * 128, :], xT_in[k * 128:(k + 1) * 128, :], [], [bxA])

    def exchange_tail(xsrc, bxsrc):
        dma("pool", tail_src, xsrc[:, TPC - HALO:TPC], [bxsrc], [btail_src])
        S.add("pool", lambda e: e.collective_compute("AllGather", ALU.bypass, replica_groups=groups,
                                                     ins=[tail_src], outs=[tail_all]),
              [btail_src], [btail_all])
        dma("pool", tailsb[:].rearrange("p r k h -> p (r k) h"),
            tail_all.rearrange("(rk p) h -> p rk h", p=128), [btail_all], [btailsb])
        xh = xhalo[:].rearrange("p k h -> p (k h)")
        for r in range(CPS):
            src = tailsb[:, r].rearrange("p k h -> p (k h)")
            if r == 0:
                ts(xh, src, cst[:, 6:7], ALU.mult, [btailsb, bcst], [bxhalo])
            else:
                stt(xh, src, cst[:, 6 + r:7 + r], xh, ALU.mult, ALU.add, [btailsb, bcst], [bxhalo])

    def modulate(xt, bxt, N, off):
        ht, bht = HT.next()
        for k in range(KD):
            act(ht[:, k, 0:N], xt[:, k, 0:N], AF.Identity, [bxt, bmod], [bht],
                scale=mod[:, off + 8 + k:off + 9 + k], bias=mod[:, off + k:off + k + 1])
        return ht, bht

    def proj_group(wsrc, c0, ncols, ht, bht, N, kparts=KD):
        wt, bw = wload(wsrc[:, c0:c0 + ncols], kparts, ncols)
        banks = []
        nchunks = (ncols + 127) // 128
        for ci in range(nchunks):
            mcols = min(128, ncols - ci * 128)
            b = nps()
            mm(ps[0:mcols, b, 0:N], [(wt[:, k, ci * 128:ci * 128 + mcols], ht[:, k, 0:N]) for k in range(kparts)],
               [bw, bht], [bps[b]])
            banks.append(b)
        return banks

    def stats(src_bf, bsrc, sq_bf, bsq, nchunk, ones_ap, N):
        bm = nps()
        mm(ps[:, bm, 0:N], [(ones_ap, src_bf[:, j, 0:N]) for j in range(nchunk)], [bsrc, bconst], [bps[bm]])
        be = nps()
        mm(ps[:, be, 0:N], [(ones_ap, sq_bf[:, j, 0:N]) for j in range(nchunk)], [bsq, bconst], [bps[be]])
        return bm, be

    def ln_finish(bm, be, N, eps_ap):
        mean, bmean = ST.next()
        act(mean[:, 0:N], ps[:, bm, 0:N], AF.Copy, [bps[bm]], [bmean])
        var, bvar = ST.next()
        tt(var[:, 0:N], mean[:, 0:N], mean[:, 0:N], ALU.mult, [bmean], [bvar])
        tt(var[:, 0:N], ps[:, be, 0:N], var[:, 0:N], ALU.subtract, [bps[be], bvar], [bvar])
        ts(var[:, 0:N], var[:, 0:N], 0.0, ALU.max, [bvar], [bvar])
        act(var[:, 0:N], var[:, 0:N], AF.Sqrt, [bvar, bconst], [bvar], bias=eps_ap)
        S.add("dve", lambda e, v=var, N=N: e.reciprocal(out=v[:, 0:N], in_=v[:, 0:N]), [bvar], [bvar])
        return mean, bmean, var, bvar

    def layer_norm_full(xin, bxin, gname, bname, N, dst_ap_fn, bdst):
        xb, bxb = B8.next()
        sq, bsq = B8.next()
        for k in range(KD):
            act(xb[:, k, 0:N], xin[:, k, 0:N], AF.Copy, [bxin], [bxb])
            act(sq[:, k, 0:N], xin[:, k, 0:N], AF.Square, [bxin], [bsq])
        bm, be = stats(xb, bxb, sq, bsq, KD, ones1k[:], N)
        mean, bmean, rstd, brstd = ln_finish(bm, be, N, eps_ln)
        for k in range(KD):
            d, bd = F1.next()
            tt(d[:, 0:N], xin[:, k, 0:N], mean[:, 0:N], ALU.subtract, [bxin, bmean], [bd])
            tt(d[:, 0:N], d[:, 0:N], rstd[:, 0:N], ALU.mult, [bd, brstd], [bd])
            act(dst_ap_fn(k), d[:, 0:N], AF.Identity, [bd, bvecs], [bdst], scale=vcol(gname, k), bias=vcol(bname, k))

    for l in range(DEPTH):
        S.barrier()
        AR.reset()
        XT = Rot(nc, AR, "xt", [128, KD, T], F32, 1)
        HT = Rot(nc, AR, "ht", [128, KD, T], BF16, 1)
        WS = Rot(nc, AR, "ws", [128, KD, 512], BF16, 3)
        F1 = Rot(nc, AR, "f1", [128, T], F32, 6)
        B1 = Rot(nc, AR, "b1", [128, T], BF16, 3)
        F4 = Rot(nc, AR, "f4", [128, 4, T], F32, 2)
        B4 = Rot(nc, AR, "b4", [128, 4, T], BF16, 5)
        B8 = Rot(nc, AR, "b8", [128, KD, T], BF16, 1)
        tabs = Rot(nc, AR, "tabs", [128, 2, T], F32, 1)
        PW = Rot(nc, AR, "pw", [128, 16 + T], F32, 3)
        bufA = AR.alloc([128, 4, 30 + T], F32)
        bufB = AR.alloc([128, 4, 2 + T], F32)
        bufP = AR.alloc([128, 4, 16 + T], F32)
        merged = AR.alloc([128, KD, T], F32)
        curl[0] = l
        if l == 0:
            gather_weights(0)
        if l + 1 < DEPTH:
            gather_weights(l + 1)
        dma("sp", vecs[:], vecs_in[l], [], [bvecs])
        for g6 in range(6):
            for half in range(2):
                wt, bw = wload(Wl("w_ada")[:, g6 * D + half * 512: g6 * D + half * 512 + 512], KD, 512)
                for ci in range(4):
                    mcol = g6 * 8 + half * 4 + ci
                    b = nps()
                    mm(ps[:, b, 0:1], [(wt[:, k, ci * 128:(ci + 1) * 128], cact[:, k:k + 1]) for k in range(KD)],
                       [bw, bcact], [bps[b]])
                    act(mod[:, mcol:mcol + 1], ps[:, b, 0:1], AF.Identity, [bps[b], bvecs], [bmod],
                        bias=vcol("b_ada", mcol))
        for g6 in (1, 2, 4, 5):
            ts(mod[:, g6 * 8:(g6 + 1) * 8], mod[:, g6 * 8:(g6 + 1) * 8], 1.0, ALU.add, [bmod], [bmod])

        exchange_tail(xA, bxA)

        wi = Wl("w_in")

        def phase1_tile(n):
            halo = n < 0
            N = HALO if halo else T
            if halo:
                xt, bxt = xhalo, bxhalo
            else:
                xt, bxt = XT.next()
                dma("sp", xt[:], xA[:, n * T:(n + 1) * T].rearrange("(k p) t -> p k t", p=128), [bxA], [bxt])
            ht, bht = modulate(xt, bxt, N, 0)

            bA = proj_group(wi, 0, 512, ht, bht, N)
            bB = proj_group(wi, 512, 512, ht, bht, N)
            for j in range(4):
                sg, bsg = F1.next()
                act(sg[:, 0:N], ps[:, bB[j], 0:N], AF.Sigmoid, [bps[bB[j]], bvecs], [bsg], bias=vcol("b_in", 4 + j))
                if halo:
                    tmp, btmp = F1.next()
                    stt(tmp[:, 0:N], ps[:, bA[j], 0:N], vcol("b_in", j), sg[:, 0:N], ALU.add, ALU.mult,
                        [bps[bA[j]], bsg, bvecs], [btmp])
                    ts(bufA[:, j, 0:30], tmp[:, 2:32], cst[:, 2:3], ALU.mult, [btmp, bcst], [bbufA[j]])
                else:
                    stt(bufA[:, j, 30:30 + T], ps[:, bA[j], 0:T], vcol("b_in", j), sg[:, 0:T], ALU.add, ALU.mult,
                        [bps[bA[j]], bsg, bvecs], [bbufA[j]])
            bC = proj_group(wi, 12 * 128, 512, ht, bht, N)
            bX = proj_group(wi, 16 * 128, 512, ht, bht, N)
            for j in range(4):
                cs_, bcs = F1.next()
                act(cs_[:, 0:N], ps[:, bC[j], 0:N], AF.Identity, [bps[bC[j]], bvecs], [bcs], bias=vcol("b_in", 12 + j))
                if halo:
                    tmp, btmp = F1.next()
                    stt(tmp[:, 0:N], ps[:, bX[j], 0:N], vcol("b_in", 16 + j), cs_[:, 0:N], ALU.add, ALU.mult,
                        [bps[bX[j]], bcs, bvecs], [btmp])
                    ts(bufB[:, j, 0:2], tmp[:, 30:32], cst[:, 2:3], ALU.mult, [btmp, bcst], [bbufB[j]])
                else:
                    stt(bufB[:, j, 2:2 + T], ps[:, bX[j], 0:T], vcol("b_in", 16 + j), cs_[:, 0:T], ALU.add, ALU.mult,
                        [bps[bX[j]], bcs, bvecs], [bbufB[j]])
            bP = proj_group(wi, 25 * 128, 512, ht, bht, N)
            for j in range(4):
                if halo:
                    tmp, btmp = F1.next()
                    act(tmp[:, 0:N], ps[:, bP[j], 0:N], AF.Identity, [bps[bP[j]], bvecs], [btmp], bias=vcol("b_in", 25 + j))
                    ts(bufP[:, j, 0:16], tmp[:, 16:32], cst[:, 2:3], ALU.mult, [btmp, bcst], [bbufP[j]])
                else:
                    act(bufP[:, j, 16:16 + T], ps[:, bP[j], 0:T], AF.Identity, [bps[bP[j]], bvecs], [bbufP[j]],
                        bias=vcol("b_in", 25 + j))
            if halo:
                return

            g2t, bg2t = B8.next()
            for half in range(2):
                bg_ = proj_group(wi, (29 + 2 * 8 + half * 4) * 128, 512, ht, bht, T)
                for ci in range(4):
                    m = half * 4 + ci
                    act(g2t[:, m, :], ps[:, bg_[ci], :], AF.Sigmoid, [bps[bg_[ci]], bvecs], [bg2t],
                        bias=vcol("b_in", 29 + 16 + m))
            dma("sp", g2_d[:, n * T:(n + 1) * T].rearrange("(k p) t -> p k t", p=128), g2t[:], [bg2t], [bg2])

            tb, btb = tabs.next()
            dma("sp", tb[64:96, 0, :], cc_d[:, n * T:(n + 1) * T], [bcc], [btb])
            dma("sp", tb[64:96, 1, :], ss_d[:, n * T:(n + 1) * T], [bss], [btb])
            bQ = proj_group(wi, 20 * 128, 512, ht, bht, T)
            bK2 = proj_group(wi, 24 * 128, 128, ht, bht, T)
            ql, bql = F4.next()
            qsq, bqsq = B4.next()
            for j in range(2):
                act(ql[:, j, :], ps[:, bQ[j], :], AF.Identity, [bps[bQ[j]], bvecs], [bql], bias=vcol("b_in", 20 + j))
                act(qsq[:, j, :], ql[:, j, :], AF.Square, [bql], [bqsq])
            act(ql[:, 2, :], ps[:, bQ[2], :], AF.Identity, [bps[bQ[2]], bvecs], [bql], bias=vcol("b_in", 22))
            act(qsq[:, 2, :], ql[:, 2, :], AF.Square, [bql], [bqsq])
            bqs = nps()
            mm(ps[:, bqs, :], [(ones[:, 1, :], qsq[:, j, :]) for j in range(2)], [bqsq, bconst], [bps[bqs]])
            bks = nps()
            mm(ps[:, bks, :], [(ones[:, 0, :], qsq[:, 2, :])], [bqsq, bconst], [bps[bks]])
            qn, bqn = B4.next()
            for (bb, js, gname) in ((bqs, (0, 1), "qn_g"), (bks, (2,), "kvn_g")):
                rs, brs = F1.next()
                act(rs[:], ps[:, bb, :], AF.Sqrt, [bps[bb], bconst], [brs], bias=eps_rms)
                S.add("dve", lambda e, v=rs: e.reciprocal(out=v[:], in_=v[:]), [brs], [brs])
                for j in js:
                    stt(qn[:, j, :], ql[:, j, :], vcol(gname, j if gname == "qn_g" else 0), rs[:], ALU.mult, ALU.mult,
                        [bql, brs, bvecs], [bqn])
            dma("sp", kv_src[n * 160:n * 160 + 128, :], qn[:, 2, :], [bqn], [bkv_src[n]])
            kr, bkr = F1.next()
            kr2, bkr2 = F1.next()
            stt(kr[R, :], ps[R, bQ[3], :], vecs[R, VEC["b_in"] + 23:VEC["b_in"] + 24], tb[R, 0, :], ALU.add, ALU.mult,
                [bps[bQ[3]], btb, bvecs], [bkr])
            stt(kr2[R, :], ps[R, bK2[0], :], vecs[R, VEC["b_in"] + 24:VEC["b_in"] + 25], tb[R, 1, :], ALU.add, ALU.mult,
                [bps[bK2[0]], btb, bvecs], [bkr2])
            krb, bkrb = B1.next()
            tt(krb[R, :], kr[R, :], kr2[R, :], ALU.add, [bkr, bkr2], [bkrb])
            dma("sp", kv_src[n * 160 + 128:(n + 1) * 160, :], krb[R, :], [bkrb], [bkv_src[n]])
            S.add("pool", lambda e, n=n: e.collective_compute(
                "AllGather", ALU.bypass, replica_groups=groups, ins=[kv_src[n * 160:(n + 1) * 160, :]],
                outs=[kv_all[n * CPS * 160:(n + 1) * CPS * 160, :]]), [bkv_src[n]], [bkv_all[n]])
            wqs = []
            for s4 in range(4):
                wq_, bwq_ = WS.next()
                dma("pool", wq_[:, 0:2, 0:512], Wl("w_uq")[:, s4 * 512:(s4 + 1) * 512].rearrange(
                    "(k p) c -> p k c", p=128), [bwfull[l]], [bwq_])
                outs4 = []
                for h4 in range(4):
                    b = nps()
                    mm(ps[0:96, b, :], [(wq_[:, k, h4 * 128:h4 * 128 + 96], qn[:, k, :]) for k in range(2)],
                       [bwq_, bqn], [bps[b]])
                    outs4.append(b)
                wqs.append(outs4)
                if s4 % 2 == 0:
                    continue
                for h4 in range(4):
                    h = (s4 // 2) * 4 + h4
                    outs = [wqs[s4 - 1][h4], wqs[s4][h4]]
                    qo, bqo = B1.next()
                    act(qo[0:64, :], ps[0:64, outs[0], :], AF.Copy, [bps[outs[0]]], [bqo])
                    t1, bt1 = F1.next()
                    t2, bt2 = F1.next()
                    tt(t1[R, :], ps[R, outs[0], :], tb[R, 0, :], ALU.mult, [bps[outs[0]], btb], [bt1])
                    tt(t2[R, :], ps[R, outs[1], :], tb[R, 1, :], ALU.mult, [bps[outs[1]], btb], [bt2])
                    tt(qo[R, :], t1[R, :], t2[R, :], ALU.add, [bt1, bt2], [bqo])
                    dma("sp", q_d[h * 96:(h + 1) * 96, n * T:(n + 1) * T], qo[0:96, :], [bqo], [bq])

            accA, baccA = F4.next()
            for j in range(4):
                for k in range(31):
                    wk = vcol("conv_dw", j * 31 + k)
                    if k == 0:
                        ts(accA[:, j, :], bufA[:, j, 0:T], wk, ALU.mult, [bbufA[j], bvecs], [baccA])
                    else:
                        stt(accA[:, j, :], bufA[:, j, k:k + T], wk, accA[:, j, :], ALU.mult, ALU.add,
                            [bbufA[j], bvecs], [baccA])
                ts(bufA[:, j, 0:30], bufA[:, j, T:T + 30], 1.0, ALU.mult, [bbufA[j]], [bbufA[j]])
            ab, bab = B4.next()
            sq, bsq = B4.next()
            for j in range(4):
                act(ab[:, j, :], accA[:, j, :], AF.Copy, [baccA], [bab])
                act(sq[:, j, :], accA[:, j, :], AF.Square, [baccA], [bsq])
            bm, be = stats(ab, bab, sq, bsq, 4, ones[:, 2, :], T)
            mean, bmean, rstd, brstd = ln_finish(bm, be, T, eps_ln)
            yA, byA = B4.next()
            for j in range(4):
                d, bd = F1.next()
                tt(d[:], accA[:, j, :], mean[:], ALU.subtract, [baccA, bmean], [bd])
                tt(d[:], d[:], rstd[:], ALU.mult, [bd, brstd], [bd])
                act(yA[:, j, :], d[:], AF.Silu, [bd, bvecs], [byA], scale=vcol("cln_g", j), bias=vcol("cln_b", j))

            bG = proj_group(wi, 8 * 128, 512, ht, bht, T)
            yB, byB = B4.next()
            for j in range(4):
                acc, bacc = F1.next()
                for k in range(3):
                    wk = vcol("sc_dw", j * 3 + k)
                    if k == 0:
                        ts(acc[:], bufB[:, j, 0:T], wk, ALU.mult, [bbufB[j], bvecs], [bacc])
                    else:
                        stt(acc[:], bufB[:, j, k:k + T], wk, acc[:], ALU.mult, ALU.add, [bbufB[j], bvecs], [bacc])
                ts(bufB[:, j, 0:2], bufB[:, j, T:T + 2], 1.0, ALU.mult, [bbufB[j]], [bbufB[j]])
                stt(yB[:, j, :], ps[:, bG[j], :], vcol("b_in", 8 + j), acc[:], ALU.add, ALU.mult,
                    [bps[bG[j]], bacc, bvecs], [byB])

            pd, bpd = B4.next()
            for g in range(4):
                w = 2 << g
                cur = bufP[:, g, :]
                bcur = bbufP[g]
                span = 16 + T
                sh = 1
                srcs = (cur, bcur)
                while sh < w:
                    tmpP, btmpP = PW.next()
                    tt(tmpP[:, sh:span], srcs[0][:, sh:span], srcs[0][:, 0:span - sh], ALU.add, [srcs[1]], [btmpP])
                    srcs = (tmpP, btmpP)
                    sh *= 2
                o, bo = F1.next()
                stt(o[:], srcs[0][:, 16:16 + T], 1.0 / w, bufP[:, g, 16:16 + T], ALU.mult, ALU.subtract,
                    [srcs[1], bbufP[g]], [bo])
                if n == 0:
                    for t_ in range(w - 1):
                        stt(o[:, t_:t_ + 1], srcs[0][:, 16 + t_:17 + t_], rc2[:, g * 16 + t_:g * 16 + t_ + 1],
                            bufP[:, g, 16 + t_:17 + t_], ALU.mult, ALU.subtract, [srcs[1], bbufP[g], brc2], [bo])
                act(pd[:, g, :], o[:], AF.Copy, [bo], [bpd])
                ts(bufP[:, g, 0:16], bufP[:, g, T:T + 16], 1.0, ALU.mult, [bbufP[g]], [bbufP[g]])
            wpt, bwp = WS.next()
            dma("pool", wpt[:, 0:4, 0:128], Wl("w_pool").rearrange("(g c) d -> c g d", c=128), [bwfull[l]], [bwp])
            yD, byD = B4.next()
            for g in range(4):
                b = nps()
                mm(ps[:, b, :], [(wpt[:, g, 0:128], pd[:, g, :])], [bwp, bpd], [bps[b]])
                act(yD[:, g, :], ps[:, b, :], AF.Copy, [bps[b], bvecs], [byD], scale=vcol("pool_scale", g))

            for bi, (wname, ysrc, bysrc) in enumerate([("w_conv_out", yA, byA), ("w_sc_out", yB, byB),
                                                      ("w_pool_out", yD, byD)]):
                gate_b = (0, 1, 3)[bi]
                for half in range(2):
                    bo_ = proj_group(Wl(wname), half * 512, 512, ysrc, bysrc, T, kparts=4)
                    bg_ = proj_group(wi, (29 + gate_b * 8 + half * 4) * 128, 512, ht, bht, T)
                    for ci in range(4):
                        m = half * 4 + ci
                        gt, bgt = F1.next()
                        act(gt[:], ps[:, bg_[ci], :], AF.Sigmoid, [bps[bg_[ci]], bvecs], [bgt],
                            bias=vcol("b_in", 29 + gate_b * 8 + m))
                        if bi == 0:
                            tt(merged[:, m, :], gt[:], ps[:, bo_[ci], :], ALU.mult, [bgt, bps[bo_[ci]]], [bmerged[m]])
                        else:
                            tt(gt[:], gt[:], ps[:, bo_[ci], :], ALU.mult, [bgt, bps[bo_[ci]]], [bgt])
                            tt(merged[:, m, :], merged[:, m, :], gt[:], ALU.add, [bgt, bmerged[m]], [bmerged[m]],
                               eng="pool")
            dma("sp", mg_d[:, n * T:(n + 1) * T].rearrange("(k p) t -> p k t", p=128), merged[:], bmerged, [bmg])
        phase1_tile(-1)
        for n in range(NT):
            phase1_tile(n)

        S.barrier()
        AR.reset()
        kvnT = AR.alloc([128, NK], BF16)
        kT = AR.alloc([128, NK], BF16)
        vaug = AR.alloc([128, NKB, 128], BF16)
        qh = Rot(nc, AR, "qh", [128, TPC], BF16, 2)
        PT = Rot(nc, AR, "pt", [128, 2, T], BF16, 3)
        F1 = Rot(nc, AR, "f1", [128, T], F32, 6)
        B1 = Rot(nc, AR, "b1", [128, T], BF16, 3)
        for r in range(NKC):
            kb0 = r * (TPC // 128)
            kb1 = (r + 1) * (TPC // 128)
            S.add("pool", lambda e, kb0=kb0, kb1=kb1: e.memset(vaug[:, kb0:kb1, 64:128], 1.0), [], [bvo])
            if r < NKC - 1:
                S.add("pool", lambda e, kb0=kb0, kb1=kb1, r=r: e.tensor_scalar(
                    out=vaug[:, kb0:kb1, 64:128], in0=vaug[:, kb0:kb1, 64:128], scalar1=cst[:, 3 + r:4 + r],
                    scalar2=None, op0=ALU.mult), [bcst, bvo], [bvo])
        for r in range(NKC):
            for n in range(NT):
                cs = slice(r * TPC + n * T, r * TPC + (n + 1) * T)
                if r < NKC - 1:
                    r0 = (n * CPS + r) * 160
                    dma("sp", kvnT[:, cs], kv_all[r0:r0 + 128, :], [bkv_all[n]], [bkvnT[r]])
                    dma("sp", kT[64:96, cs], kv_all[r0 + 128:r0 + 160, :], [bkv_all[n]], [bkTr])
                else:
                    dma("sp", kvnT[:, cs], kv_src[n * 160:n * 160 + 128, :], [bkv_src[n]], [bkvnT[r]])
                    dma("sp", kT[64:96, cs], kv_src[n * 160 + 128:(n + 1) * 160, :], [bkv_src[n]], [bkTr])
        dma("pool", wukv[:], Wl("w_ukv"), [bwfull[l]], [bwukv])
        OB = (4, 5)
        oi = 0
        for h in range(NH):
            for kb5 in range(NK // 512):
                b = 6 + (kb5 % 2)
                mm(ps[0:64, b, :], [(wukv[:, h * 128:h * 128 + 64], kvnT[:, kb5 * 512:(kb5 + 1) * 512])],
                   [bwukv] + bkvnT, [bps[b]])
                act(kT[0:64, kb5 * 512:(kb5 + 1) * 512], ps[0:64, b, :], AF.Copy, [bps[b]], [bkTn])
            for kb8 in range(NKB // 8):
                b = 6 + (kb8 % 2)
                for i8 in range(8):
                    kb = kb8 * 8 + i8
                    mm(ps[:, b, i8 * 64:(i8 + 1) * 64], [(kvnT[:, kb * 128:(kb + 1) * 128],
                                                          wukv[:, h * 128 + 64:(h + 1) * 128])],
                       [bwukv] + bkvnT, [bps[b]])
                r = (kb8 * 8 * 128) // TPC
                src = ps[:, b, :].rearrange("p (a d) -> p a d", d=64)
                if r < NKC - 1:
                    act(vaug[:, kb8 * 8:(kb8 + 1) * 8, 0:64], src, AF.Copy, [bps[b], bcst], [bvv],
                        scale=cst[:, 3 + r:4 + r])
                else:
                    act(vaug[:, kb8 * 8:(kb8 + 1) * 8, 0:64], src, AF.Copy, [bps[b]], [bvv])
            qt, bqt = qh.next()
            dma("sp", qt[0:96, :], q_d[h * 96:(h + 1) * 96, :], [bq], [bqt])
            for n in range(NT):
                kbl = [(kb, 0, False) for kb in range((NKC - 1) * (TPC // 128))]
                own0 = (NKC - 1) * (TPC // 128)
                for kbo in range(4 * n + 4):
                    i = kbo - 4 * n
                    if i < 0:
                        kbl.append((own0 + kbo, 0, False))
                    else:
                        kbl.append((own0 + kbo, i * 128, True))
                ob = OB[oi % 2]
                oi += 1
                nblk = len(kbl)
                pairs = [kbl[i:i + 2] for i in range(0, nblk, 2)]
                npairs = len(pairs)

                def emit_S(pi):
                    sb0 = (pi % 2) * 2
                    for e_, (kb, c0, dg) in enumerate(pairs[pi]):
                        mm(ps[:, sb0 + e_, c0:T], [(kT[0:96, kb * 128:(kb + 1) * 128], qt[0:96, n * T + c0:(n + 1) * T])],
                           [bkTn, bkTr, bqt], [bps[sb0 + e_]])

                emit_S(0)
                for pi in range(npairs):
                    pair = pairs[pi]
                    sb0 = (pi % 2) * 2
                    if pi + 1 < npairs:
                        emit_S(pi + 1)
                    pt, bpt = PT.next()
                    if len(pair) == 2 and pair[0][1] == 0 and pair[1][1] == 0:
                        act(pt[:, 0:2, :], ps[:, sb0:sb0 + 2, :], AF.Exp, [bps[sb0], bps[sb0 + 1]], [bpt], scale=SCALE)
                    else:
                        for e_, (kb, c0, dg) in enumerate(pair):
                            act(pt[:, e_, c0:T], ps[:, sb0 + e_, c0:T], AF.Exp, [bps[sb0 + e_]], [bpt], scale=SCALE)
                    for e_, (kb, c0, dg) in enumerate(pair):
                        if dg:
                            tt(pt[:, e_, c0:c0 + 128], pt[:, e_, c0:c0 + 128], tri[:], ALU.mult, [bpt, bconst], [bpt],
                               eng="pool")
                    for e_, (kb, c0, dg) in enumerate(pair):
                        gi = pi * 2 + e_
                        S.add("pe", lambda e, ob=ob, kb=kb, c0=c0, pt=pt, e_=e_, st=(gi == 0), sp_=(gi == nblk - 1):
                              e.matmul(ps[:, ob, c0:T], lhsT=vaug[:, kb, :], rhs=pt[:, e_, c0:T], start=st, stop=sp_),
                              [bvv, bvo, bpt], [bps[ob]])
                osb, bosb = F1.next()
                act(osb[:], ps[:, ob, :], AF.Copy, [bps[ob]], [bosb])
                bd_ = 6 + (oi % 2)
                mm(ps[0:64, bd_, :], [(shiftm[:], osb[:])], [bosb, bconst], [bps[bd_]])
                rd, brd = F1.next()
                S.add("dve", lambda e, rd=rd, bd_=bd_: e.reciprocal(out=rd[0:64, :], in_=ps[0:64, bd_, :]),
                      [bps[bd_]], [brd])
                ao, bao = B1.next()
                tt(ao[0:64, :], osb[0:64, :], rd[0:64, :], ALU.mult, [bosb, brd], [bao])
                dma("sp", at_d[h * 64:(h + 1) * 64, n * T:(n + 1) * T], ao[0:64, :], [bao], [bat])

        S.barrier()
        AR.reset()
        XT = Rot(nc, AR, "xt", [128, KD, T], F32, 2)
        F8 = Rot(nc, AR, "f8", [128, KD, T], F32, 2)
        F1 = Rot(nc, AR, "f1", [128, T], F32, 6)
        B4 = Rot(nc, AR, "b4", [128, 4, T], BF16, 2)
        B8 = Rot(nc, AR, "b8", [128, KD, T], BF16, 4)
        WS = Rot(nc, AR, "ws", [128, KD, 512], BF16, 3)
        for n in range(NT):
            cs = slice(n * T, (n + 1) * T)
            at, bat_t = B4.next()
            dma("sp", at[:], at_d[:, cs].rearrange("(k p) t -> p k t", p=128), [bat], [bat_t])
            mgt, bmgt = F8.next()
            dma("sp", mgt[:], mg_d[:, cs].rearrange("(k p) t -> p k t", p=128), [bmg], [bmgt])
            g2t, bg2t = B8.next()
            dma("sp", g2t[:], g2_d[:, cs].rearrange("(k p) t -> p k t", p=128), [bg2], [bg2t])
            xt, bxt = XT.next()
            dma("sp", xt[:], xA[:, cs].rearrange("(k p) t -> p k t", p=128), [bxA], [bxt])
            mb, bmb = B8.next()
            for half in range(2):
                bo_ = proj_group(Wl("w_mla_out"), half * 512, 512, at, bat_t, T, kparts=4)
                for ci in range(4):
                    m = half * 4 + ci
                    tmp, btmp = F1.next()
                    tt(tmp[:], g2t[:, m, :], ps[:, bo_[ci], :], ALU.mult, [bg2t, bps[bo_[ci]]], [btmp])
                    tt(mb[:, m, :], tmp[:], mgt[:, m, :], ALU.add, [btmp, bmgt], [bmb], eng="pool")
            x1, bx1 = F8.next()
            for half in range(2):
                bo_ = proj_group(Wl("w_o"), half * 512, 512, mb, bmb, T)
                for ci in range(4):
                    m = half * 4 + ci
                    tmp, btmp = F1.next()
                    act(tmp[:], ps[:, bo_[ci], :], AF.Copy, [bps[bo_[ci]], bmod], [btmp], scale=mod[:, 16 + m:17 + m])
                    stt(x1[:, m, :], xt[:, m, :], ALPHA, tmp[:], ALU.mult, ALU.add, [bxt, btmp], [bx1])
            xo, bxo = XT.next()
            layer_norm_full(x1, bx1, "ln1_g", "ln1_b", T, lambda k, xo=xo: xo[:, k, :], bxo)
            dma("sp", xM[:, cs].rearrange("(k p) t -> p k t", p=128), xo[:], [bxo], [bxM])

        exchange_tail(xM, bxM)

        def ffn_tile(n):
            halo = n < 0
            N = HALO if halo else T
            if halo:
                xt, bxt = xhalo, bxhalo
            else:
                xt, bxt = XT.next()
                dma("sp", xt[:], xM[:, n * T:(n + 1) * T].rearrange("(k p) t -> p k t", p=128), [bxM], [bxt])
            ht, bht = modulate(xt, bxt, N, 24)
            vals = [None, None]
            for g in range(NUP // 4):
                banks = proj_group(Wl("w_up"), g * 512, 512, ht, bht, N)
                for ci in range(4):
                    c = g * 4 + ci
                    if halo:
                        ts(uph[:, c, :], ps[:, banks[ci], 30:32], cst[:, 2:3], ALU.mult, [bps[banks[ci]], bcst], [buph])
                        continue
                    ub, bub = upb.next()
                    act(ub[:, 2:2 + T], ps[:, banks[ci], :], AF.Copy, [bps[banks[ci]]], [bub])
                    ts(ub[:, 0:2], uph[:, c, :], 1.0, ALU.mult, [buph], [bub], eng="pool")
                    ts(uph[:, c, :], ub[:, T:T + 2], 1.0, ALU.mult, [bub], [buph], eng="pool")
                    acc, bacc = F1.next()
                    for k in range(3):
                        wk = vcol("ffn_dw", c * 3 + k)
                        if k == 0:
                            ts(acc[:], ub[:, 0:T], wk, ALU.mult, [bub, bvecs], [bacc])
                        else:
                            stt(acc[:], ub[:, k:k + T], wk, acc[:], ALU.mult, ALU.add, [bub, bvecs], [bacc])
                    if ci < 2:
                        vals[ci] = (acc, bacc)
                    else:
                        i = g * 2 + ci - 2
                        sg, bsg = F1.next()
                        act(sg[:], acc[:], AF.Silu, [bacc], [bsg])
                        tt(ffin[:, i, :], sg[:], vals[ci - 2][0][:], ALU.mult, [bsg, vals[ci - 2][1]], [bffin[i]])
            if halo:
                return
            x2, bx2 = F8.next()
            for half in range(2):
                pb = []
                for ci in range(4):
                    pb.append(nps())
                kgs = [(0, 8), (8, 8), (16, 6)]
                wts = []
                for (k0, kn) in kgs:
                    wt, bw = wload(Wl("w_down")[k0 * 128:(k0 + kn) * 128, half * 512:(half + 1) * 512], kn, 512)
                    wts.append((wt, bw, k0, kn))
                for ci in range(4):
                    pairs = []
                    rd_ = []
                    for (wt, bw, k0, kn) in wts:
                        for k in range(kn):
                            pairs.append((wt[:, k, ci * 128:(ci + 1) * 128], ffin[:, k0 + k, :]))
                        rd_.append(bw)
                    mm(ps[:, pb[ci], :], pairs, rd_ + bffin, [bps[pb[ci]]])
                    m = half * 4 + ci
                    tmp, btmp = F1.next()
                    act(tmp[:], ps[:, pb[ci], :], AF.Copy, [bps[pb[ci]], bmod], [btmp], scale=mod[:, 40 + m:41 + m])
                    stt(x2[:, m, :], xt[:, m, :], ALPHA, tmp[:], ALU.mult, ALU.add, [bxt, btmp], [bx2])
            xo, bxo = XT.next()
            layer_norm_full(x2, bx2, "ln2_g", "ln2_b", T, lambda k, xo=xo: xo[:, k, :], bxo)
            if l == DEPTH - 1:
                dma("sp", out_T[:, n * T:(n + 1) * T].rearrange("(k p) t -> p k t", p=128), xo[:], [bxo], [Buf("o")])
            else:
                dma("sp", xA[:, n * T:(n + 1) * T].rearrange("(k p) t -> p k t", p=128), xo[:], [bxo], [bxA])

        S.barrier()
        AR.reset()
        XT = Rot(nc, AR, "xt", [128, KD, T], F32, 2)
        HT = Rot(nc, AR, "ht", [128, KD, T], BF16, 1)
        WS = Rot(nc, AR, "ws", [128, KD, 512], BF16, 4)
        upb = Rot(nc, AR, "upb", [128, 2 + T], F32, 4)
        F1 = Rot(nc, AR, "f1", [128, T], F32, 8)
        F8 = Rot(nc, AR, "f8", [128, KD, T], F32, 1)
        B8 = Rot(nc, AR, "b8", [128, KD, T], BF16, 2)
        ffin = AR.alloc([128, NFF, T], BF16)
        ffn_tile(-1)
        for n in range(NT):
            ffn_tile(n)

    S.emit(nc, es)
    es.close()
    return nc


DEPTH_FULL = 4
WCOLS = 2048
WLAY = [("w_ada", D, 6 * D), ("w_in", D, NCH_IN * 128), ("w_conv_out", 512, D), ("w_sc_out", 512, D),
        ("w_uq", 256, 2048), ("w_ukv", 128, 1024), ("w_mla_out", 512, D), ("w_pool", 512, 128),
        ("w_pool_out", 512, D), ("w_o", D, D), ("w_up", D, 2 * DFF), ("w_down", DFF, D)]
WOFF = {}
_o = 0
for _n, _r, _c in WLAY:
    WOFF[_n] = (_o, _r, _c)
    _o += _r * _c
WTOT = _o


WCH = 256


def wrows(n_cores):
    per = n_cores * WCH * WCOLS
    return (WTOT + per - 1) // per * n_cores * WCH
DEBUG = False
DEBUG_NAMES = ("xM", "mg_d", "g2_d", "q_d", "at_d", "cc_d", "ss_d", "kv_dbg")
LAST = {}


def _pt(v, nchunk):
    return np.ascontiguousarray(np.asarray(v, np.float32).reshape(nchunk, 128).T)


def prep_weights(inp, DEPTH, n_cores):
    f = lambda a: np.ascontiguousarray(np.asarray(a, dtype=np.float32))
    w_in = f(inp["w_in"]); b_in = f(inp["b_in"])
    idx = np.zeros(NCH_IN * 128, np.int64)
    valid = np.zeros(NCH_IN * 128, bool)

    def put(chunk, cols, off=0):
        idx[chunk * 128 + off: chunk * 128 + off + len(cols)] = cols
        valid[chunk * 128 + off: chunk * 128 + off + len(cols)] = True
    put(0, np.arange(0, 512)); put(4, np.arange(512, 1024)); put(8, np.arange(1024, 1536))
    put(12, np.arange(1536, 2048)); put(16, np.arange(2048, 2560)); put(20, np.arange(2560, 2816))
    put(22, np.arange(2816, 2944))
    kr = np.arange(2944, 2976)
    put(23, kr, 64)
    put(24, np.concatenate([kr[16:], kr[:16]]), 64)
    put(25, np.arange(2976, 3488))
    put(29, np.arange(3488, 7584))
    w_inP = np.where(valid[None, None, :], w_in[:, :, idx], 0.0).astype(np.float32)
    b_inP = np.where(valid[None, :], b_in[:, idx], 0.0).astype(np.float32)
    w_uq = f(inp["w_uq"])
    hid = np.arange(768).reshape(NH, 96)
    sw = np.concatenate([hid[:, :64], hid[:, 80:96], hid[:, 64:80]], axis=1).reshape(-1)
    w_uq_sw = w_uq[:, :, sw]
    w_uqP = np.zeros((w_uq.shape[0], 256, 2048), np.float32)
    for h in range(NH):
        for v, src_ in enumerate((w_uq, w_uq_sw)):
            s4 = (h // 4) * 2 + v
            c0 = s4 * 512 + (h % 4) * 128
            w_uqP[:, :, c0:c0 + 96] = src_[:, :, h * 96:(h + 1) * 96]
    upidx = []
    for g in range(NUP // 4):
        for ci in range(4):
            ch = (2 * g + ci) if ci < 2 else (NFF + 2 * g + ci - 2)
            upidx.append(np.arange(ch * 128, (ch + 1) * 128))
    upidx = np.concatenate(upidx)
    w_upP = f(inp["w_up"])[:, :, upidx]
    ffn_dwP = f(inp["ffn_dw"])[:, :, upidx]
    vecs = np.zeros((DEPTH, 128, NV), np.float32)

    def setv(name, l, arr):
        vecs[l, :, VEC[name]:VEC[name] + arr.shape[1]] = arr
    for l in range(DEPTH):
        setv("b_ada", l, _pt(inp["b_ada"][l], 48))
        setv("b_in", l, _pt(b_inP[l], NCH_IN))
        setv("conv_dw", l, f(inp["conv_dw"][l]).T.reshape(4, 128, 31).transpose(1, 0, 2).reshape(128, 124))
        setv("cln_g", l, _pt(inp["conv_ln_g"][l], 4)); setv("cln_b", l, _pt(inp["conv_ln_b"][l], 4))
        setv("sc_dw", l, f(inp["sc_dw"][l]).T.reshape(4, 128, 3).transpose(1, 0, 2).reshape(128, 12))
        setv("qn_g", l, _pt(inp["q_norm_g"][l], 2)); setv("kvn_g", l, _pt(inp["kv_norm_g"][l], 1))
        setv("pool_scale", l, _pt(inp["pool_scale"][l], 4))
        setv("ln1_g", l, _pt(inp["ln1_g"][l], 8)); setv("ln1_b", l, _pt(inp["ln1_b"][l], 8))
        setv("ln2_g", l, _pt(inp["ln2_g"][l], 8)); setv("ln2_b", l, _pt(inp["ln2_b"][l], 8))
        setv("ffn_dw", l, ffn_dwP[l].T.reshape(NUP, 128, 3).transpose(1, 0, 2).reshape(128, NUP * 3))
    Wd = {"w_ada": f(inp["w_ada"]), "w_in": w_inP, "w_conv_out": f(inp["w_conv_out"]),
          "w_sc_out": f(inp["w_sc_out"]), "w_uq": w_uqP, "w_ukv": f(inp["w_ukv"]), "w_mla_out": f(inp["w_mla_out"]),
          "w_pool": f(inp["w_pool"]), "w_pool_out": f(inp["w_pool_out"]), "w_o": f(inp["w_o"]),
          "w_up": w_upP, "w_down": f(inp["w_down"])}
    RT = wrows(n_cores)
    blob = np.zeros((DEPTH, RT * WCOLS), np.float32)
    for n_, r_, c_ in WLAY:
        o_ = WOFF[n_][0]
        blob[:, o_:o_ + r_ * c_] = Wd[n_][:DEPTH].reshape(DEPTH, r_ * c_)
    return vecs, blob.reshape(DEPTH, RT, WCOLS)


def run_model(inp, NB, CPS, TPC, DEPTH, layers_per_launch=None):
    n_cores = NB * CPS
    lpl = layers_per_launch or DEPTH
    x = np.asarray(inp["x"], np.float32)
    c = np.asarray(inp["c"], np.float32)
    pos = np.asarray(inp["positions"], np.int32)
    vecs_h, blob_full = prep_weights(inp, DEPTH, n_cores)
    inv = (1.0 / (10000.0 ** (np.arange(0, 32, 2, dtype=np.float32) / np.float32(32)))).astype(np.float32)
    tri = (np.arange(128)[:, None] <= np.arange(128)[None, :]).astype(np.float32)
    shift = np.zeros((128, 64), np.float32)
    shift[64 + np.arange(64), np.arange(64)] = 1.0
    base_maps = []
    for core in range(n_cores):
        b, j = core // CPS, core % CPS
        sl = slice(j * TPC, (j + 1) * TPC)
        cst = np.zeros((128, 16), np.float32)
        cst[64:80, 0] = inv; cst[80:96, 0] = inv
        cst[64:80, 1] = -1.0; cst[80:96, 1] = 1.0
        cst[:, 2] = 1.0 if j > 0 else 0.0
        for r in range(3):
            cst[:, 3 + r] = 1.0 if r < j else 0.0
        for r in range(CPS):
            cst[:, 6 + r] = 1.0 if r == j - 1 else 0.0
        rc = np.zeros((128, 4, 16), np.float32)
        m = {"xT": np.ascontiguousarray(x[b, sl, :].T),
             "posr": np.ascontiguousarray(np.broadcast_to(pos[b, sl][None, :], (128, TPC))).astype(np.int32),
             "cT": _pt(c[b], KD), "cst": cst, "rcnt": rc.reshape(128, 64), "tri": tri, "shift": shift}
        base_maps.append(m)
    nc = build_program(CPS, TPC, lpl, n_cores, alpha_depth=DEPTH)
    res = None
    for l0 in range(0, DEPTH, lpl):
        in_maps = []
        for core in range(n_cores):
            m = dict(base_maps[core])
            if res is not None:
                m["xT"] = np.ascontiguousarray(res.results[core]["outT"])
            m["vecs"] = np.ascontiguousarray(vecs_h[l0:l0 + lpl])
            m["wfull"] = blob_full[l0:l0 + lpl]
            in_maps.append(m)
        res = run_bass_kernel_spmd(nc, in_maps, core_ids=list(range(n_cores)))
    if DEBUG:
        LAST["res"] = res.results
    out = np.zeros((NB, CPS * TPC, D), np.float32)
    for core in range(n_cores):
        b, j = core // CPS, core % CPS
        out[b, j * TPC:(j + 1) * TPC, :] = res.results[core]["outT"].T
    return out


LAYERS_PER_LAUNCH = 4


def kernel(**inputs):
    return run_model(inputs, NB=2, CPS=4, TPC=4096, DEPTH=4, layers_per_launch=LAYERS_PER_LAUNCH)
```

```python
import math
from contextlib import ExitStack
import numpy as np
import concourse.bass as bass
import concourse.mybir as mybir
from concourse.bass_utils import run_bass_kernel_spmd

F32 = mybir.dt.float32
BF16 = mybir.dt.bfloat16
I32 = mybir.dt.int32
AF = mybir.ActivationFunctionType
ALU = mybir.AluOpType

D = 1024
KD = 8
DFF = 2816
NUP = 44
NFF = 22
NH = 8
LN_EPS = 1e-5
RMS_EPS = 1e-6
NCH_IN = 61
HALO = 32
SCALE = 96.0 ** -0.5

VEC = {}
_o = 0
for _n, _w in [("b_ada", 48), ("b_in", NCH_IN), ("conv_dw", 4 * 31), ("cln_g", 4), ("cln_b", 4), ("sc_dw", 12),
               ("qn_g", 2), ("kvn_g", 1), ("pool_scale", 4), ("ln1_g", 8), ("ln1_b", 8), ("ln2_g", 8),
               ("ln2_b", 8), ("ffn_dw", NUP * 3)]:
    VEC[_n] = _o
    _o += _w
NV = _o


class Buf:
    __slots__ = ("name", "last_w", "readers")

    def __init__(self, name):
        self.name = name
        self.last_w = None
        self.readers = []


class Op:
    __slots__ = ("eng", "fn", "deps", "dma", "needs_inc", "sem", "val", "prev_same_sem")

    def __init__(self, eng, fn, deps, dma):
        self.eng, self.fn, self.deps, self.dma = eng, fn, deps, dma
        self.needs_inc = False
        self.sem = None
        self.val = 0
        self.prev_same_sem = None


ENGS = ("pe", "act", "dve", "pool", "sp")
EPOCH = 4000
NDMASEM = 12


class Sched:
    def __init__(self, same_engine_sync=False):
        self.ops = []
        self.same = same_engine_sync
        self.pending_bar = {}

    def add(self, eng, fn, reads=(), writes=(), dma=False):
        idx = len(self.ops)
        deps = set()
        for b in reads:
            if b.last_w is not None:
                deps.add(b.last_w)
        for b in writes:
            if b.last_w is not None:
                deps.add(b.last_w)
            deps.update(b.readers)
        for b in reads:
            b.readers.append(idx)
        for b in writes:
            b.last_w = idx
            b.readers = []
        keep = []
        for d in deps:
            od = self.ops[d]
            if (not od.dma) and od.eng == eng and not dma:
                if eng == "pe" or not self.same:
                    continue
            keep.append(d)
        if eng in self.pending_bar:
            keep = sorted(set(keep) | set(self.pending_bar.pop(eng)))
        self.ops.append(Op(eng, fn, sorted(keep), dma))
        return idx

    def barrier(self):
        deps = []
        for e in ENGS:
            comp = [i for i in range(len(self.ops) - 1, -1, -1) if self.ops[i].eng == e and not self.ops[i].dma][:1]
            deps += comp
            dm = [i for i in range(len(self.ops) - 1, max(-1, len(self.ops) - 4000), -1)
                  if self.ops[i].eng == e and self.ops[i].dma][:NDMASEM]
            deps += dm
        self.pending_bar = {e: sorted(set(deps)) for e in ENGS}

    def emit(self, nc, es):
        ops = self.ops
        for op in ops:
            for d in op.deps:
                ops[d].needs_inc = True
        cnt = {e: 0 for e in ENGS}
        dcnt = {e: 0 for e in ENGS}
        sems = {}

        def getsem(key):
            if key not in sems:
                sems[key] = es.enter_context(nc.semaphore("s_%s_%s_%d" % key))
            return sems[key]

        last_on_dsem = {}
        for i, op in enumerate(ops):
            if op.dma:
                k = dcnt[op.eng]
                dcnt[op.eng] += 1
                key = ("d", op.eng, k % NDMASEM)
                op.sem = key
                op.val = 16 * (k // NDMASEM + 1)
                op.prev_same_sem = last_on_dsem.get(key)
                last_on_dsem[key] = i
            elif op.needs_inc:
                c = cnt[op.eng]
                cnt[op.eng] += 1
                op.sem = ("c", op.eng, c // EPOCH)
                op.val = c % EPOCH + 1
        for op in ops:
            if op.sem is not None:
                getsem(op.sem)
        import os as _os
        if _os.environ.get("KSTATS"):
            print("KSTATS ops", {e: sum(1 for o in ops if o.eng == e) for e in ENGS}, "incs", cnt, "dmas", dcnt,
                  "nsems", len(sems), flush=True)
        block = es.enter_context(nc.Block())
        streams = {e: [i for i, op in enumerate(ops) if op.eng == e] for e in ENGS}

        def run(eng_name, eng):
            known = {}
            for i in streams[eng_name]:
                op = ops[i]
                waits = [(ops[d].sem, ops[d].val) for d in op.deps]
                if op.dma and op.prev_same_sem is not None:
                    p = ops[op.prev_same_sem]
                    waits.append((p.sem, p.val))
                for key, val in waits:
                    if known.get(key, 0) < val:
                        eng.wait_ge(sems[key], val)
                        known[key] = val
                ins = op.fn(eng)
                if op.dma:
                    ins.then_inc(sems[op.sem], 16)
                elif op.needs_inc:
                    ins.then_inc(sems[op.sem], 1)
            for key, s in sems.items():
                if key[0] == "d" and key[1] == eng_name:
                    li = last_on_dsem[key]
                    if known.get(key, 0) < ops[li].val:
                        eng.wait_ge(s, ops[li].val)

        @block.tensor
        def _(e):
            run("pe", e)

        @block.scalar
        def _(e):
            run("act", e)

        @block.vector
        def _(e):
            run("dve", e)

        @block.gpsimd
        def _(e):
            run("pool", e)

        @block.sync
        def _(e):
            run("sp", e)


class Arena:
    def __init__(self, nc, es, nbytes, name="arena"):
        self.t = es.enter_context(nc.sbuf_tensor(name, [128, nbytes // 4], F32))
        self.off = 0
        self.cap = nbytes
        self.peak = 0

    def reset(self):
        self.off = 0

    def alloc(self, shape, dt):
        free = 1
        for s in shape[1:]:
            free *= s
        esz = 4 if dt in (F32, I32) else 2
        nb = (free * esz + 31) // 32 * 32
        assert self.off + nb <= self.cap, ("arena overflow", self.off, nb, self.cap)
        ap = self.t[:, self.off // 4:(self.off + nb) // 4]
        if dt != F32:
            ap = ap.bitcast(dt)
        ap = ap[:, 0:free]
        self.off += nb
        self.peak = max(self.peak, self.off)
        if len(shape) == 3:
            ap = ap.rearrange("p (a b) -> p a b", b=shape[2])
        elif len(shape) == 4:
            ap = ap.rearrange("p (a b c) -> p a b c", b=shape[2], c=shape[3])
        return ap


class Rot:
    def __init__(self, nc, es, name, shape, dt, n):
        self.t = [es.alloc(shape, dt) for i in range(n)]
        self.b = [Buf("%s%d" % (name, i)) for i in range(n)]
        self.i = 0

    def next(self):
        k = self.i % len(self.t)
        self.i += 1
        return self.t[k], self.b[k]


def build_program(CPS, TPC, DEPTH, n_cores, alpha_depth=None):
    T = 512
    NT = TPC // T
    NKC = CPS
    NK = NKC * TPC
    NKB = NK // 128
    groups = [list(range(g * CPS, (g + 1) * CPS)) for g in range(n_cores // CPS)]
    nc = bass.Bass("TRN2", target_bir_lowering=False)
    S = Sched()
    es = ExitStack()

    def din(name, shape, dt=F32):
        return nc.dram_tensor(name, list(shape), dt, kind="ExternalInput").ap()

    def dint(name, shape, dt=F32):
        if DEBUG and name in DEBUG_NAMES:
            return nc.dram_tensor(name, list(shape), dt, kind="ExternalOutput").ap()
        return nc.dram_tensor(name, list(shape), dt, kind="Internal").ap()

    xT_in = din("xT", [D, TPC])
    pos_in = din("posr", [128, TPC], I32)
    cT_in = din("cT", [128, KD])
    cst_in = din("cst", [128, 16])
    rcnt_in = din("rcnt", [128, 4 * 16])
    tri_in = din("tri", [128, 128])
    shift_in = din("shift", [128, 64])
    RT = wrows(n_cores)
    RSH = RT // n_cores
    vecs_in = din("vecs", [DEPTH, 128, NV])
    wfull_in = din("wfull", [DEPTH, RT, WCOLS])
    wfull = [dint("wbf%d" % i, [RT, WCOLS], BF16) for i in range(DEPTH)]
    bwsrc = [Buf("wsrc%d" % i) for i in range(DEPTH)]
    bwfull = [Buf("wfull%d" % i) for i in range(DEPTH)]
    curl = [0]

    def Wl(name):
        off, r, c = WOFF[name]
        return wfull[curl[0]].rearrange("r c -> (r c)")[off:off + r * c].rearrange("(r c) -> r c", c=c)

    def gather_weights(i, part=0, nparts=1):
        CH = 256
        chunks = list(range(0, RT, CH))
        per = (len(chunks) + nparts - 1) // nparts
        for r0 in chunks[part * per:(part + 1) * per]:
            dma("pool", wfull[i][r0:r0 + CH, :], wfull_in[i][r0:r0 + CH, :], [], [bwfull[i]])

    out_T = nc.dram_tensor("outT", [D, TPC], F32, kind="ExternalOutput").ap()

    xA = dint("xA", [D, TPC]); bxA = Buf("xA")
    xM = dint("xM", [D, TPC]); bxM = Buf("xM")
    tail_src = dint("tail_src", [D, HALO]); btail_src = Buf("tail_src")
    tail_all = dint("tail_all", [CPS * D, HALO]); btail_all = Buf("tail_all")
    kv_src = dint("kv_src", [NT * 160, T], BF16); bkv_src = [Buf("kv_src%d" % i) for i in range(NT)]
    kv_all = dint("kv_all", [NT * CPS * 160, T], BF16); bkv_all = [Buf("kv_all%d" % i) for i in range(NT)]
    mg_d = dint("mg_d", [D, TPC]); bmg = Buf("mg_d")
    g2_d = dint("g2_d", [D, TPC], BF16); bg2 = Buf("g2_d")
    q_d = dint("q_d", [NH * 96, TPC], BF16); bq = Buf("q_d")
    at_d = dint("at_d", [NH * 64, TPC], BF16); bat = Buf("at_d")
    cc_d = dint("cc_d", [32, TPC]); bcc = Buf("cc_d")
    ss_d = dint("ss_d", [32, TPC]); bss = Buf("ss_d")

    def sb(name, shape, dt=F32):
        return es.enter_context(nc.sbuf_tensor("sb_" + name, list(shape), dt))

    cst = sb("cst", [128, 16]); bcst = Buf("cst")
    rcnt = sb("rcnt", [128, 4, 16])
    tri = sb("tri", [128, 128], BF16)
    shiftm = sb("shiftm", [128, 64])
    epsc = sb("epsc", [128, 4]); bconst = Buf("const")
    ones = sb("ones", [128, 3, 128], BF16)
    ones1k = sb("ones1k", [128, 128], BF16)
    cact = sb("cact", [128, KD], BF16); bcact = Buf("cact")
    mod = sb("mod", [128, 48]); bmod = Buf("mod")
    vecs = sb("vecs", [128, NV]); bvecs = Buf("vecs")
    ps = es.enter_context(nc.psum_tensor("ps", [128, 8, 512], F32))
    bps = [Buf("ps%d" % i) for i in range(8)]
    psi = [0]

    def nps():
        k = psi[0] % 8
        psi[0] += 1
        return k

    AR = Arena(nc, es, 168 * 1024)
    ST = Rot(nc, Arena(nc, es, 8 * 1024, "arena_st"), "st", [128, T], F32, 4)
    rc2 = sb("rc2", [128, 64]); brc2 = Buf("rc2")
    uph = sb("uph", [128, NUP, 2]); buph = Buf("uph")
    tailsb = sb("tailsb", [128, CPS, KD, HALO]); btailsb = Buf("tailsb")
    xhalo = sb("xhalo", [128, KD, HALO]); bxhalo = Buf("xhalo")
    wukv = sb("wukv", [128, 1024], BF16); bwukv = Buf("wukv")
    ALPHA = (2.0 * (alpha_depth or DEPTH)) ** 0.25
    XT = HT = WS = F1 = B1 = F4 = B4 = B8 = F8 = tabs = upb = PW = qh = PT = None
    bufA = bufB = bufP = merged = ffin = kvnT = kT = vaug = None
    bbufA = [Buf("bufA%d" % j) for j in range(4)]
    bbufB = [Buf("bufB%d" % j) for j in range(4)]
    bbufP = [Buf("bufP%d" % j) for j in range(4)]
    bmerged = [Buf("mg%d" % m) for m in range(KD)]
    bffin = [Buf("ffin%d" % i) for i in range(NFF)]
    bkvnT = [Buf("kvnT%d" % r) for r in range(NKC)]
    bkTn = Buf("kTn"); bkTr = Buf("kTr"); bvv = Buf("vaug_v"); bvo = Buf("vaug_o")

    P = lambda a: a

    def dma(q, out, in_, reads, writes):
        S.add(q, lambda e, o=out, i=in_: e.dma_start(out=o, in_=i), reads, writes, dma=True)

    def act(out, in_, func, reads, writes, scale=None, bias=None):
        kw = {}
        if scale is not None:
            kw["scale"] = scale
        if bias is not None:
            kw["bias"] = bias
        S.add("act", lambda e, o=out, i=in_, f=func, kw=kw: e.activation(out=o, in_=i, func=f, **kw), reads, writes)

    def tt(out, in0, in1, op, reads, writes, eng="dve"):
        S.add(eng, lambda e, o=out, a=in0, b=in1, p=op: e.tensor_tensor(out=o, in0=a, in1=b, op=p), reads, writes)

    def ts(out, in0, s1, op0, reads, writes, s2=None, op1=None, eng="dve"):
        if op1 is None:
            S.add(eng, lambda e, o=out, a=in0, s=s1, p=op0: e.tensor_scalar(out=o, in0=a, scalar1=s, scalar2=None, op0=p),
                  reads, writes)
        else:
            S.add(eng, lambda e, o=out, a=in0, s=s1, p=op0, s_2=s2, p1=op1: e.tensor_scalar(
                out=o, in0=a, scalar1=s, scalar2=s_2, op0=p, op1=p1), reads, writes)

    def stt(out, in0, scalar, in1, op0, op1, reads, writes):
        S.add("dve", lambda e, o=out, a=in0, s=scalar, b=in1, p0=op0, p1=op1: e.scalar_tensor_tensor(
            out=o, in0=a, scalar=s, in1=b, op0=p0, op1=p1), reads, writes)

    def mm(out, pairs, reads, writes):
        def fn(e, out=out, pairs=pairs):
            n = len(pairs)
            ins = None
            for i, (l, r) in enumerate(pairs):
                ins = e.matmul(out, lhsT=l, rhs=r, start=(i == 0), stop=(i == n - 1))
            return ins
        S.add("pe", fn, reads, writes)

    def vcol(name, i=0, n=1):
        o = VEC[name] + i
        return vecs[:, o:o + n]

    def wload(src_ap, kparts, ncols, reads=()):
        t, b = WS.next()
        dma("sp", t[:, 0:kparts, 0:ncols], src_ap.rearrange("(k p) c -> p k c", p=128),
            list(reads) + [bwfull[curl[0]]], [b])
        return t, b

    S.add("sp", lambda e: e.dma_start(out=cst[:], in_=cst_in), [], [bcst], dma=True)
    S.add("sp", lambda e: e.dma_start(out=rcnt[:].rearrange("p a b -> p (a b)"), in_=rcnt_in), [], [bconst], dma=True)
    S.add("pool", lambda e: e.dma_start(out=tri[:], in_=tri_in), [], [bconst], dma=True)
    S.add("sp", lambda e: e.dma_start(out=shiftm[:], in_=shift_in), [], [bconst], dma=True)
    for col, v in enumerate([LN_EPS, RMS_EPS, 0.0, 1.0]):
        S.add("dve", lambda e, c=col, v=v: e.memset(epsc[:, c:c + 1], v), [], [bconst])
    for k, v in enumerate([1.0 / 128, 1.0 / 256, 1.0 / 512]):
        S.add("dve", lambda e, k=k, v=v: e.memset(ones[:, k, :], v), [], [bconst])
    S.add("dve", lambda e: e.memset(ones1k[:], 1.0 / 1024), [], [bconst])
    for g_, w_ in enumerate((2, 4, 8, 16)):
        for t_ in range(w_ - 1):
            A_ = 1.0 / (t_ + 1)
            B_ = 1.0 / w_ - A_
            ts(rc2[:, g_ * 16 + t_:g_ * 16 + t_ + 1], cst[:, 2:3], B_, ALU.mult, [bcst], [brc2], s2=A_, op1=ALU.add)
    eps_ln = epsc[:, 0:1]
    eps_rms = epsc[:, 1:2]

    ctmp = sb("ctmp", [128, KD])
    S.add("sp", lambda e: e.dma_start(out=ctmp[:], in_=cT_in), [], [bcact], dma=True)
    act(cact[:], ctmp[:], AF.Silu, [bcact], [bcact])

    R = slice(64, 96)
    TWO_PI = 2.0 * math.pi
    PI_HI = 6.28125
    PI_LO = TWO_PI - PI_HI
    PI_SAFE = 3.14159
    AR.reset()
    posi = AR.alloc([128, T], I32)
    bro = Buf("ropetmp")
    ang = AR.alloc([128, T], F32); kf = AR.alloc([128, T], F32); ki = AR.alloc([128, T], I32)
    rr = AR.alloc([128, T], F32); mk = AR.alloc([128, T], F32); sn = AR.alloc([128, T], F32)
    for n in range(NT):
        cs = slice(n * T, (n + 1) * T)
        dma("sp", posi[R, :], pos_in[64:96, cs], [], [bro])
        S.add("dve", lambda e: e.tensor_copy(out=ang[R, :], in_=posi[R, :]), [bro], [bro])
        ts(ang[R, :], ang[R, :], cst[R, 0:1], ALU.mult, [bro, bcst], [bro])
        ts(kf[R, :], ang[R, :], 1.0 / TWO_PI, ALU.mult, [bro], [bro])
        S.add("dve", lambda e: e.tensor_copy(out=ki[R, :], in_=kf[R, :]), [bro], [bro])
        S.add("dve", lambda e: e.tensor_copy(out=kf[R, :], in_=ki[R, :]), [bro], [bro])
        stt(rr[R, :], kf[R, :], -PI_HI, ang[R, :], ALU.mult, ALU.add, [bro], [bro])
        stt(rr[R, :], kf[R, :], -PI_LO, rr[R, :], ALU.mult, ALU.add, [bro], [bro])

        def wrap(dst, src):
            ts(mk[R, :], src, math.pi, ALU.is_gt, [bro], [bro])
            stt(dst, mk[R, :], -TWO_PI, src, ALU.mult, ALU.add, [bro], [bro])
            ts(mk[R, :], dst, -math.pi, ALU.is_lt, [bro], [bro])
            stt(dst, mk[R, :], TWO_PI, dst, ALU.mult, ALU.add, [bro], [bro])
            ts(dst, dst, -PI_SAFE, ALU.max, [bro], [bro], s2=PI_SAFE, op1=ALU.min)
        wrap(rr[R, :], rr[R, :])
        act(sn[R, :], rr[R, :], AF.Sin, [bro], [bro])
        ts(sn[R, :], sn[R, :], cst[R, 1:2], ALU.mult, [bro, bcst], [bro])
        dma("sp", ss_d[:, cs], sn[R, :], [bro], [bss])
        ts(ang[R, :], rr[R, :], math.pi / 2, ALU.add, [bro], [bro])
        wrap(ang[R, :], ang[R, :])
        act(kf[R, :], ang[R, :], AF.Sin, [bro], [bro])
        dma("sp", cc_d[:, cs], kf[R, :], [bro], [bcc])

    for k in range(KD):
        dma("sp", xA[k * 128:(k + 1) * 128, :], xT_in[k * 128:(k + 1) * 128, :], [], [bxA])

    def exchange_tail(xsrc, bxsrc):
        dma("pool", tail_src, xsrc[:, TPC - HALO:TPC], [bxsrc], [btail_src])
        S.add("pool", lambda e: e.collective_compute("AllGather", ALU.bypass, replica_groups=groups,
                                                     ins=[tail_src], outs=[tail_all]),
              [btail_src], [btail_all])
        dma("pool", tailsb[:].rearrange("p r k h -> p (r k) h"),
            tail_all.rearrange("(rk p) h -> p rk h", p=128), [btail_all], [btailsb])
        xh = xhalo[:].rearrange("p k h -> p (k h)")
        for r in range(CPS):
            src = tailsb[:, r].rearrange("p k h -> p (k h)")
            if r == 0:
                ts(xh, src, cst[:, 6:7], ALU.mult, [btailsb, bcst], [bxhalo])
            else:
                stt(xh, src, cst[:, 6 + r:7 + r], xh, ALU.mult, ALU.add, [btailsb, bcst], [bxhalo])

    def modulate(xt, bxt, N, off):
        ht, bht = HT.next()
        for k in range(KD):
            act(ht[:, k, 0:N], xt[:, k, 0:N], AF.Identity, [bxt, bmod], [bht],
                scale=mod[:, off + 8 + k:off + 9 + k], bias=mod[:, off + k:off + k + 1])
        return ht, bht

    def proj_group(wsrc, c0, ncols, ht, bht, N, kparts=KD):
        wt, bw = wload(wsrc[:, c0:c0 + ncols], kparts, ncols)
        banks = []
        nchunks = (ncols + 127) // 128
        for ci in range(nchunks):
            mcols = min(128, ncols - ci * 128)
            b = nps()
            mm(ps[0:mcols, b, 0:N], [(wt[:, k, ci * 128:ci * 128 + mcols], ht[:, k, 0:N]) for k in range(kparts)],
               [bw, bht], [bps[b]])
            banks.append(b)
        return banks

    def stats(src_bf, bsrc, sq_bf, bsq, nchunk, ones_ap, N):
        bm = nps()
        mm(ps[:, bm, 0:N], [(ones_ap, src_bf[:, j, 0:N]) for j in range(nchunk)], [bsrc, bconst], [bps[bm]])
        be = nps()
        mm(ps[:, be, 0:N], [(ones_ap, sq_bf[:, j, 0:N]) for j in range(nchunk)], [bsq, bconst], [bps[be]])
        return bm, be

    def ln_finish(bm, be, N, eps_ap):
        mean, bmean = ST.next()
        act(mean[:, 0:N], ps[:, bm, 0:N], AF.Copy, [bps[bm]], [bmean])
        var, bvar = ST.next()
        tt(var[:, 0:N], mean[:, 0:N], mean[:, 0:N], ALU.mult, [bmean], [bvar])
        tt(var[:, 0:N], ps[:, be, 0:N], var[:, 0:N], ALU.subtract, [bps[be], bvar], [bvar])
        ts(var[:, 0:N], var[:, 0:N], 0.0, ALU.max, [bvar], [bvar])
        act(var[:, 0:N], var[:, 0:N], AF.Sqrt, [bvar, bconst], [bvar], bias=eps_ap)
        S.add("dve", lambda e, v=var, N=N: e.reciprocal(out=v[:, 0:N], in_=v[:, 0:N]), [bvar], [bvar])
        return mean, bmean, var, bvar

    def layer_norm_full(xin, bxin, gname, bname, N, dst_ap_fn, bdst):
        xb, bxb = B8.next()
        sq, bsq = B8.next()
        for k in range(KD):
            act(xb[:, k, 0:N], xin[:, k, 0:N], AF.Copy, [bxin], [bxb])
            act(sq[:, k, 0:N], xin[:, k, 0:N], AF.Square, [bxin], [bsq])
        bm, be = stats(xb, bxb, sq, bsq, KD, ones1k[:], N)
        mean, bmean, rstd, brstd = ln_finish(bm, be, N, eps_ln)
        for k in range(KD):
            d, bd = F1.next()
            tt(d[:, 0:N], xin[:, k, 0:N], mean[:, 0:N], ALU.subtract, [bxin, bmean], [bd])
            tt(d[:, 0:N], d[:, 0:N], rstd[:, 0:N], ALU.mult, [bd, brstd], [bd])
            act(dst_ap_fn(k), d[:, 0:N], AF.Identity, [bd, bvecs], [bdst], scale=vcol(gname, k), bias=vcol(bname, k))

    for l in range(DEPTH):
        S.barrier()
        AR.reset()
        XT = Rot(nc, AR, "xt", [128, KD, T], F32, 1)
        HT = Rot(nc, AR, "ht", [128, KD, T], BF16, 1)
        WS = Rot(nc, AR, "ws", [128, KD, 512], BF16, 3)
        F1 = Rot(nc, AR, "f1", [128, T], F32, 6)
        B1 = Rot(nc, AR, "b1", [128, T], BF16, 3)
        F4 = Rot(nc, AR, "f4", [128, 4, T], F32, 2)
        B4 = Rot(nc, AR, "b4", [128, 4, T], BF16, 5)
        B8 = Rot(nc, AR, "b8", [128, KD, T], BF16, 1)
        tabs = Rot(nc, AR, "tabs", [128, 2, T], F32, 1)
        PW = Rot(nc, AR, "pw", [128, 16 + T], F32, 3)
        bufA = AR.alloc([128, 4, 30 + T], F32)
        bufB = AR.alloc([128, 4, 2 + T], F32)
        bufP = AR.alloc([128, 4, 16 + T], F32)
        merged = AR.alloc([128, KD, T], F32)
        curl[0] = l
        if l == 0:
            gather_weights(0)
        dma("sp", vecs[:], vecs_in[l], [], [bvecs])
        for g6 in range(6):
            for half in range(2):
                wt, bw = wload(Wl("w_ada")[:, g6 * D + half * 512: g6 * D + half * 512 + 512], KD, 512)
                for ci in range(4):
                    mcol = g6 * 8 + half * 4 + ci
                    b = nps()
                    mm(ps[:, b, 0:1], [(wt[:, k, ci * 128:(ci + 1) * 128], cact[:, k:k + 1]) for k in range(KD)],
                       [bw, bcact], [bps[b]])
                    act(mod[:, mcol:mcol + 1], ps[:, b, 0:1], AF.Identity, [bps[b], bvecs], [bmod],
                        bias=vcol("b_ada", mcol))
        for g6 in (1, 2, 4, 5):
            ts(mod[:, g6 * 8:(g6 + 1) * 8], mod[:, g6 * 8:(g6 + 1) * 8], 1.0, ALU.add, [bmod], [bmod])

        exchange_tail(xA, bxA)

        wi = Wl("w_in")

        def phase1_tile(n):
            halo = n < 0
            N = HALO if halo else T
            if halo:
                xt, bxt = xhalo, bxhalo
            else:
                xt, bxt = XT.next()
                dma("pool", xt[:], xA[:, n * T:(n + 1) * T].rearrange("(k p) t -> p k t", p=128), [bxA], [bxt])
            ht, bht = modulate(xt, bxt, N, 0)

            bA = proj_group(wi, 0, 512, ht, bht, N)
            bB = proj_group(wi, 512, 512, ht, bht, N)
            for j in range(4):
                sg, bsg = F1.next()
                act(sg[:, 0:N], ps[:, bB[j], 0:N], AF.Sigmoid, [bps[bB[j]], bvecs], [bsg], bias=vcol("b_in", 4 + j))
                if halo:
                    tmp, btmp = F1.next()
                    stt(tmp[:, 0:N], ps[:, bA[j], 0:N], vcol("b_in", j), sg[:, 0:N], ALU.add, ALU.mult,
                        [bps[bA[j]], bsg, bvecs], [btmp])
                    ts(bufA[:, j, 0:30], tmp[:, 2:32], cst[:, 2:3], ALU.mult, [btmp, bcst], [bbufA[j]])
                else:
                    stt(bufA[:, j, 30:30 + T], ps[:, bA[j], 0:T], vcol("b_in", j), sg[:, 0:T], ALU.add, ALU.mult,
                        [bps[bA[j]], bsg, bvecs], [bbufA[j]])
            bC = proj_group(wi, 12 * 128, 512, ht, bht, N)
            bX = proj_group(wi, 16 * 128, 512, ht, bht, N)
            for j in range(4):
                cs_, bcs = F1.next()
                act(cs_[:, 0:N], ps[:, bC[j], 0:N], AF.Identity, [bps[bC[j]], bvecs], [bcs], bias=vcol("b_in", 12 + j))
                if halo:
                    tmp, btmp = F1.next()
                    stt(tmp[:, 0:N], ps[:, bX[j], 0:N], vcol("b_in", 16 + j), cs_[:, 0:N], ALU.add, ALU.mult,
                        [bps[bX[j]], bcs, bvecs], [btmp])
                    ts(bufB[:, j, 0:2], tmp[:, 30:32], cst[:, 2:3], ALU.mult, [btmp, bcst], [bbufB[j]])
                else:
                    stt(bufB[:, j, 2:2 + T], ps[:, bX[j], 0:T], vcol("b_in", 16 + j), cs_[:, 0:T], ALU.add, ALU.mult,
                        [bps[bX[j]], bcs, bvecs], [bbufB[j]])
            bP = proj_group(wi, 25 * 128, 512, ht, bht, N)
            for j in range(4):
                if halo:
                    tmp, btmp = F1.next()
                    act(tmp[:, 0:N], ps[:, bP[j], 0:N], AF.Identity, [bps[bP[j]], bvecs], [btmp], bias=vcol("b_in", 25 + j))
                    ts(bufP[:, j, 0:16], tmp[:, 16:32], cst[:, 2:3], ALU.mult, [btmp, bcst], [bbufP[j]])
                else:
                    act(bufP[:, j, 16:16 + T], ps[:, bP[j], 0:T], AF.Identity, [bps[bP[j]], bvecs], [bbufP[j]],
                        bias=vcol("b_in", 25 + j))
            if halo:
                return

            g2t, bg2t = B8.next()
            for half in range(2):
                bg_ = proj_group(wi, (29 + 2 * 8 + half * 4) * 128, 512, ht, bht, T)
                for ci in range(4):
                    m = half * 4 + ci
                    act(g2t[:, m, :], ps[:, bg_[ci], :], AF.Sigmoid, [bps[bg_[ci]], bvecs], [bg2t],
                        bias=vcol("b_in", 29 + 16 + m))
            dma("pool", g2_d[:, n * T:(n + 1) * T].rearrange("(k p) t -> p k t", p=128), g2t[:], [bg2t], [bg2])

            tb, btb = tabs.next()
            dma("pool", tb[64:96, 0, :], cc_d[:, n * T:(n + 1) * T], [bcc], [btb])
            dma("pool", tb[64:96, 1, :], ss_d[:, n * T:(n + 1) * T], [bss], [btb])
            bQ = proj_group(wi, 20 * 128, 512, ht, bht, T)
            bK2 = proj_group(wi, 24 * 128, 128, ht, bht, T)
            ql, bql = F4.next()
            qsq, bqsq = B4.next()
            for j in range(2):
                act(ql[:, j, :], ps[:, bQ[j], :], AF.Identity, [bps[bQ[j]], bvecs], [bql], bias=vcol("b_in", 20 + j))
                act(qsq[:, j, :], ql[:, j, :], AF.Square, [bql], [bqsq])
            act(ql[:, 2, :], ps[:, bQ[2], :], AF.Identity, [bps[bQ[2]], bvecs], [bql], bias=vcol("b_in", 22))
            act(qsq[:, 2, :], ql[:, 2, :], AF.Square, [bql], [bqsq])
            bqs = nps()
            mm(ps[:, bqs, :], [(ones[:, 1, :], qsq[:, j, :]) for j in range(2)], [bqsq, bconst], [bps[bqs]])
            bks = nps()
            mm(ps[:, bks, :], [(ones[:, 0, :], qsq[:, 2, :])], [bqsq, bconst], [bps[bks]])
            qn, bqn = B4.next()
            for (bb, js, gname) in ((bqs, (0, 1), "qn_g"), (bks, (2,), "kvn_g")):
                rs, brs = F1.next()
                act(rs[:], ps[:, bb, :], AF.Sqrt, [bps[bb], bconst], [brs], bias=eps_rms)
                S.add("dve", lambda e, v=rs: e.reciprocal(out=v[:], in_=v[:]), [brs], [brs])
                for j in js:
                    stt(qn[:, j, :], ql[:, j, :], vcol(gname, j if gname == "qn_g" else 0), rs[:], ALU.mult, ALU.mult,
                        [bql, brs, bvecs], [bqn])
            dma("pool", kv_src[n * 160:n * 160 + 128, :], qn[:, 2, :], [bqn], [bkv_src[n]])
            kr, bkr = F1.next()
            kr2, bkr2 = F1.next()
            stt(kr[R, :], ps[R, bQ[3], :], vecs[R, VEC["b_in"] + 23:VEC["b_in"] + 24], tb[R, 0, :], ALU.add, ALU.mult,
                [bps[bQ[3]], btb, bvecs], [bkr])
            stt(kr2[R, :], ps[R, bK2[0], :], vecs[R, VEC["b_in"] + 24:VEC["b_in"] + 25], tb[R, 1, :], ALU.add, ALU.mult,
                [bps[bK2[0]], btb, bvecs], [bkr2])
            krb, bkrb = B1.next()
            tt(krb[R, :], kr[R, :], kr2[R, :], ALU.add, [bkr, bkr2], [bkrb])
            dma("pool", kv_src[n * 160 + 128:(n + 1) * 160, :], krb[R, :], [bkrb], [bkv_src[n]])
            S.add("pool", lambda e, n=n: e.collective_compute(
                "AllGather", ALU.bypass, replica_groups=groups, ins=[kv_src[n * 160:(n + 1) * 160, :]],
                outs=[kv_all[n * CPS * 160:(n + 1) * CPS * 160, :]]), [bkv_src[n]], [bkv_all[n]])
            wqs = []
            for s4 in range(4):
                wq_, bwq_ = WS.next()
                dma("sp", wq_[:, 0:2, 0:512], Wl("w_uq")[:, s4 * 512:(s4 + 1) * 512].rearrange(
                    "(k p) c -> p k c", p=128), [bwfull[l]], [bwq_])
                outs4 = []
                for h4 in range(4):
                    b = nps()
                    mm(ps[0:96, b, :], [(wq_[:, k, h4 * 128:h4 * 128 + 96], qn[:, k, :]) for k in range(2)],
                       [bwq_, bqn], [bps[b]])
                    outs4.append(b)
                wqs.append(outs4)
                if s4 % 2 == 0:
                    continue
                for h4 in range(4):
                    h = (s4 // 2) * 4 + h4
                    outs = [wqs[s4 - 1][h4], wqs[s4][h4]]
                    qo, bqo = B1.next()
                    act(qo[0:64, :], ps[0:64, outs[0], :], AF.Copy, [bps[outs[0]]], [bqo])
                    t1, bt1 = F1.next()
                    t2, bt2 = F1.next()
                    tt(t1[R, :], ps[R, outs[0], :], tb[R, 0, :], ALU.mult, [bps[outs[0]], btb], [bt1])
                    tt(t2[R, :], ps[R, outs[1], :], tb[R, 1, :], ALU.mult, [bps[outs[1]], btb], [bt2])
                    tt(qo[R, :], t1[R, :], t2[R, :], ALU.add, [bt1, bt2], [bqo])
                    dma("pool", q_d[h * 96:(h + 1) * 96, n * T:(n + 1) * T], qo[0:96, :], [bqo], [bq])

            accA, baccA = F4.next()
            for j in range(4):
                for k in range(31):
                    wk = vcol("conv_dw", j * 31 + k)
                    if k == 0:
                        ts(accA[:, j, :], bufA[:, j, 0:T], wk, ALU.mult, [bbufA[j], bvecs], [baccA])
                    else:
                        stt(accA[:, j, :], bufA[:, j, k:k + T], wk, accA[:, j, :], ALU.mult, ALU.add,
                            [bbufA[j], bvecs], [baccA])
                ts(bufA[:, j, 0:30], bufA[:, j, T:T + 30], 1.0, ALU.mult, [bbufA[j]], [bbufA[j]])
            ab, bab = B4.next()
            sq, bsq = B4.next()
            for j in range(4):
                act(ab[:, j, :], accA[:, j, :], AF.Copy, [baccA], [bab])
                act(sq[:, j, :], accA[:, j, :], AF.Square, [baccA], [bsq])
            bm, be = stats(ab, bab, sq, bsq, 4, ones[:, 2, :], T)
            mean, bmean, rstd, brstd = ln_finish(bm, be, T, eps_ln)
            yA, byA = B4.next()
            for j in range(4):
                d, bd = F1.next()
                tt(d[:], accA[:, j, :], mean[:], ALU.subtract, [baccA, bmean], [bd])
                tt(d[:], d[:], rstd[:], ALU.mult, [bd, brstd], [bd])
                act(yA[:, j, :], d[:], AF.Silu, [bd, bvecs], [byA], scale=vcol("cln_g", j), bias=vcol("cln_b", j))

            bG = proj_group(wi, 8 * 128, 512, ht, bht, T)
            yB, byB = B4.next()
            for j in range(4):
                acc, bacc = F1.next()
                for k in range(3):
                    wk = vcol("sc_dw", j * 3 + k)
                    if k == 0:
                        ts(acc[:], bufB[:, j, 0:T], wk, ALU.mult, [bbufB[j], bvecs], [bacc])
                    else:
                        stt(acc[:], bufB[:, j, k:k + T], wk, acc[:], ALU.mult, ALU.add, [bbufB[j], bvecs], [bacc])
                ts(bufB[:, j, 0:2], bufB[:, j, T:T + 2], 1.0, ALU.mult, [bbufB[j]], [bbufB[j]])
                stt(yB[:, j, :], ps[:, bG[j], :], vcol("b_in", 8 + j), acc[:], ALU.add, ALU.mult,
                    [bps[bG[j]], bacc, bvecs], [byB])

            pd, bpd = B4.next()
            for g in range(4):
                w = 2 << g
                cur = bufP[:, g, :]
                bcur = bbufP[g]
                span = 16 + T
                sh = 1
                srcs = (cur, bcur)
                while sh < w:
                    tmpP, btmpP = PW.next()
                    tt(tmpP[:, sh:span], srcs[0][:, sh:span], srcs[0][:, 0:span - sh], ALU.add, [srcs[1]], [btmpP])
                    srcs = (tmpP, btmpP)
                    sh *= 2
                o, bo = F1.next()
                stt(o[:], srcs[0][:, 16:16 + T], 1.0 / w, bufP[:, g, 16:16 + T], ALU.mult, ALU.subtract,
                    [srcs[1], bbufP[g]], [bo])
                if n == 0:
                    for t_ in range(w - 1):
                        stt(o[:, t_:t_ + 1], srcs[0][:, 16 + t_:17 + t_], rc2[:, g * 16 + t_:g * 16 + t_ + 1],
                            bufP[:, g, 16 + t_:17 + t_], ALU.mult, ALU.subtract, [srcs[1], bbufP[g], brc2], [bo])
                act(pd[:, g, :], o[:], AF.Copy, [bo], [bpd])
                ts(bufP[:, g, 0:16], bufP[:, g, T:T + 16], 1.0, ALU.mult, [bbufP[g]], [bbufP[g]])
            wpt, bwp = WS.next()
            dma("sp", wpt[:, 0:4, 0:128], Wl("w_pool").rearrange("(g c) d -> c g d", c=128), [bwfull[l]], [bwp])
            yD, byD = B4.next()
            for g in range(4):
                b = nps()
                mm(ps[:, b, :], [(wpt[:, g, 0:128], pd[:, g, :])], [bwp, bpd], [bps[b]])
                act(yD[:, g, :], ps[:, b, :], AF.Copy, [bps[b], bvecs], [byD], scale=vcol("pool_scale", g))

            for bi, (wname, ysrc, bysrc) in enumerate([("w_conv_out", yA, byA), ("w_sc_out", yB, byB),
                                                      ("w_pool_out", yD, byD)]):
                gate_b = (0, 1, 3)[bi]
                for half in range(2):
                    bo_ = proj_group(Wl(wname), half * 512, 512, ysrc, bysrc, T, kparts=4)
                    bg_ = proj_group(wi, (29 + gate_b * 8 + half * 4) * 128, 512, ht, bht, T)
                    for ci in range(4):
                        m = half * 4 + ci
                        gt, bgt = F1.next()
                        act(gt[:], ps[:, bg_[ci], :], AF.Sigmoid, [bps[bg_[ci]], bvecs], [bgt],
                            bias=vcol("b_in", 29 + gate_b * 8 + m))
                        if bi == 0:
                            tt(merged[:, m, :], gt[:], ps[:, bo_[ci], :], ALU.mult, [bgt, bps[bo_[ci]]], [bmerged[m]])
                        else:
                            tt(gt[:], gt[:], ps[:, bo_[ci], :], ALU.mult, [bgt, bps[bo_[ci]]], [bgt])
                            tt(merged[:, m, :], merged[:, m, :], gt[:], ALU.add, [bgt, bmerged[m]], [bmerged[m]],
                               eng="pool")
            dma("pool", mg_d[:, n * T:(n + 1) * T].rearrange("(k p) t -> p k t", p=128), merged[:], bmerged, [bmg])
        phase1_tile(-1)
        for n in range(NT):
            phase1_tile(n)
            if l + 1 < DEPTH:
                gather_weights(l + 1, n, NT)

        S.barrier()
        AR.reset()
        kvnT = AR.alloc([128, NK], BF16)
        kT = AR.alloc([128, NK], BF16)
        vaug = AR.alloc([128, NKB, 128], BF16)
        qh = Rot(nc, AR, "qh", [128, TPC], BF16, 2)
        PT = Rot(nc, AR, "pt", [128, 2, T], BF16, 3)
        F1 = Rot(nc, AR, "f1", [128, T], F32, 6)
        B1 = Rot(nc, AR, "b1", [128, T], BF16, 3)
        for r in range(NKC):
            kb0 = r * (TPC // 128)
            kb1 = (r + 1) * (TPC // 128)
            S.add("pool", lambda e, kb0=kb0, kb1=kb1: e.memset(vaug[:, kb0:kb1, 64:128], 1.0), [], [bvo])
            if r < NKC - 1:
                S.add("pool", lambda e, kb0=kb0, kb1=kb1, r=r: e.tensor_scalar(
                    out=vaug[:, kb0:kb1, 64:128], in0=vaug[:, kb0:kb1, 64:128], scalar1=cst[:, 3 + r:4 + r],
                    scalar2=None, op0=ALU.mult), [bcst, bvo], [bvo])
        for r in range(NKC):
            for n in range(NT):
                cs = slice(r * TPC + n * T, r * TPC + (n + 1) * T)
                if r < NKC - 1:
                    r0 = (n * CPS + r) * 160
                    dma("pool", kvnT[:, cs], kv_all[r0:r0 + 128, :], [bkv_all[n]], [bkvnT[r]])
                    dma("pool", kT[64:96, cs], kv_all[r0 + 128:r0 + 160, :], [bkv_all[n]], [bkTr])
                else:
                    dma("pool", kvnT[:, cs], kv_src[n * 160:n * 160 + 128, :], [bkv_src[n]], [bkvnT[r]])
                    dma("pool", kT[64:96, cs], kv_src[n * 160 + 128:(n + 1) * 160, :], [bkv_src[n]], [bkTr])
        dma("sp", wukv[:], Wl("w_ukv"), [bwfull[l]], [bwukv])
        OB = (4, 5)
        oi = 0
        for h in range(NH):
            for kb5 in range(NK // 512):
                b = 6 + (kb5 % 2)
                mm(ps[0:64, b, :], [(wukv[:, h * 128:h * 128 + 64], kvnT[:, kb5 * 512:(kb5 + 1) * 512])],
                   [bwukv] + bkvnT, [bps[b]])
                act(kT[0:64, kb5 * 512:(kb5 + 1) * 512], ps[0:64, b, :], AF.Copy, [bps[b]], [bkTn])
            for kb8 in range(NKB // 8):
                b = 6 + (kb8 % 2)
                for i8 in range(8):
                    kb = kb8 * 8 + i8
                    mm(ps[:, b, i8 * 64:(i8 + 1) * 64], [(kvnT[:, kb * 128:(kb + 1) * 128],
                                                          wukv[:, h * 128 + 64:(h + 1) * 128])],
                       [bwukv] + bkvnT, [bps[b]])
                r = (kb8 * 8 * 128) // TPC
                src = ps[:, b, :].rearrange("p (a d) -> p a d", d=64)
                if r < NKC - 1:
                    act(vaug[:, kb8 * 8:(kb8 + 1) * 8, 0:64], src, AF.Copy, [bps[b], bcst], [bvv],
                        scale=cst[:, 3 + r:4 + r])
                else:
                    act(vaug[:, kb8 * 8:(kb8 + 1) * 8, 0:64], src, AF.Copy, [bps[b]], [bvv])
            qt, bqt = qh.next()
            dma("pool", qt[0:96, :], q_d[h * 96:(h + 1) * 96, :], [bq], [bqt])
            for n in range(NT):
                kbl = [(kb, 0, False) for kb in range((NKC - 1) * (TPC // 128))]
                own0 = (NKC - 1) * (TPC // 128)
                for kbo in range(4 * n + 4):
                    i = kbo - 4 * n
                    if i < 0:
                        kbl.append((own0 + kbo, 0, False))
                    else:
                        kbl.append((own0 + kbo, i * 128, True))
                ob = OB[oi % 2]
                oi += 1
                nblk = len(kbl)
                pairs = [kbl[i:i + 2] for i in range(0, nblk, 2)]
                npairs = len(pairs)

                def emit_S(pi):
                    sb0 = (pi % 2) * 2
                    for e_, (kb, c0, dg) in enumerate(pairs[pi]):
                        mm(ps[:, sb0 + e_, c0:T], [(kT[0:96, kb * 128:(kb + 1) * 128], qt[0:96, n * T + c0:(n + 1) * T])],
                           [bkTn, bkTr, bqt], [bps[sb0 + e_]])

                emit_S(0)
                for pi in range(npairs):
                    pair = pairs[pi]
                    sb0 = (pi % 2) * 2
                    if pi + 1 < npairs:
                        emit_S(pi + 1)
                    pt, bpt = PT.next()
                    if len(pair) == 2 and pair[0][1] == 0 and pair[1][1] == 0:
                        act(pt[:, 0:2, :], ps[:, sb0:sb0 + 2, :], AF.Exp, [bps[sb0], bps[sb0 + 1]], [bpt], scale=SCALE)
                    else:
                        for e_, (kb, c0, dg) in enumerate(pair):
                            act(pt[:, e_, c0:T], ps[:, sb0 + e_, c0:T], AF.Exp, [bps[sb0 + e_]], [bpt], scale=SCALE)
                    for e_, (kb, c0, dg) in enumerate(pair):
                        if dg:
                            tt(pt[:, e_, c0:c0 + 128], pt[:, e_, c0:c0 + 128], tri[:], ALU.mult, [bpt, bconst], [bpt],
                               eng="pool")
                    for e_, (kb, c0, dg) in enumerate(pair):
                        gi = pi * 2 + e_
                        S.add("pe", lambda e, ob=ob, kb=kb, c0=c0, pt=pt, e_=e_, st=(gi == 0), sp_=(gi == nblk - 1):
                              e.matmul(ps[:, ob, c0:T], lhsT=vaug[:, kb, :], rhs=pt[:, e_, c0:T], start=st, stop=sp_),
                              [bvv, bvo, bpt], [bps[ob]])
                osb, bosb = F1.next()
                act(osb[:], ps[:, ob, :], AF.Copy, [bps[ob]], [bosb])
                bd_ = 6 + (oi % 2)
                mm(ps[0:64, bd_, :], [(shiftm[:], osb[:])], [bosb, bconst], [bps[bd_]])
                rd, brd = F1.next()
                S.add("dve", lambda e, rd=rd, bd_=bd_: e.reciprocal(out=rd[0:64, :], in_=ps[0:64, bd_, :]),
                      [bps[bd_]], [brd])
                ao, bao = B1.next()
                tt(ao[0:64, :], osb[0:64, :], rd[0:64, :], ALU.mult, [bosb, brd], [bao])
                dma("pool", at_d[h * 64:(h + 1) * 64, n * T:(n + 1) * T], ao[0:64, :], [bao], [bat])

        S.barrier()
        AR.reset()
        XT = Rot(nc, AR, "xt", [128, KD, T], F32, 2)
        F8 = Rot(nc, AR, "f8", [128, KD, T], F32, 2)
        F1 = Rot(nc, AR, "f1", [128, T], F32, 6)
        B4 = Rot(nc, AR, "b4", [128, 4, T], BF16, 2)
        B8 = Rot(nc, AR, "b8", [128, KD, T], BF16, 4)
        WS = Rot(nc, AR, "ws", [128, KD, 512], BF16, 3)
        for n in range(NT):
            cs = slice(n * T, (n + 1) * T)
            at, bat_t = B4.next()
            dma("pool", at[:], at_d[:, cs].rearrange("(k p) t -> p k t", p=128), [bat], [bat_t])
            mgt, bmgt = F8.next()
            dma("pool", mgt[:], mg_d[:, cs].rearrange("(k p) t -> p k t", p=128), [bmg], [bmgt])
            g2t, bg2t = B8.next()
            dma("pool", g2t[:], g2_d[:, cs].rearrange("(k p) t -> p k t", p=128), [bg2], [bg2t])
            xt, bxt = XT.next()
            dma("pool", xt[:], xA[:, cs].rearrange("(k p) t -> p k t", p=128), [bxA], [bxt])
            mb, bmb = B8.next()
            for half in range(2):
                bo_ = proj_group(Wl("w_mla_out"), half * 512, 512, at, bat_t, T, kparts=4)
                for ci in range(4):
                    m = half * 4 + ci
                    tmp, btmp = F1.next()
                    tt(tmp[:], g2t[:, m, :], ps[:, bo_[ci], :], ALU.mult, [bg2t, bps[bo_[ci]]], [btmp])
                    tt(mb[:, m, :], tmp[:], mgt[:, m, :], ALU.add, [btmp, bmgt], [bmb], eng="pool")
            x1, bx1 = F8.next()
            for half in range(2):
                bo_ = proj_group(Wl("w_o"), half * 512, 512, mb, bmb, T)
                for ci in range(4):
                    m = half * 4 + ci
                    tmp, btmp = F1.next()
                    act(tmp[:], ps[:, bo_[ci], :], AF.Copy, [bps[bo_[ci]], bmod], [btmp], scale=mod[:, 16 + m:17 + m])
                    stt(x1[:, m, :], xt[:, m, :], ALPHA, tmp[:], ALU.mult, ALU.add, [bxt, btmp], [bx1])
            xo, bxo = XT.next()
            layer_norm_full(x1, bx1, "ln1_g", "ln1_b", T, lambda k, xo=xo: xo[:, k, :], bxo)
            dma("pool", xM[:, cs].rearrange("(k p) t -> p k t", p=128), xo[:], [bxo], [bxM])

        exchange_tail(xM, bxM)

        def ffn_tile(n):
            halo = n < 0
            N = HALO if halo else T
            if halo:
                xt, bxt = xhalo, bxhalo
            else:
                xt, bxt = XT.next()
                dma("pool", xt[:], xM[:, n * T:(n + 1) * T].rearrange("(k p) t -> p k t", p=128), [bxM], [bxt])
            ht, bht = modulate(xt, bxt, N, 24)
            vals = [None, None]
            for g in range(NUP // 4):
                banks = proj_group(Wl("w_up"), g * 512, 512, ht, bht, N)
                for ci in range(4):
                    c = g * 4 + ci
                    if halo:
                        ts(uph[:, c, :], ps[:, banks[ci], 30:32], cst[:, 2:3], ALU.mult, [bps[banks[ci]], bcst], [buph])
                        continue
                    ub, bub = upb.next()
                    act(ub[:, 2:2 + T], ps[:, banks[ci], :], AF.Copy, [bps[banks[ci]]], [bub])
                    ts(ub[:, 0:2], uph[:, c, :], 1.0, ALU.mult, [buph], [bub], eng="pool")
                    ts(uph[:, c, :], ub[:, T:T + 2], 1.0, ALU.mult, [bub], [buph], eng="pool")
                    acc, bacc = F1.next()
                    for k in range(3):
                        wk = vcol("ffn_dw", c * 3 + k)
                        if k == 0:
                            ts(acc[:], ub[:, 0:T], wk, ALU.mult, [bub, bvecs], [bacc])
                        else:
                            stt(acc[:], ub[:, k:k + T], wk, acc[:], ALU.mult, ALU.add, [bub, bvecs], [bacc])
                    if ci < 2:
                        vals[ci] = (acc, bacc)
                    else:
                        i = g * 2 + ci - 2
                        sg, bsg = F1.next()
                        act(sg[:], acc[:], AF.Silu, [bacc], [bsg])
                        tt(ffin[:, i, :], sg[:], vals[ci - 2][0][:], ALU.mult, [bsg, vals[ci - 2][1]], [bffin[i]])
            if halo:
                return
            x2, bx2 = F8.next()
            for half in range(2):
                pb = []
                for ci in range(4):
                    pb.append(nps())
                kgs = [(0, 8), (8, 8), (16, 6)]
                wts = []
                for (k0, kn) in kgs:
                    wt, bw = wload(Wl("w_down")[k0 * 128:(k0 + kn) * 128, half * 512:(half + 1) * 512], kn, 512)
                    wts.append((wt, bw, k0, kn))
                for ci in range(4):
                    pairs = []
                    rd_ = []
                    for (wt, bw, k0, kn) in wts:
                        for k in range(kn):
                            pairs.append((wt[:, k, ci * 128:(ci + 1) * 128], ffin[:, k0 + k, :]))
                        rd_.append(bw)
                    mm(ps[:, pb[ci], :], pairs, rd_ + bffin, [bps[pb[ci]]])
                    m = half * 4 + ci
                    tmp, btmp = F1.next()
                    act(tmp[:], ps[:, pb[ci], :], AF.Copy, [bps[pb[ci]], bmod], [btmp], scale=mod[:, 40 + m:41 + m])
                    stt(x2[:, m, :], xt[:, m, :], ALPHA, tmp[:], ALU.mult, ALU.add, [bxt, btmp], [bx2])
            xo, bxo = XT.next()
            layer_norm_full(x2, bx2, "ln2_g", "ln2_b", T, lambda k, xo=xo: xo[:, k, :], bxo)
            if l == DEPTH - 1:
                dma("pool", out_T[:, n * T:(n + 1) * T].rearrange("(k p) t -> p k t", p=128), xo[:], [bxo], [Buf("o")])
            else:
                dma("pool", xA[:, n * T:(n + 1) * T].rearrange("(k p) t -> p k t", p=128), xo[:], [bxo], [bxA])

        S.barrier()
        AR.reset()
        XT = Rot(nc, AR, "xt", [128, KD, T], F32, 2)
        HT = Rot(nc, AR, "ht", [128, KD, T], BF16, 1)
        WS = Rot(nc, AR, "ws", [128, KD, 512], BF16, 4)
        upb = Rot(nc, AR, "upb", [128, 2 + T], F32, 4)
        F1 = Rot(nc, AR, "f1", [128, T], F32, 8)
        F8 = Rot(nc, AR, "f8", [128, KD, T], F32, 1)
        B8 = Rot(nc, AR, "b8", [128, KD, T], BF16, 2)
        ffin = AR.alloc([128, NFF, T], BF16)
        ffn_tile(-1)
        for n in range(NT):
            ffn_tile(n)

    S.emit(nc, es)
    es.close()
    return nc


DEPTH_FULL = 4
WCOLS = 2048
WLAY = [("w_ada", D, 6 * D), ("w_in", D, NCH_IN * 128), ("w_conv_out", 512, D), ("w_sc_out", 512, D),
        ("w_uq", 256, 2048), ("w_ukv", 128, 1024), ("w_mla_out", 512, D), ("w_pool", 512, 128),
        ("w_pool_out", 512, D), ("w_o", D, D), ("w_up", D, 2 * DFF), ("w_down", DFF, D)]
WOFF = {}
_o = 0
for _n, _r, _c in WLAY:
    WOFF[_n] = (_o, _r, _c)
    _o += _r * _c
WTOT = _o


WCH = 256


def wrows(n_cores):
    per = n_cores * WCH * WCOLS
    return (WTOT + per - 1) // per * n_cores * WCH
DEBUG = False
DEBUG_NAMES = ("xM", "mg_d", "g2_d", "q_d", "at_d", "cc_d", "ss_d", "kv_dbg")
LAST = {}


def _pt(v, nchunk):
    return np.ascontiguousarray(np.asarray(v, np.float32).reshape(nchunk, 128).T)


def prep_weights(inp, DEPTH, n_cores):
    f = lambda a: np.ascontiguousarray(np.asarray(a, dtype=np.float32))
    w_in = f(inp["w_in"]); b_in = f(inp["b_in"])
    idx = np.zeros(NCH_IN * 128, np.int64)
    valid = np.zeros(NCH_IN * 128, bool)

    def put(chunk, cols, off=0):
        idx[chunk * 128 + off: chunk * 128 + off + len(cols)] = cols
        valid[chunk * 128 + off: chunk * 128 + off + len(cols)] = True
    put(0, np.arange(0, 512)); put(4, np.arange(512, 1024)); put(8, np.arange(1024, 1536))
    put(12, np.arange(1536, 2048)); put(16, np.arange(2048, 2560)); put(20, np.arange(2560, 2816))
    put(22, np.arange(2816, 2944))
    kr = np.arange(2944, 2976)
    put(23, kr, 64)
    put(24, np.concatenate([kr[16:], kr[:16]]), 64)
    put(25, np.arange(2976, 3488))
    put(29, np.arange(3488, 7584))
    w_inP = np.where(valid[None, None, :], w_in[:, :, idx], 0.0).astype(np.float32)
    b_inP = np.where(valid[None, :], b_in[:, idx], 0.0).astype(np.float32)
    w_uq = f(inp["w_uq"])
    hid = np.arange(768).reshape(NH, 96)
    sw = np.concatenate([hid[:, :64], hid[:, 80:96], hid[:, 64:80]], axis=1).reshape(-1)
    w_uq_sw = w_uq[:, :, sw]
    w_uqP = np.zeros((w_uq.shape[0], 256, 2048), np.float32)
    for h in range(NH):
        for v, src_ in enumerate((w_uq, w_uq_sw)):
            s4 = (h // 4) * 2 + v
            c0 = s4 * 512 + (h % 4) * 128
            w_uqP[:, :, c0:c0 + 96] = src_[:, :, h * 96:(h + 1) * 96]
    upidx = []
    for g in range(NUP // 4):
        for ci in range(4):
            ch = (2 * g + ci) if ci < 2 else (NFF + 2 * g + ci - 2)
            upidx.append(np.arange(ch * 128, (ch + 1) * 128))
    upidx = np.concatenate(upidx)
    w_upP = f(inp["w_up"])[:, :, upidx]
    ffn_dwP = f(inp["ffn_dw"])[:, :, upidx]
    vecs = np.zeros((DEPTH, 128, NV), np.float32)

    def setv(name, l, arr):
        vecs[l, :, VEC[name]:VEC[name] + arr.shape[1]] = arr
    for l in range(DEPTH):
        setv("b_ada", l, _pt(inp["b_ada"][l], 48))
        setv("b_in", l, _pt(b_inP[l], NCH_IN))
        setv("conv_dw", l, f(inp["conv_dw"][l]).T.reshape(4, 128, 31).transpose(1, 0, 2).reshape(128, 124))
        setv("cln_g", l, _pt(inp["conv_ln_g"][l], 4)); setv("cln_b", l, _pt(inp["conv_ln_b"][l], 4))
        setv("sc_dw", l, f(inp["sc_dw"][l]).T.reshape(4, 128, 3).transpose(1, 0, 2).reshape(128, 12))
        setv("qn_g", l, _pt(inp["q_norm_g"][l], 2)); setv("kvn_g", l, _pt(inp["kv_norm_g"][l], 1))
        setv("pool_scale", l, _pt(inp["pool_scale"][l], 4))
        setv("ln1_g", l, _pt(inp["ln1_g"][l], 8)); setv("ln1_b", l, _pt(inp["ln1_b"][l], 8))
        setv("ln2_g", l, _pt(inp["ln2_g"][l], 8)); setv("ln2_b", l, _pt(inp["ln2_b"][l], 8))
        setv("ffn_dw", l, ffn_dwP[l].T.reshape(NUP, 128, 3).transpose(1, 0, 2).reshape(128, NUP * 3))
    Wd = {"w_ada": f(inp["w_ada"]), "w_in": w_inP, "w_conv_out": f(inp["w_conv_out"]),
          "w_sc_out": f(inp["w_sc_out"]), "w_uq": w_uqP, "w_ukv": f(inp["w_ukv"]), "w_mla_out": f(inp["w_mla_out"]),
          "w_pool": f(inp["w_pool"]), "w_pool_out": f(inp["w_pool_out"]), "w_o": f(inp["w_o"]),
          "w_up": w_upP, "w_down": f(inp["w_down"])}
    RT = wrows(n_cores)
    blob = np.zeros((DEPTH, RT * WCOLS), np.float32)
    for n_, r_, c_ in WLAY:
        o_ = WOFF[n_][0]
        blob[:, o_:o_ + r_ * c_] = Wd[n_][:DEPTH].reshape(DEPTH, r_ * c_)
    return vecs, blob.reshape(DEPTH, RT, WCOLS)


def run_model(inp, NB, CPS, TPC, DEPTH, layers_per_launch=None):
    n_cores = NB * CPS
    lpl = layers_per_launch or DEPTH
    x = np.asarray(inp["x"], np.float32)
    c = np.asarray(inp["c"], np.float32)
    pos = np.asarray(inp["positions"], np.int32)
    vecs_h, blob_full = prep_weights(inp, DEPTH, n_cores)
    inv = (1.0 / (10000.0 ** (np.arange(0, 32, 2, dtype=np.float32) / np.float32(32)))).astype(np.float32)
    tri = (np.arange(128)[:, None] <= np.arange(128)[None, :]).astype(np.float32)
    shift = np.zeros((128, 64), np.float32)
    shift[64 + np.arange(64), np.arange(64)] = 1.0
    base_maps = []
    for core in range(n_cores):
        b, j = core // CPS, core % CPS
        sl = slice(j * TPC, (j + 1) * TPC)
        cst = np.zeros((128, 16), np.float32)
        cst[64:80, 0] = inv; cst[80:96, 0] = inv
        cst[64:80, 1] = -1.0; cst[80:96, 1] = 1.0
        cst[:, 2] = 1.0 if j > 0 else 0.0
        for r in range(3):
            cst[:, 3 + r] = 1.0 if r < j else 0.0
        for r in range(CPS):
            cst[:, 6 + r] = 1.0 if r == j - 1 else 0.0
        rc = np.zeros((128, 4, 16), np.float32)
        m = {"xT": np.ascontiguousarray(x[b, sl, :].T),
             "posr": np.ascontiguousarray(np.broadcast_to(pos[b, sl][None, :], (128, TPC))).astype(np.int32),
             "cT": _pt(c[b], KD), "cst": cst, "rcnt": rc.reshape(128, 64), "tri": tri, "shift": shift}
        base_maps.append(m)
    nc = build_program(CPS, TPC, lpl, n_cores, alpha_depth=DEPTH)
    res = None
    for l0 in range(0, DEPTH, lpl):
        in_maps = []
        for core in range(n_cores):
            m = dict(base_maps[core])
            if res is not None:
                m["xT"] = np.ascontiguousarray(res.results[core]["outT"])
            m["vecs"] = np.ascontiguousarray(vecs_h[l0:l0 + lpl])
            m["wfull"] = blob_full[l0:l0 + lpl]
            in_maps.append(m)
        res = run_bass_kernel_spmd(nc, in_maps, core_ids=list(range(n_cores)))
    if DEBUG:
        LAST["res"] = res.results
    out = np.zeros((NB, CPS * TPC, D), np.float32)
    for core in range(n_cores):
        b, j = core // CPS, core % CPS
        out[b, j * TPC:(j + 1) * TPC, :] = res.results[core]["outT"].T
    return out


LAYERS_PER_LAUNCH = 4


def kernel(**inputs):
    return run_model(inputs, NB=2, CPS=4, TPC=4096, DEPTH=4, layers_per_launch=LAYERS_PER_LAUNCH)
```

```python
import math
from contextlib import ExitStack
import numpy as np
import concourse.bass as bass
import concourse.mybir as mybir
from concourse.bass_utils import run_bass_kernel_spmd

F32 = mybir.dt.float32
BF16 = mybir.dt.bfloat16
I32 = mybir.dt.int32
AF = mybir.ActivationFunctionType
ALU = mybir.AluOpType

D = 1024
KD = 8
DFF = 2816
NUP = 44
NFF = 22
NH = 8
LN_EPS = 1e-5
RMS_EPS = 1e-6
NCH_IN = 61
HALO = 32
SCALE = 96.0 ** -0.5

VEC = {}
_o = 0
for _n, _w in [("b_ada", 48), ("b_in", NCH_IN), ("conv_dw", 4 * 31), ("cln_g", 4), ("cln_b", 4), ("sc_dw", 12),
               ("qn_g", 2), ("kvn_g", 1), ("pool_scale", 4), ("ln1_g", 8), ("ln1_b", 8), ("ln2_g", 8),
               ("ln2_b", 8), ("ffn_dw", NUP * 3)]:
    VEC[_n] = _o
    _o += _w
NV = _o


class Buf:
    __slots__ = ("name", "last_w", "readers")

    def __init__(self, name):
        self.name = name
        self.last_w = None
        self.readers = []


class Op:
    __slots__ = ("eng", "fn", "deps", "dma", "needs_inc", "sem", "val", "prev_same_sem")

    def __init__(self, eng, fn, deps, dma):
        self.eng, self.fn, self.deps, self.dma = eng, fn, deps, dma
        self.needs_inc = False
        self.sem = None
        self.val = 0
        self.prev_same_sem = None


ENGS = ("pe", "act", "dve", "pool", "sp")
EPOCH = 4000
NDMASEM = 12


class Sched:
    def __init__(self, same_engine_sync=False):
        self.ops = []
        self.same = same_engine_sync
        self.pending_bar = {}

    def add(self, eng, fn, reads=(), writes=(), dma=False):
        idx = len(self.ops)
        deps = set()
        for b in reads:
            if b.last_w is not None:
                deps.add(b.last_w)
        for b in writes:
            if b.last_w is not None:
                deps.add(b.last_w)
            deps.update(b.readers)
        for b in reads:
            b.readers.append(idx)
        for b in writes:
            b.last_w = idx
            b.readers = []
        keep = []
        for d in deps:
            od = self.ops[d]
            if (not od.dma) and od.eng == eng and not dma:
                if eng == "pe" or not self.same:
                    continue
            keep.append(d)
        if eng in self.pending_bar:
            keep = sorted(set(keep) | set(self.pending_bar.pop(eng)))
        self.ops.append(Op(eng, fn, sorted(keep), dma))
        return idx

    def barrier(self):
        deps = []
        for e in ENGS:
            comp = [i for i in range(len(self.ops) - 1, -1, -1) if self.ops[i].eng == e and not self.ops[i].dma][:1]
            deps += comp
            dm = [i for i in range(len(self.ops) - 1, max(-1, len(self.ops) - 4000), -1)
                  if self.ops[i].eng == e and self.ops[i].dma][:NDMASEM]
            deps += dm
        self.pending_bar = {e: sorted(set(deps)) for e in ENGS}

    def emit(self, nc, es):
        ops = self.ops
        for op in ops:
            for d in op.deps:
                ops[d].needs_inc = True
        cnt = {e: 0 for e in ENGS}
        dcnt = {e: 0 for e in ENGS}
        sems = {}

        def getsem(key):
            if key not in sems:
                sems[key] = es.enter_context(nc.semaphore("s_%s_%s_%d" % key))
            return sems[key]

        last_on_dsem = {}
        for i, op in enumerate(ops):
            if op.dma:
                k = dcnt[op.eng]
                dcnt[op.eng] += 1
                key = ("d", op.eng, k % NDMASEM)
                op.sem = key
                op.val = 16 * (k // NDMASEM + 1)
                op.prev_same_sem = last_on_dsem.get(key)
                last_on_dsem[key] = i
            elif op.needs_inc:
                c = cnt[op.eng]
                cnt[op.eng] += 1
                op.sem = ("c", op.eng, c // EPOCH)
                op.val = c % EPOCH + 1
        for op in ops:
            if op.sem is not None:
                getsem(op.sem)
        import os as _os
        if _os.environ.get("KSTATS"):
            print("KSTATS ops", {e: sum(1 for o in ops if o.eng == e) for e in ENGS}, "incs", cnt, "dmas", dcnt,
                  "nsems", len(sems), flush=True)
        block = es.enter_context(nc.Block())
        streams = {e: [i for i, op in enumerate(ops) if op.eng == e] for e in ENGS}

        def run(eng_name, eng):
            known = {}
            for i in streams[eng_name]:
                op = ops[i]
                waits = [(ops[d].sem, ops[d].val) for d in op.deps]
                if op.dma and op.prev_same_sem is not None:
                    p = ops[op.prev_same_sem]
                    waits.append((p.sem, p.val))
                for key, val in waits:
                    if known.get(key, 0) < val:
                        eng.wait_ge(sems[key], val)
                        known[key] = val
                ins = op.fn(eng)
                if op.dma:
                    ins.then_inc(sems[op.sem], 16)
                elif op.needs_inc:
                    ins.then_inc(sems[op.sem], 1)
            for key, s in sems.items():
                if key[0] == "d" and key[1] == eng_name:
                    li = last_on_dsem[key]
                    if known.get(key, 0) < ops[li].val:
                        eng.wait_ge(s, ops[li].val)

        @block.tensor
        def _(e):
            run("pe", e)

        @block.scalar
        def _(e):
            run("act", e)

        @block.vector
        def _(e):
            run("dve", e)

        @block.gpsimd
        def _(e):
            run("pool", e)

        @block.sync
        def _(e):
            run("sp", e)


class Arena:
    def __init__(self, nc, es, nbytes, name="arena"):
        self.t = es.enter_context(nc.sbuf_tensor(name, [128, nbytes // 4], F32))
        self.off = 0
        self.cap = nbytes
        self.peak = 0

    def reset(self):
        self.off = 0

    def alloc(self, shape, dt):
        free = 1
        for s in shape[1:]:
            free *= s
        esz = 4 if dt in (F32, I32) else 2
        nb = (free * esz + 31) // 32 * 32
        assert self.off + nb <= self.cap, ("arena overflow", self.off, nb, self.cap)
        ap = self.t[:, self.off // 4:(self.off + nb) // 4]
        if dt != F32:
            ap = ap.bitcast(dt)
        ap = ap[:, 0:free]
        self.off += nb
        self.peak = max(self.peak, self.off)
        if len(shape) == 3:
            ap = ap.rearrange("p (a b) -> p a b", b=shape[2])
        elif len(shape) == 4:
            ap = ap.rearrange("p (a b c) -> p a b c", b=shape[2], c=shape[3])
        return ap


class Rot:
    def __init__(self, nc, es, name, shape, dt, n):
        self.t = [es.alloc(shape, dt) for i in range(n)]
        self.b = [Buf("%s%d" % (name, i)) for i in range(n)]
        self.i = 0

    def next(self):
        k = self.i % len(self.t)
        self.i += 1
        return self.t[k], self.b[k]


def build_program(CPS, TPC, DEPTH, n_cores, alpha_depth=None):
    T = 512
    NT = TPC // T
    NKC = CPS
    NK = NKC * TPC
    NKB = NK // 128
    groups = [list(range(g * CPS, (g + 1) * CPS)) for g in range(n_cores // CPS)]
    nc = bass.Bass("TRN2", target_bir_lowering=False)
    S = Sched()
    es = ExitStack()

    def din(name, shape, dt=F32):
        return nc.dram_tensor(name, list(shape), dt, kind="ExternalInput").ap()

    def dint(name, shape, dt=F32):
        if DEBUG and name in DEBUG_NAMES:
            return nc.dram_tensor(name, list(shape), dt, kind="ExternalOutput").ap()
        return nc.dram_tensor(name, list(shape), dt, kind="Internal").ap()

    xT_in = din("xT", [D, TPC])
    pos_in = din("posr", [128, TPC], I32)
    cT_in = din("cT", [128, KD])
    cst_in = din("cst", [128, 16])
    rcnt_in = din("rcnt", [128, 4 * 16])
    tri_in = din("tri", [128, 128])
    shift_in = din("shift", [128, 64])
    RT = wrows(n_cores)
    RSH = RT // n_cores
    vecs_in = din("vecs", [DEPTH, 128, NV])
    wfull_in = din("wfull", [DEPTH, RT, WCOLS])
    wfull = [dint("wbf%d" % i, [RT, WCOLS], BF16) for i in range(DEPTH)]
    bwsrc = [Buf("wsrc%d" % i) for i in range(DEPTH)]
    bwfull = [Buf("wfull%d" % i) for i in range(DEPTH)]
    curl = [0]

    def Wl(name):
        off, r, c = WOFF[name]
        return wfull[curl[0]].rearrange("r c -> (r c)")[off:off + r * c].rearrange("(r c) -> r c", c=c)

    def wslot(name, c0, layer=None):
        o_, kp_, w_ = WSLOT[(name, c0)]
        li = curl[0] if layer is None else layer
        return wfull[li].rearrange("r c -> (r c)")[o_:o_ + 128 * kp_ * w_].rearrange(
            "(p k c) -> p k c", p=128, k=kp_, c=w_)

    def gather_weights(i, part=0, nparts=1):
        keys = list(WSLOT.keys())
        per = (len(keys) + nparts - 1) // nparts
        for (name, c0) in keys[part * per:(part + 1) * per]:
            o_, kp_, w_ = WSLOT[(name, c0)]
            mo_, r_, c_ = WOFF[name]
            srcm = wfull_in[i].rearrange("r c -> (r c)")[mo_:mo_ + r_ * c_].rearrange("(r c) -> r c", c=c_)
            dma("pool", wslot(name, c0, i), srcm[:, c0:c0 + w_].rearrange("(k p) c -> p k c", p=128),
                [], [bwfull[i]])

    out_T = nc.dram_tensor("outT", [D, TPC], F32, kind="ExternalOutput").ap()

    xA = dint("xA", [D, TPC]); bxA = Buf("xA")
    xM = dint("xM", [D, TPC]); bxM = Buf("xM")
    tail_src = dint("tail_src", [D, HALO]); btail_src = Buf("tail_src")
    tail_all = dint("tail_all", [CPS * D, HALO]); btail_all = Buf("tail_all")
    kv_src = dint("kv_src", [NT * 160, T], BF16); bkv_src = [Buf("kv_src%d" % i) for i in range(NT)]
    kv_all = dint("kv_all", [NT * CPS * 160, T], BF16); bkv_all = [Buf("kv_all%d" % i) for i in range(NT)]
    mg_d = dint("mg_d", [D, TPC]); bmg = Buf("mg_d")
    g2_d = dint("g2_d", [D, TPC], BF16); bg2 = Buf("g2_d")
    q_d = dint("q_d", [NH * 96, TPC], BF16); bq = Buf("q_d")
    at_d = dint("at_d", [NH * 64, TPC], BF16); bat = Buf("at_d")
    cc_d = dint("cc_d", [32, TPC]); bcc = Buf("cc_d")
    ss_d = dint("ss_d", [32, TPC]); bss = Buf("ss_d")

    def sb(name, shape, dt=F32):
        return es.enter_context(nc.sbuf_tensor("sb_" + name, list(shape), dt))

    cst = sb("cst", [128, 16]); bcst = Buf("cst")
    rcnt = sb("rcnt", [128, 4, 16])
    tri = sb("tri", [128, 128], BF16)
    shiftm = sb("shiftm", [128, 64])
    epsc = sb("epsc", [128, 4]); bconst = Buf("const")
    ones = sb("ones", [128, 3, 128], BF16)
    ones1k = sb("ones1k", [128, 128], BF16)
    cact = sb("cact", [128, KD], BF16); bcact = Buf("cact")
    mod = sb("mod", [128, 48]); bmod = Buf("mod")
    vecs = sb("vecs", [128, NV]); bvecs = Buf("vecs")
    ps = es.enter_context(nc.psum_tensor("ps", [128, 8, 512], F32))
    bps = [Buf("ps%d" % i) for i in range(8)]
    psi = [0]

    def nps():
        k = psi[0] % 8
        psi[0] += 1
        return k

    AR = Arena(nc, es, 168 * 1024)
    ST = Rot(nc, Arena(nc, es, 8 * 1024, "arena_st"), "st", [128, T], F32, 4)
    rc2 = sb("rc2", [128, 64]); brc2 = Buf("rc2")
    uph = sb("uph", [128, NUP, 2]); buph = Buf("uph")
    tailsb = sb("tailsb", [128, CPS, KD, HALO]); btailsb = Buf("tailsb")
    xhalo = sb("xhalo", [128, KD, HALO]); bxhalo = Buf("xhalo")
    wukv = sb("wukv", [128, 1024], BF16); bwukv = Buf("wukv")
    ALPHA = (2.0 * (alpha_depth or DEPTH)) ** 0.25
    XT = HT = WS = F1 = B1 = F4 = B4 = B8 = F8 = tabs = upb = PW = qh = PT = None
    bufA = bufB = bufP = merged = ffin = kvnT = kT = vaug = None
    bbufA = [Buf("bufA%d" % j) for j in range(4)]
    bbufB = [Buf("bufB%d" % j) for j in range(4)]
    bbufP = [Buf("bufP%d" % j) for j in range(4)]
    bmerged = [Buf("mg%d" % m) for m in range(KD)]
    bffin = [Buf("ffin%d" % i) for i in range(NFF)]
    bkvnT = [Buf("kvnT%d" % r) for r in range(NKC)]
    bkTn = Buf("kTn"); bkTr = Buf("kTr"); bvv = Buf("vaug_v"); bvo = Buf("vaug_o")

    P = lambda a: a

    def dma(q, out, in_, reads, writes):
        S.add(q, lambda e, o=out, i=in_: e.dma_start(out=o, in_=i), reads, writes, dma=True)

    def act(out, in_, func, reads, writes, scale=None, bias=None):
        kw = {}
        if scale is not None:
            kw["scale"] = scale
        if bias is not None:
            kw["bias"] = bias
        S.add("act", lambda e, o=out, i=in_, f=func, kw=kw: e.activation(out=o, in_=i, func=f, **kw), reads, writes)

    def tt(out, in0, in1, op, reads, writes, eng="dve"):
        S.add(eng, lambda e, o=out, a=in0, b=in1, p=op: e.tensor_tensor(out=o, in0=a, in1=b, op=p), reads, writes)

    def ts(out, in0, s1, op0, reads, writes, s2=None, op1=None, eng="dve"):
        if op1 is None:
            S.add(eng, lambda e, o=out, a=in0, s=s1, p=op0: e.tensor_scalar(out=o, in0=a, scalar1=s, scalar2=None, op0=p),
                  reads, writes)
        else:
            S.add(eng, lambda e, o=out, a=in0, s=s1, p=op0, s_2=s2, p1=op1: e.tensor_scalar(
                out=o, in0=a, scalar1=s, scalar2=s_2, op0=p, op1=p1), reads, writes)

    def stt(out, in0, scalar, in1, op0, op1, reads, writes):
        S.add("dve", lambda e, o=out, a=in0, s=scalar, b=in1, p0=op0, p1=op1: e.scalar_tensor_tensor(
            out=o, in0=a, scalar=s, in1=b, op0=p0, op1=p1), reads, writes)

    def mm(out, pairs, reads, writes):
        def fn(e, out=out, pairs=pairs):
            n = len(pairs)
            ins = None
            for i, (l, r) in enumerate(pairs):
                ins = e.matmul(out, lhsT=l, rhs=r, start=(i == 0), stop=(i == n - 1))
            return ins
        S.add("pe", fn, reads, writes)

    def vcol(name, i=0, n=1):
        o = VEC[name] + i
        return vecs[:, o:o + n]

    def wload(name, c0, ncols, k0=0, kn=None):
        o_, kp_, w_ = WSLOT[(name, c0)]
        assert w_ == ncols, (name, c0, ncols, w_)
        kn = kp_ if kn is None else kn
        t, b = WS.next()
        dma("sp", t[:, 0:kn, 0:ncols], wslot(name, c0)[:, k0:k0 + kn, :], [bwfull[curl[0]]], [b])
        return t, b

    S.add("sp", lambda e: e.dma_start(out=cst[:], in_=cst_in), [], [bcst], dma=True)
    S.add("sp", lambda e: e.dma_start(out=rcnt[:].rearrange("p a b -> p (a b)"), in_=rcnt_in), [], [bconst], dma=True)
    S.add("pool", lambda e: e.dma_start(out=tri[:], in_=tri_in), [], [bconst], dma=True)
    S.add("sp", lambda e: e.dma_start(out=shiftm[:], in_=shift_in), [], [bconst], dma=True)
    for col, v in enumerate([LN_EPS, RMS_EPS, 0.0, 1.0]):
        S.add("dve", lambda e, c=col, v=v: e.memset(epsc[:, c:c + 1], v), [], [bconst])
    for k, v in enumerate([1.0 / 128, 1.0 / 256, 1.0 / 512]):
        S.add("dve", lambda e, k=k, v=v: e.memset(ones[:, k, :], v), [], [bconst])
    S.add("dve", lambda e: e.memset(ones1k[:], 1.0 / 1024), [], [bconst])
    for g_, w_ in enumerate((2, 4, 8, 16)):
        for t_ in range(w_ - 1):
            A_ = 1.0 / (t_ + 1)
            B_ = 1.0 / w_ - A_
            ts(rc2[:, g_ * 16 + t_:g_ * 16 + t_ + 1], cst[:, 2:3], B_, ALU.mult, [bcst], [brc2], s2=A_, op1=ALU.add)
    eps_ln = epsc[:, 0:1]
    eps_rms = epsc[:, 1:2]

    ctmp = sb("ctmp", [128, KD])
    S.add("sp", lambda e: e.dma_start(out=ctmp[:], in_=cT_in), [], [bcact], dma=True)
    act(cact[:], ctmp[:], AF.Silu, [bcact], [bcact])

    R = slice(64, 96)
    TWO_PI = 2.0 * math.pi
    PI_HI = 6.28125
    PI_LO = TWO_PI - PI_HI
    PI_SAFE = 3.14159
    AR.reset()
    posi = AR.alloc([128, T], I32)
    bro = Buf("ropetmp")
    ang = AR.alloc([128, T], F32); kf = AR.alloc([128, T], F32); ki = AR.alloc([128, T], I32)
    rr = AR.alloc([128, T], F32); mk = AR.alloc([128, T], F32); sn = AR.alloc([128, T], F32)
    for n in range(NT):
        cs = slice(n * T, (n + 1) * T)
        dma("sp", posi[R, :], pos_in[64:96, cs], [], [bro])
        S.add("dve", lambda e: e.tensor_copy(out=ang[R, :], in_=posi[R, :]), [bro], [bro])
        ts(ang[R, :], ang[R, :], cst[R, 0:1], ALU.mult, [bro, bcst], [bro])
        ts(kf[R, :], ang[R, :], 1.0 / TWO_PI, ALU.mult, [bro], [bro])
        S.add("dve", lambda e: e.tensor_copy(out=ki[R, :], in_=kf[R, :]), [bro], [bro])
        S.add("dve", lambda e: e.tensor_copy(out=kf[R, :], in_=ki[R, :]), [bro], [bro])
        stt(rr[R, :], kf[R, :], -PI_HI, ang[R, :], ALU.mult, ALU.add, [bro], [bro])
        stt(rr[R, :], kf[R, :], -PI_LO, rr[R, :], ALU.mult, ALU.add, [bro], [bro])

        def wrap(dst, src):
            ts(mk[R, :], src, math.pi, ALU.is_gt, [bro], [bro])
            stt(dst, mk[R, :], -TWO_PI, src, ALU.mult, ALU.add, [bro], [bro])
            ts(mk[R, :], dst, -math.pi, ALU.is_lt, [bro], [bro])
            stt(dst, mk[R, :], TWO_PI, dst, ALU.mult, ALU.add, [bro], [bro])
            ts(dst, dst, -PI_SAFE, ALU.max, [bro], [bro], s2=PI_SAFE, op1=ALU.min)
        wrap(rr[R, :], rr[R, :])
        act(sn[R, :], rr[R, :], AF.Sin, [bro], [bro])
        ts(sn[R, :], sn[R, :], cst[R, 1:2], ALU.mult, [bro, bcst], [bro])
        dma("sp", ss_d[:, cs], sn[R, :], [bro], [bss])
        ts(ang[R, :], rr[R, :], math.pi / 2, ALU.add, [bro], [bro])
        wrap(ang[R, :], ang[R, :])
        act(kf[R, :], ang[R, :], AF.Sin, [bro], [bro])
        dma("sp", cc_d[:, cs], kf[R, :], [bro], [bcc])

    for k in range(KD):
        dma("sp", xA[k * 128:(k + 1) * 128, :], xT_in[k * 128:(k + 1) * 128, :], [], [bxA])

    def exchange_tail(xsrc, bxsrc):
        dma("pool", tail_src, xsrc[:, TPC - HALO:TPC], [bxsrc], [btail_src])
        S.add("pool", lambda e: e.collective_compute("AllGather", ALU.bypass, replica_groups=groups,
                                                     ins=[tail_src], outs=[tail_all]),
              [btail_src], [btail_all])
        dma("pool", tailsb[:].rearrange("p r k h -> p (r k) h"),
            tail_all.rearrange("(rk p) h -> p rk h", p=128), [btail_all], [btailsb])
        xh = xhalo[:].rearrange("p k h -> p (k h)")
        for r in range(CPS):
            src = tailsb[:, r].rearrange("p k h -> p (k h)")
            if r == 0:
                ts(xh, src, cst[:, 6:7], ALU.mult, [btailsb, bcst], [bxhalo])
            else:
                stt(xh, src, cst[:, 6 + r:7 + r], xh, ALU.mult, ALU.add, [btailsb, bcst], [bxhalo])

    def modulate(xt, bxt, N, off):
        ht, bht = HT.next()
        for k in range(KD):
            act(ht[:, k, 0:N], xt[:, k, 0:N], AF.Identity, [bxt, bmod], [bht],
                scale=mod[:, off + 8 + k:off + 9 + k], bias=mod[:, off + k:off + k + 1])
        return ht, bht

    def proj_group(wsrc, c0, ncols, ht, bht, N, kparts=KD):
        wt, bw = wload(wsrc, c0, ncols)
        banks = []
        nchunks = (ncols + 127) // 128
        for ci in range(nchunks):
            mcols = min(128, ncols - ci * 128)
            b = nps()
            mm(ps[0:mcols, b, 0:N], [(wt[:, k, ci * 128:ci * 128 + mcols], ht[:, k, 0:N]) for k in range(kparts)],
               [bw, bht], [bps[b]])
            banks.append(b)
        return banks

    def stats(src_bf, bsrc, sq_bf, bsq, nchunk, ones_ap, N):
        bm = nps()
        mm(ps[:, bm, 0:N], [(ones_ap, src_bf[:, j, 0:N]) for j in range(nchunk)], [bsrc, bconst], [bps[bm]])
        be = nps()
        mm(ps[:, be, 0:N], [(ones_ap, sq_bf[:, j, 0:N]) for j in range(nchunk)], [bsq, bconst], [bps[be]])
        return bm, be

    def ln_finish(bm, be, N, eps_ap):
        mean, bmean = ST.next()
        act(mean[:, 0:N], ps[:, bm, 0:N], AF.Copy, [bps[bm]], [bmean])
        var, bvar = ST.next()
        tt(var[:, 0:N], mean[:, 0:N], mean[:, 0:N], ALU.mult, [bmean], [bvar])
        tt(var[:, 0:N], ps[:, be, 0:N], var[:, 0:N], ALU.subtract, [bps[be], bvar], [bvar])
        ts(var[:, 0:N], var[:, 0:N], 0.0, ALU.max, [bvar], [bvar])
        act(var[:, 0:N], var[:, 0:N], AF.Sqrt, [bvar, bconst], [bvar], bias=eps_ap)
        S.add("dve", lambda e, v=var, N=N: e.reciprocal(out=v[:, 0:N], in_=v[:, 0:N]), [bvar], [bvar])
        return mean, bmean, var, bvar

    def layer_norm_full(xin, bxin, gname, bname, N, dst_ap_fn, bdst):
        xb, bxb = B8.next()
        sq, bsq = B8.next()
        for k in range(KD):
            act(xb[:, k, 0:N], xin[:, k, 0:N], AF.Copy, [bxin], [bxb])
            act(sq[:, k, 0:N], xin[:, k, 0:N], AF.Square, [bxin], [bsq])
        bm, be = stats(xb, bxb, sq, bsq, KD, ones1k[:], N)
        mean, bmean, rstd, brstd = ln_finish(bm, be, N, eps_ln)
        for k in range(KD):
            d, bd = F1.next()
            tt(d[:, 0:N], xin[:, k, 0:N], mean[:, 0:N], ALU.subtract, [bxin, bmean], [bd])
            tt(d[:, 0:N], d[:, 0:N], rstd[:, 0:N], ALU.mult, [bd, brstd], [bd])
            act(dst_ap_fn(k), d[:, 0:N], AF.Identity, [bd, bvecs], [bdst], scale=vcol(gname, k), bias=vcol(bname, k))

    for l in range(DEPTH):
        S.barrier()
        AR.reset()
        XT = Rot(nc, AR, "xt", [128, KD, T], F32, 1)
        HT = Rot(nc, AR, "ht", [128, KD, T], BF16, 1)
        WS = Rot(nc, AR, "ws", [128, KD, 512], BF16, 3)
        F1 = Rot(nc, AR, "f1", [128, T], F32, 6)
        B1 = Rot(nc, AR, "b1", [128, T], BF16, 3)
        F4 = Rot(nc, AR, "f4", [128, 4, T], F32, 2)
        B4 = Rot(nc, AR, "b4", [128, 4, T], BF16, 5)
        B8 = Rot(nc, AR, "b8", [128, KD, T], BF16, 1)
        tabs = Rot(nc, AR, "tabs", [128, 2, T], F32, 1)
        PW = Rot(nc, AR, "pw", [128, 16 + T], F32, 3)
        bufA = AR.alloc([128, 4, 30 + T], F32)
        bufB = AR.alloc([128, 4, 2 + T], F32)
        bufP = AR.alloc([128, 4, 16 + T], F32)
        merged = AR.alloc([128, KD, T], F32)
        curl[0] = l
        if l == 0:
            gather_weights(0)
        dma("sp", vecs[:], vecs_in[l], [], [bvecs])
        for g6 in range(6):
            for half in range(2):
                wt, bw = wload("w_ada", g6 * D + half * 512, 512)
                for ci in range(4):
                    mcol = g6 * 8 + half * 4 + ci
                    b = nps()
                    mm(ps[:, b, 0:1], [(wt[:, k, ci * 128:(ci + 1) * 128], cact[:, k:k + 1]) for k in range(KD)],
                       [bw, bcact], [bps[b]])
                    act(mod[:, mcol:mcol + 1], ps[:, b, 0:1], AF.Identity, [bps[b], bvecs], [bmod],
                        bias=vcol("b_ada", mcol))
        for g6 in (1, 2, 4, 5):
            ts(mod[:, g6 * 8:(g6 + 1) * 8], mod[:, g6 * 8:(g6 + 1) * 8], 1.0, ALU.add, [bmod], [bmod])

        exchange_tail(xA, bxA)

        wi = "w_in"

        def phase1_tile(n):
            halo = n < 0
            N = HALO if halo else T
            if halo:
                xt, bxt = xhalo, bxhalo
            else:
                xt, bxt = XT.next()
                dma("pool", xt[:], xA[:, n * T:(n + 1) * T].rearrange("(k p) t -> p k t", p=128), [bxA], [bxt])
            ht, bht = modulate(xt, bxt, N, 0)

            bA = proj_group(wi, 0, 512, ht, bht, N)
            bB = proj_group(wi, 512, 512, ht, bht, N)
            for j in range(4):
                sg, bsg = F1.next()
                act(sg[:, 0:N], ps[:, bB[j], 0:N], AF.Sigmoid, [bps[bB[j]], bvecs], [bsg], bias=vcol("b_in", 4 + j))
                if halo:
                    tmp, btmp = F1.next()
                    stt(tmp[:, 0:N], ps[:, bA[j], 0:N], vcol("b_in", j), sg[:, 0:N], ALU.add, ALU.mult,
                        [bps[bA[j]], bsg, bvecs], [btmp])
                    ts(bufA[:, j, 0:30], tmp[:, 2:32], cst[:, 2:3], ALU.mult, [btmp, bcst], [bbufA[j]])
                else:
                    stt(bufA[:, j, 30:30 + T], ps[:, bA[j], 0:T], vcol("b_in", j), sg[:, 0:T], ALU.add, ALU.mult,
                        [bps[bA[j]], bsg, bvecs], [bbufA[j]])
            bC = proj_group(wi, 12 * 128, 512, ht, bht, N)
            bX = proj_group(wi, 16 * 128, 512, ht, bht, N)
            for j in range(4):
                cs_, bcs = F1.next()
                act(cs_[:, 0:N], ps[:, bC[j], 0:N], AF.Identity, [bps[bC[j]], bvecs], [bcs], bias=vcol("b_in", 12 + j))
                if halo:
                    tmp, btmp = F1.next()
                    stt(tmp[:, 0:N], ps[:, bX[j], 0:N], vcol("b_in", 16 + j), cs_[:, 0:N], ALU.add, ALU.mult,
                        [bps[bX[j]], bcs, bvecs], [btmp])
                    ts(bufB[:, j, 0:2], tmp[:, 30:32], cst[:, 2:3], ALU.mult, [btmp, bcst], [bbufB[j]])
                else:
                    stt(bufB[:, j, 2:2 + T], ps[:, bX[j], 0:T], vcol("b_in", 16 + j), cs_[:, 0:T], ALU.add, ALU.mult,
                        [bps[bX[j]], bcs, bvecs], [bbufB[j]])
            bP = proj_group(wi, 25 * 128, 512, ht, bht, N)
            for j in range(4):
                if halo:
                    tmp, btmp = F1.next()
                    act(tmp[:, 0:N], ps[:, bP[j], 0:N], AF.Identity, [bps[bP[j]], bvecs], [btmp], bias=vcol("b_in", 25 + j))
                    ts(bufP[:, j, 0:16], tmp[:, 16:32], cst[:, 2:3], ALU.mult, [btmp, bcst], [bbufP[j]])
                else:
                    act(bufP[:, j, 16:16 + T], ps[:, bP[j], 0:T], AF.Identity, [bps[bP[j]], bvecs], [bbufP[j]],
                        bias=vcol("b_in", 25 + j))
            if halo:
                return

            g2t, bg2t = B8.next()
            for half in range(2):
                bg_ = proj_group(wi, (29 + 2 * 8 + half * 4) * 128, 512, ht, bht, T)
                for ci in range(4):
                    m = half * 4 + ci
                    act(g2t[:, m, :], ps[:, bg_[ci], :], AF.Sigmoid, [bps[bg_[ci]], bvecs], [bg2t],
                        bias=vcol("b_in", 29 + 16 + m))
            dma("pool", g2_d[:, n * T:(n + 1) * T].rearrange("(k p) t -> p k t", p=128), g2t[:], [bg2t], [bg2])

            tb, btb = tabs.next()
            dma("pool", tb[64:96, 0, :], cc_d[:, n * T:(n + 1) * T], [bcc], [btb])
            dma("pool", tb[64:96, 1, :], ss_d[:, n * T:(n + 1) * T], [bss], [btb])
            bQ = proj_group(wi, 20 * 128, 512, ht, bht, T)
            bK2 = proj_group(wi, 24 * 128, 128, ht, bht, T)
            ql, bql = F4.next()
            qsq, bqsq = B4.next()
            for j in range(2):
                act(ql[:, j, :], ps[:, bQ[j], :], AF.Identity, [bps[bQ[j]], bvecs], [bql], bias=vcol("b_in", 20 + j))
                act(qsq[:, j, :], ql[:, j, :], AF.Square, [bql], [bqsq])
            act(ql[:, 2, :], ps[:, bQ[2], :], AF.Identity, [bps[bQ[2]], bvecs], [bql], bias=vcol("b_in", 22))
            act(qsq[:, 2, :], ql[:, 2, :], AF.Square, [bql], [bqsq])
            bqs = nps()
            mm(ps[:, bqs, :], [(ones[:, 1, :], qsq[:, j, :]) for j in range(2)], [bqsq, bconst], [bps[bqs]])
            bks = nps()
            mm(ps[:, bks, :], [(ones[:, 0, :], qsq[:, 2, :])], [bqsq, bconst], [bps[bks]])
            qn, bqn = B4.next()
            for (bb, js, gname) in ((bqs, (0, 1), "qn_g"), (bks, (2,), "kvn_g")):
                rs, brs = F1.next()
                act(rs[:], ps[:, bb, :], AF.Sqrt, [bps[bb], bconst], [brs], bias=eps_rms)
                S.add("dve", lambda e, v=rs: e.reciprocal(out=v[:], in_=v[:]), [brs], [brs])
                for j in js:
                    stt(qn[:, j, :], ql[:, j, :], vcol(gname, j if gname == "qn_g" else 0), rs[:], ALU.mult, ALU.mult,
                        [bql, brs, bvecs], [bqn])
            dma("pool", kv_src[n * 160:n * 160 + 128, :], qn[:, 2, :], [bqn], [bkv_src[n]])
            kr, bkr = F1.next()
            kr2, bkr2 = F1.next()
            stt(kr[R, :], ps[R, bQ[3], :], vecs[R, VEC["b_in"] + 23:VEC["b_in"] + 24], tb[R, 0, :], ALU.add, ALU.mult,
                [bps[bQ[3]], btb, bvecs], [bkr])
            stt(kr2[R, :], ps[R, bK2[0], :], vecs[R, VEC["b_in"] + 24:VEC["b_in"] + 25], tb[R, 1, :], ALU.add, ALU.mult,
                [bps[bK2[0]], btb, bvecs], [bkr2])
            krb, bkrb = B1.next()
            tt(krb[R, :], kr[R, :], kr2[R, :], ALU.add, [bkr, bkr2], [bkrb])
            dma("pool", kv_src[n * 160 + 128:(n + 1) * 160, :], krb[R, :], [bkrb], [bkv_src[n]])
            S.add("pool", lambda e, n=n: e.collective_compute(
                "AllGather", ALU.bypass, replica_groups=groups, ins=[kv_src[n * 160:(n + 1) * 160, :]],
                outs=[kv_all[n * CPS * 160:(n + 1) * CPS * 160, :]]), [bkv_src[n]], [bkv_all[n]])
            wqs = []
            for s4 in range(4):
                wq_, bwq_ = WS.next()
                dma("sp", wq_[:, 0:2, 0:512], wslot("w_uq", s4 * 512), [bwfull[l]], [bwq_])
                outs4 = []
                for h4 in range(4):
                    b = nps()
                    mm(ps[0:96, b, :], [(wq_[:, k, h4 * 128:h4 * 128 + 96], qn[:, k, :]) for k in range(2)],
                       [bwq_, bqn], [bps[b]])
                    outs4.append(b)
                wqs.append(outs4)
                if s4 % 2 == 0:
                    continue
                for h4 in range(4):
                    h = (s4 // 2) * 4 + h4
                    outs = [wqs[s4 - 1][h4], wqs[s4][h4]]
                    qo, bqo = B1.next()
                    act(qo[0:64, :], ps[0:64, outs[0], :], AF.Copy, [bps[outs[0]]], [bqo])
                    t1, bt1 = F1.next()
                    t2, bt2 = F1.next()
                    tt(t1[R, :], ps[R, outs[0], :], tb[R, 0, :], ALU.mult, [bps[outs[0]], btb], [bt1])
                    tt(t2[R, :], ps[R, outs[1], :], tb[R, 1, :], ALU.mult, [bps[outs[1]], btb], [bt2])
                    tt(qo[R, :], t1[R, :], t2[R, :], ALU.add, [bt1, bt2], [bqo])
                    dma("pool", q_d[h * 96:(h + 1) * 96, n * T:(n + 1) * T], qo[0:96, :], [bqo], [bq])

            accA, baccA = F4.next()
            for j in range(4):
                for k in range(31):
                    wk = vcol("conv_dw", j * 31 + k)
                    if k == 0:
                        ts(accA[:, j, :], bufA[:, j, 0:T], wk, ALU.mult, [bbufA[j], bvecs], [baccA])
                    else:
                        stt(accA[:, j, :], bufA[:, j, k:k + T], wk, accA[:, j, :], ALU.mult, ALU.add,
                            [bbufA[j], bvecs], [baccA])
                ts(bufA[:, j, 0:30], bufA[:, j, T:T + 30], 1.0, ALU.mult, [bbufA[j]], [bbufA[j]])
            ab, bab = B4.next()
            sq, bsq = B4.next()
            for j in range(4):
                act(ab[:, j, :], accA[:, j, :], AF.Copy, [baccA], [bab])
                act(sq[:, j, :], accA[:, j, :], AF.Square, [baccA], [bsq])
            bm, be = stats(ab, bab, sq, bsq, 4, ones[:, 2, :], T)
            mean, bmean, rstd, brstd = ln_finish(bm, be, T, eps_ln)
            yA, byA = B4.next()
            for j in range(4):
                d, bd = F1.next()
                tt(d[:], accA[:, j, :], mean[:], ALU.subtract, [baccA, bmean], [bd])
                tt(d[:], d[:], rstd[:], ALU.mult, [bd, brstd], [bd])
                act(yA[:, j, :], d[:], AF.Silu, [bd, bvecs], [byA], scale=vcol("cln_g", j), bias=vcol("cln_b", j))

            bG = proj_group(wi, 8 * 128, 512, ht, bht, T)
            yB, byB = B4.next()
            for j in range(4):
                acc, bacc = F1.next()
                for k in range(3):
                    wk = vcol("sc_dw", j * 3 + k)
                    if k == 0:
                        ts(acc[:], bufB[:, j, 0:T], wk, ALU.mult, [bbufB[j], bvecs], [bacc])
                    else:
                        stt(acc[:], bufB[:, j, k:k + T], wk, acc[:], ALU.mult, ALU.add, [bbufB[j], bvecs], [bacc])
                ts(bufB[:, j, 0:2], bufB[:, j, T:T + 2], 1.0, ALU.mult, [bbufB[j]], [bbufB[j]])
                stt(yB[:, j, :], ps[:, bG[j], :], vcol("b_in", 8 + j), acc[:], ALU.add, ALU.mult,
                    [bps[bG[j]], bacc, bvecs], [byB])

            pd, bpd = B4.next()
            for g in range(4):
                w = 2 << g
                cur = bufP[:, g, :]
                bcur = bbufP[g]
                span = 16 + T
                sh = 1
                srcs = (cur, bcur)
                while sh < w:
                    tmpP, btmpP = PW.next()
                    tt(tmpP[:, sh:span], srcs[0][:, sh:span], srcs[0][:, 0:span - sh], ALU.add, [srcs[1]], [btmpP])
                    srcs = (tmpP, btmpP)
                    sh *= 2
                o, bo = F1.next()
                stt(o[:], srcs[0][:, 16:16 + T], 1.0 / w, bufP[:, g, 16:16 + T], ALU.mult, ALU.subtract,
                    [srcs[1], bbufP[g]], [bo])
                if n == 0:
                    for t_ in range(w - 1):
                        stt(o[:, t_:t_ + 1], srcs[0][:, 16 + t_:17 + t_], rc2[:, g * 16 + t_:g * 16 + t_ + 1],
                            bufP[:, g, 16 + t_:17 + t_], ALU.mult, ALU.subtract, [srcs[1], bbufP[g], brc2], [bo])
                act(pd[:, g, :], o[:], AF.Copy, [bo], [bpd])
                ts(bufP[:, g, 0:16], bufP[:, g, T:T + 16], 1.0, ALU.mult, [bbufP[g]], [bbufP[g]])
            wpt, bwp = WS.next()
            dma("sp", wpt[:, 0:4, 0:128], wslot("w_pool", 0), [bwfull[l]], [bwp])
            yD, byD = B4.next()
            for g in range(4):
                b = nps()
                mm(ps[:, b, :], [(wpt[:, g, 0:128], pd[:, g, :])], [bwp, bpd], [bps[b]])
                act(yD[:, g, :], ps[:, b, :], AF.Copy, [bps[b], bvecs], [byD], scale=vcol("pool_scale", g))

            for bi, (wname, ysrc, bysrc) in enumerate([("w_conv_out", yA, byA), ("w_sc_out", yB, byB),
                                                      ("w_pool_out", yD, byD)]):
                gate_b = (0, 1, 3)[bi]
                for half in range(2):
                    bo_ = proj_group(wname, half * 512, 512, ysrc, bysrc, T, kparts=4)
                    bg_ = proj_group(wi, (29 + gate_b * 8 + half * 4) * 128, 512, ht, bht, T)
                    for ci in range(4):
                        m = half * 4 + ci
                        gt, bgt = F1.next()
                        act(gt[:], ps[:, bg_[ci], :], AF.Sigmoid, [bps[bg_[ci]], bvecs], [bgt],
                            bias=vcol("b_in", 29 + gate_b * 8 + m))
                        if bi == 0:
                            tt(merged[:, m, :], gt[:], ps[:, bo_[ci], :], ALU.mult, [bgt, bps[bo_[ci]]], [bmerged[m]])
                        else:
                            tt(gt[:], gt[:], ps[:, bo_[ci], :], ALU.mult, [bgt, bps[bo_[ci]]], [bgt])
                            tt(merged[:, m, :], merged[:, m, :], gt[:], ALU.add, [bgt, bmerged[m]], [bmerged[m]],
                               eng="pool")
            dma("pool", mg_d[:, n * T:(n + 1) * T].rearrange("(k p) t -> p k t", p=128), merged[:], bmerged, [bmg])
        phase1_tile(-1)
        for n in range(NT):
            phase1_tile(n)
            if l + 1 < DEPTH:
                gather_weights(l + 1, n, NT)

        S.barrier()
        AR.reset()
        kvnT = AR.alloc([128, NK], BF16)
        kT = AR.alloc([128, NK], BF16)
        vaug = AR.alloc([128, NKB, 128], BF16)
        qh = Rot(nc, AR, "qh", [128, TPC], BF16, 2)
        PT = Rot(nc, AR, "pt", [128, 2, T], BF16, 3)
        F1 = Rot(nc, AR, "f1", [128, T], F32, 6)
        B1 = Rot(nc, AR, "b1", [128, T], BF16, 3)
        for r in range(NKC):
            kb0 = r * (TPC // 128)
            kb1 = (r + 1) * (TPC // 128)
            S.add("pool", lambda e, kb0=kb0, kb1=kb1: e.memset(vaug[:, kb0:kb1, 64:128], 1.0), [], [bvo])
            if r < NKC - 1:
                S.add("pool", lambda e, kb0=kb0, kb1=kb1, r=r: e.tensor_scalar(
                    out=vaug[:, kb0:kb1, 64:128], in0=vaug[:, kb0:kb1, 64:128], scalar1=cst[:, 3 + r:4 + r],
                    scalar2=None, op0=ALU.mult), [bcst, bvo], [bvo])
        for r in range(NKC):
            for n in range(NT):
                cs = slice(r * TPC + n * T, r * TPC + (n + 1) * T)
                if r < NKC - 1:
                    r0 = (n * CPS + r) * 160
                    dma("pool", kvnT[:, cs], kv_all[r0:r0 + 128, :], [bkv_all[n]], [bkvnT[r]])
                    dma("pool", kT[64:96, cs], kv_all[r0 + 128:r0 + 160, :], [bkv_all[n]], [bkTr])
                else:
                    dma("pool", kvnT[:, cs], kv_src[n * 160:n * 160 + 128, :], [bkv_src[n]], [bkvnT[r]])
                    dma("pool", kT[64:96, cs], kv_src[n * 160 + 128:(n + 1) * 160, :], [bkv_src[n]], [bkTr])
        dma("sp", wukv[:], wslot("w_ukv", 0)[:, 0, :], [bwfull[l]], [bwukv])
        OB = (4, 5)
        oi = 0
        for h in range(NH):
            for kb5 in range(NK // 512):
                b = 6 + (kb5 % 2)
                mm(ps[0:64, b, :], [(wukv[:, h * 128:h * 128 + 64], kvnT[:, kb5 * 512:(kb5 + 1) * 512])],
                   [bwukv] + bkvnT, [bps[b]])
                act(kT[0:64, kb5 * 512:(kb5 + 1) * 512], ps[0:64, b, :], AF.Copy, [bps[b]], [bkTn])
            for kb8 in range(NKB // 8):
                b = 6 + (kb8 % 2)
                for i8 in range(8):
                    kb = kb8 * 8 + i8
                    mm(ps[:, b, i8 * 64:(i8 + 1) * 64], [(kvnT[:, kb * 128:(kb + 1) * 128],
                                                          wukv[:, h * 128 + 64:(h + 1) * 128])],
                       [bwukv] + bkvnT, [bps[b]])
                r = (kb8 * 8 * 128) // TPC
                src = ps[:, b, :].rearrange("p (a d) -> p a d", d=64)
                if r < NKC - 1:
                    act(vaug[:, kb8 * 8:(kb8 + 1) * 8, 0:64], src, AF.Copy, [bps[b], bcst], [bvv],
                        scale=cst[:, 3 + r:4 + r])
                else:
                    act(vaug[:, kb8 * 8:(kb8 + 1) * 8, 0:64], src, AF.Copy, [bps[b]], [bvv])
            qt, bqt = qh.next()
            dma("pool", qt[0:96, :], q_d[h * 96:(h + 1) * 96, :], [bq], [bqt])
            for n in range(NT):
                kbl = [(kb, 0, False) for kb in range((NKC - 1) * (TPC // 128))]
                own0 = (NKC - 1) * (TPC // 128)
                for kbo in range(4 * n + 4):
                    i = kbo - 4 * n
                    if i < 0:
                        kbl.append((own0 + kbo, 0, False))
                    else:
                        kbl.append((own0 + kbo, i * 128, True))
                ob = OB[oi % 2]
                oi += 1
                nblk = len(kbl)
                pairs = [kbl[i:i + 2] for i in range(0, nblk, 2)]
                npairs = len(pairs)

                def emit_S(pi):
                    sb0 = (pi % 2) * 2
                    for e_, (kb, c0, dg) in enumerate(pairs[pi]):
                        mm(ps[:, sb0 + e_, c0:T], [(kT[0:96, kb * 128:(kb + 1) * 128], qt[0:96, n * T + c0:(n + 1) * T])],
                           [bkTn, bkTr, bqt], [bps[sb0 + e_]])

                emit_S(0)
                for pi in range(npairs):
                    pair = pairs[pi]
                    sb0 = (pi % 2) * 2
                    if pi + 1 < npairs:
                        emit_S(pi + 1)
                    pt, bpt = PT.next()
                    if len(pair) == 2 and pair[0][1] == 0 and pair[1][1] == 0:
                        act(pt[:, 0:2, :], ps[:, sb0:sb0 + 2, :], AF.Exp, [bps[sb0], bps[sb0 + 1]], [bpt], scale=SCALE)
                    else:
                        for e_, (kb, c0, dg) in enumerate(pair):
                            act(pt[:, e_, c0:T], ps[:, sb0 + e_, c0:T], AF.Exp, [bps[sb0 + e_]], [bpt], scale=SCALE)
                    for e_, (kb, c0, dg) in enumerate(pair):
                        if dg:
                            tt(pt[:, e_, c0:c0 + 128], pt[:, e_, c0:c0 + 128], tri[:], ALU.mult, [bpt, bconst], [bpt],
                               eng="pool")
                    for e_, (kb, c0, dg) in enumerate(pair):
                        gi = pi * 2 + e_
                        S.add("pe", lambda e, ob=ob, kb=kb, c0=c0, pt=pt, e_=e_, st=(gi == 0), sp_=(gi == nblk - 1):
                              e.matmul(ps[:, ob, c0:T], lhsT=vaug[:, kb, :], rhs=pt[:, e_, c0:T], start=st, stop=sp_),
                              [bvv, bvo, bpt], [bps[ob]])
                osb, bosb = F1.next()
                act(osb[:], ps[:, ob, :], AF.Copy, [bps[ob]], [bosb])
                bd_ = 6 + (oi % 2)
                mm(ps[0:64, bd_, :], [(shiftm[:], osb[:])], [bosb, bconst], [bps[bd_]])
                rd, brd = F1.next()
                S.add("dve", lambda e, rd=rd, bd_=bd_: e.reciprocal(out=rd[0:64, :], in_=ps[0:64, bd_, :]),
                      [bps[bd_]], [brd])
                ao, bao = B1.next()
                tt(ao[0:64, :], osb[0:64, :], rd[0:64, :], ALU.mult, [bosb, brd], [bao])
                dma("pool", at_d[h * 64:(h + 1) * 64, n * T:(n + 1) * T], ao[0:64, :], [bao], [bat])

        S.barrier()
        AR.reset()
        XT = Rot(nc, AR, "xt", [128, KD, T], F32, 2)
        F8 = Rot(nc, AR, "f8", [128, KD, T], F32, 2)
        F1 = Rot(nc, AR, "f1", [128, T], F32, 6)
        B4 = Rot(nc, AR, "b4", [128, 4, T], BF16, 2)
        B8 = Rot(nc, AR, "b8", [128, KD, T], BF16, 4)
        WS = Rot(nc, AR, "ws", [128, KD, 512], BF16, 3)
        for n in range(NT):
            cs = slice(n * T, (n + 1) * T)
            at, bat_t = B4.next()
            dma("pool", at[:], at_d[:, cs].rearrange("(k p) t -> p k t", p=128), [bat], [bat_t])
            mgt, bmgt = F8.next()
            dma("pool", mgt[:], mg_d[:, cs].rearrange("(k p) t -> p k t", p=128), [bmg], [bmgt])
            g2t, bg2t = B8.next()
            dma("pool", g2t[:], g2_d[:, cs].rearrange("(k p) t -> p k t", p=128), [bg2], [bg2t])
            xt, bxt = XT.next()
            dma("pool", xt[:], xA[:, cs].rearrange("(k p) t -> p k t", p=128), [bxA], [bxt])
            mb, bmb = B8.next()
            for half in range(2):
                bo_ = proj_group("w_mla_out", half * 512, 512, at, bat_t, T, kparts=4)
                for ci in range(4):
                    m = half * 4 + ci
                    tmp, btmp = F1.next()
                    tt(tmp[:], g2t[:, m, :], ps[:, bo_[ci], :], ALU.mult, [bg2t, bps[bo_[ci]]], [btmp])
                    tt(mb[:, m, :], tmp[:], mgt[:, m, :], ALU.add, [btmp, bmgt], [bmb], eng="pool")
            x1, bx1 = F8.next()
            for half in range(2):
                bo_ = proj_group("w_o", half * 512, 512, mb, bmb, T)
                for ci in range(4):
                    m = half * 4 + ci
                    tmp, btmp = F1.next()
                    act(tmp[:], ps[:, bo_[ci], :], AF.Copy, [bps[bo_[ci]], bmod], [btmp], scale=mod[:, 16 + m:17 + m])
                    stt(x1[:, m, :], xt[:, m, :], ALPHA, tmp[:], ALU.mult, ALU.add, [bxt, btmp], [bx1])
            xo, bxo = XT.next()
            layer_norm_full(x1, bx1, "ln1_g", "ln1_b", T, lambda k, xo=xo: xo[:, k, :], bxo)
            dma("pool", xM[:, cs].rearrange("(k p) t -> p k t", p=128), xo[:], [bxo], [bxM])

        exchange_tail(xM, bxM)

        def ffn_tile(n):
            halo = n < 0
            N = HALO if halo else T
            if halo:
                xt, bxt = xhalo, bxhalo
            else:
                xt, bxt = XT.next()
                dma("pool", xt[:], xM[:, n * T:(n + 1) * T].rearrange("(k p) t -> p k t", p=128), [bxM], [bxt])
            ht, bht = modulate(xt, bxt, N, 24)
            vals = [None, None]
            for g in range(NUP // 4):
                banks = proj_group("w_up", g * 512, 512, ht, bht, N)
                for ci in range(4):
                    c = g * 4 + ci
                    if halo:
                        ts(uph[:, c, :], ps[:, banks[ci], 30:32], cst[:, 2:3], ALU.mult, [bps[banks[ci]], bcst], [buph])
                        continue
                    ub, bub = upb.next()
                    act(ub[:, 2:2 + T], ps[:, banks[ci], :], AF.Copy, [bps[banks[ci]]], [bub])
                    ts(ub[:, 0:2], uph[:, c, :], 1.0, ALU.mult, [buph], [bub], eng="pool")
                    ts(uph[:, c, :], ub[:, T:T + 2], 1.0, ALU.mult, [bub], [buph], eng="pool")
                    acc, bacc = F1.next()
                    for k in range(3):
                        wk = vcol("ffn_dw", c * 3 + k)
                        if k == 0:
                            ts(acc[:], ub[:, 0:T], wk, ALU.mult, [bub, bvecs], [bacc])
                        else:
                            stt(acc[:], ub[:, k:k + T], wk, acc[:], ALU.mult, ALU.add, [bub, bvecs], [bacc])
                    if ci < 2:
                        vals[ci] = (acc, bacc)
                    else:
                        i = g * 2 + ci - 2
                        sg, bsg = F1.next()
                        act(sg[:], acc[:], AF.Silu, [bacc], [bsg])
                        tt(ffin[:, i, :], sg[:], vals[ci - 2][0][:], ALU.mult, [bsg, vals[ci - 2][1]], [bffin[i]])
            if halo:
                return
            x2, bx2 = F8.next()
            for half in range(2):
                pb = []
                for ci in range(4):
                    pb.append(nps())
                kgs = [(0, 8), (8, 8), (16, 6)]
                wts = []
                for (k0, kn) in kgs:
                    wt, bw = wload("w_down", half * 512, 512, k0, kn)
                    wts.append((wt, bw, k0, kn))
                for ci in range(4):
                    pairs = []
                    rd_ = []
                    for (wt, bw, k0, kn) in wts:
                        for k in range(kn):
                            pairs.append((wt[:, k, ci * 128:(ci + 1) * 128], ffin[:, k0 + k, :]))
                        rd_.append(bw)
                    mm(ps[:, pb[ci], :], pairs, rd_ + bffin, [bps[pb[ci]]])
                    m = half * 4 + ci
                    tmp, btmp = F1.next()
                    act(tmp[:], ps[:, pb[ci], :], AF.Copy, [bps[pb[ci]], bmod], [btmp], scale=mod[:, 40 + m:41 + m])
                    stt(x2[:, m, :], xt[:, m, :], ALPHA, tmp[:], ALU.mult, ALU.add, [bxt, btmp], [bx2])
            xo, bxo = XT.next()
            layer_norm_full(x2, bx2, "ln2_g", "ln2_b", T, lambda k, xo=xo: xo[:, k, :], bxo)
            if l == DEPTH - 1:
                dma("pool", out_T[:, n * T:(n + 1) * T].rearrange("(k p) t -> p k t", p=128), xo[:], [bxo], [Buf("o")])
            else:
                dma("pool", xA[:, n * T:(n + 1) * T].rearrange("(k p) t -> p k t", p=128), xo[:], [bxo], [bxA])

        S.barrier()
        AR.reset()
        XT = Rot(nc, AR, "xt", [128, KD, T], F32, 2)
        HT = Rot(nc, AR, "ht", [128, KD, T], BF16, 1)
        WS = Rot(nc, AR, "ws", [128, KD, 512], BF16, 4)
        upb = Rot(nc, AR, "upb", [128, 2 + T], F32, 4)
        F1 = Rot(nc, AR, "f1", [128, T], F32, 8)
        F8 = Rot(nc, AR, "f8", [128, KD, T], F32, 1)
        B8 = Rot(nc, AR, "b8", [128, KD, T], BF16, 2)
        ffin = AR.alloc([128, NFF, T], BF16)
        ffn_tile(-1)
        for n in range(NT):
            ffn_tile(n)

    S.emit(nc, es)
    es.close()
    return nc


DEPTH_FULL = 4
WCOLS = 2048
WLAY = [("w_ada", D, 6 * D), ("w_in", D, NCH_IN * 128), ("w_conv_out", 512, D), ("w_sc_out", 512, D),
        ("w_uq", 256, 2048), ("w_ukv", 128, 1024), ("w_mla_out", 512, D), ("w_pool", 512, 128),
        ("w_pool_out", 512, D), ("w_o", D, D), ("w_up", D, 2 * DFF), ("w_down", DFF, D)]
WOFF = {}
_o = 0
for _n, _r, _c in WLAY:
    WOFF[_n] = (_o, _r, _c)
    _o += _r * _c
WTOT = _o


WCH = 256
WGROUPS = {}
for _n, _r, _c in WLAY:
    if _n == "w_in":
        _g = [(i * 512, 512) for i in range(6)] + [(3072, 128), (3200, 512)] + [(3712 + 512 * i, 512) for i in range(8)]
    elif _n == "w_ukv":
        _g = [(0, 1024)]
    elif _n == "w_pool":
        _g = [(0, 128)]
    else:
        _g = [(i * 512, 512) for i in range(_c // 512)]
    assert sum(w for _, w in _g) == _c, _n
    WGROUPS[_n] = _g
WSLOT = {}
_o = 0
for _n, _r, _c in WLAY:
    for (_c0, _w) in WGROUPS[_n]:
        WSLOT[(_n, _c0)] = (_o, _r // 128, _w)
        _o += _r * _w
assert _o == WTOT


def wrows(n_cores):
    per = n_cores * WCH * WCOLS
    return (WTOT + per - 1) // per * n_cores * WCH
DEBUG = False
DEBUG_NAMES = ("xM", "mg_d", "g2_d", "q_d", "at_d", "cc_d", "ss_d", "kv_dbg")
LAST = {}


def _pt(v, nchunk):
    return np.ascontiguousarray(np.asarray(v, np.float32).reshape(nchunk, 128).T)


def prep_weights(inp, DEPTH, n_cores):
    f = lambda a: np.ascontiguousarray(np.asarray(a, dtype=np.float32))
    w_in = f(inp["w_in"]); b_in = f(inp["b_in"])
    idx = np.zeros(NCH_IN * 128, np.int64)
    valid = np.zeros(NCH_IN * 128, bool)

    def put(chunk, cols, off=0):
        idx[chunk * 128 + off: chunk * 128 + off + len(cols)] = cols
        valid[chunk * 128 + off: chunk * 128 + off + len(cols)] = True
    put(0, np.arange(0, 512)); put(4, np.arange(512, 1024)); put(8, np.arange(1024, 1536))
    put(12, np.arange(1536, 2048)); put(16, np.arange(2048, 2560)); put(20, np.arange(2560, 2816))
    put(22, np.arange(2816, 2944))
    kr = np.arange(2944, 2976)
    put(23, kr, 64)
    put(24, np.concatenate([kr[16:], kr[:16]]), 64)
    put(25, np.arange(2976, 3488))
    put(29, np.arange(3488, 7584))
    w_inP = np.where(valid[None, None, :], w_in[:, :, idx], 0.0).astype(np.float32)
    b_inP = np.where(valid[None, :], b_in[:, idx], 0.0).astype(np.float32)
    w_uq = f(inp["w_uq"])
    hid = np.arange(768).reshape(NH, 96)
    sw = np.concatenate([hid[:, :64], hid[:, 80:96], hid[:, 64:80]], axis=1).reshape(-1)
    w_uq_sw = w_uq[:, :, sw]
    w_uqP = np.zeros((w_uq.shape[0], 256, 2048), np.float32)
    for h in range(NH):
        for v, src_ in enumerate((w_uq, w_uq_sw)):
            s4 = (h // 4) * 2 + v
            c0 = s4 * 512 + (h % 4) * 128
            w_uqP[:, :, c0:c0 + 96] = src_[:, :, h * 96:(h + 1) * 96]
    upidx = []
    for g in range(NUP // 4):
        for ci in range(4):
            ch = (2 * g + ci) if ci < 2 else (NFF + 2 * g + ci - 2)
            upidx.append(np.arange(ch * 128, (ch + 1) * 128))
    upidx = np.concatenate(upidx)
    w_upP = f(inp["w_up"])[:, :, upidx]
    ffn_dwP = f(inp["ffn_dw"])[:, :, upidx]
    vecs = np.zeros((DEPTH, 128, NV), np.float32)

    def setv(name, l, arr):
        vecs[l, :, VEC[name]:VEC[name] + arr.shape[1]] = arr
    for l in range(DEPTH):
        setv("b_ada", l, _pt(inp["b_ada"][l], 48))
        setv("b_in", l, _pt(b_inP[l], NCH_IN))
        setv("conv_dw", l, f(inp["conv_dw"][l]).T.reshape(4, 128, 31).transpose(1, 0, 2).reshape(128, 124))
        setv("cln_g", l, _pt(inp["conv_ln_g"][l], 4)); setv("cln_b", l, _pt(inp["conv_ln_b"][l], 4))
        setv("sc_dw", l, f(inp["sc_dw"][l]).T.reshape(4, 128, 3).transpose(1, 0, 2).reshape(128, 12))
        setv("qn_g", l, _pt(inp["q_norm_g"][l], 2)); setv("kvn_g", l, _pt(inp["kv_norm_g"][l], 1))
        setv("pool_scale", l, _pt(inp["pool_scale"][l], 4))
        setv("ln1_g", l, _pt(inp["ln1_g"][l], 8)); setv("ln1_b", l, _pt(inp["ln1_b"][l], 8))
        setv("ln2_g", l, _pt(inp["ln2_g"][l], 8)); setv("ln2_b", l, _pt(inp["ln2_b"][l], 8))
        setv("ffn_dw", l, ffn_dwP[l].T.reshape(NUP, 128, 3).transpose(1, 0, 2).reshape(128, NUP * 3))
    Wd = {"w_ada": f(inp["w_ada"]), "w_in": w_inP, "w_conv_out": f(inp["w_conv_out"]),
          "w_sc_out": f(inp["w_sc_out"]), "w_uq": w_uqP, "w_ukv": f(inp["w_ukv"]), "w_mla_out": f(inp["w_mla_out"]),
          "w_pool": f(inp["w_pool"]), "w_pool_out": f(inp["w_pool_out"]), "w_o": f(inp["w_o"]),
          "w_up": w_upP, "w_down": f(inp["w_down"])}
    RT = wrows(n_cores)
    blob = np.zeros((DEPTH, RT * WCOLS), np.float32)
    for n_, r_, c_ in WLAY:
        o_ = WOFF[n_][0]
        blob[:, o_:o_ + r_ * c_] = Wd[n_][:DEPTH].reshape(DEPTH, r_ * c_)
    return vecs, blob.reshape(DEPTH, RT, WCOLS)


def run_model(inp, NB, CPS, TPC, DEPTH, layers_per_launch=None):
    n_cores = NB * CPS
    lpl = layers_per_launch or DEPTH
    x = np.asarray(inp["x"], np.float32)
    c = np.asarray(inp["c"], np.float32)
    pos = np.asarray(inp["positions"], np.int32)
    vecs_h, blob_full = prep_weights(inp, DEPTH, n_cores)
    inv = (1.0 / (10000.0 ** (np.arange(0, 32, 2, dtype=np.float32) / np.float32(32)))).astype(np.float32)
    tri = (np.arange(128)[:, None] <= np.arange(128)[None, :]).astype(np.float32)
    shift = np.zeros((128, 64), np.float32)
    shift[64 + np.arange(64), np.arange(64)] = 1.0
    base_maps = []
    for core in range(n_cores):
        b, j = core // CPS, core % CPS
        sl = slice(j * TPC, (j + 1) * TPC)
        cst = np.zeros((128, 16), np.float32)
        cst[64:80, 0] = inv; cst[80:96, 0] = inv
        cst[64:80, 1] = -1.0; cst[80:96, 1] = 1.0
        cst[:, 2] = 1.0 if j > 0 else 0.0
        for r in range(3):
            cst[:, 3 + r] = 1.0 if r < j else 0.0
        for r in range(CPS):
            cst[:, 6 + r] = 1.0 if r == j - 1 else 0.0
        rc = np.zeros((128, 4, 16), np.float32)
        m = {"xT": np.ascontiguousarray(x[b, sl, :].T),
             "posr": np.ascontiguousarray(np.broadcast_to(pos[b, sl][None, :], (128, TPC))).astype(np.int32),
             "cT": _pt(c[b], KD), "cst": cst, "rcnt": rc.reshape(128, 64), "tri": tri, "shift": shift}
        base_maps.append(m)
    nc = build_program(CPS, TPC, lpl, n_cores, alpha_depth=DEPTH)
    res = None
    for l0 in range(0, DEPTH, lpl):
        in_maps = []
        for core in range(n_cores):
            m = dict(base_maps[core])
            if res is not None:
                m["xT"] = np.ascontiguousarray(res.results[core]["outT"])
            m["vecs"] = np.ascontiguousarray(vecs_h[l0:l0 + lpl])
            m["wfull"] = blob_full[l0:l0 + lpl]
            in_maps.append(m)
        res = run_bass_kernel_spmd(nc, in_maps, core_ids=list(range(n_cores)))
    if DEBUG:
        LAST["res"] = res.results
    out = np.zeros((NB, CPS * TPC, D), np.float32)
    for core in range(n_cores):
        b, j = core // CPS, core % CPS
        out[b, j * TPC:(j + 1) * TPC, :] = res.results[core]["outT"].T
    return out


LAYERS_PER_LAUNCH = 4


def kernel(**inputs):
    return run_model(inputs, NB=2, CPS=4, TPC=4096, DEPTH=4, layers_per_launch=LAYERS_PER_LAUNCH)
```

```python
import math
from contextlib import ExitStack
import numpy as np
import concourse.bass as bass
import concourse.mybir as mybir
from concourse.bass_utils import run_bass_kernel_spmd

F32 = mybir.dt.float32
BF16 = mybir.dt.bfloat16
I32 = mybir.dt.int32
AF = mybir.ActivationFunctionType
ALU = mybir.AluOpType

D = 1024
KD = 8
DFF = 2816
NUP = 44
NFF = 22
NH = 8
LN_EPS = 1e-5
RMS_EPS = 1e-6
NCH_IN = 61
HALO = 32
SCALE = 96.0 ** -0.5

VEC = {}
_o = 0
for _n, _w in [("b_ada", 48), ("b_in", NCH_IN), ("conv_dw", 4 * 31), ("cln_g", 4), ("cln_b", 4), ("sc_dw", 12),
               ("qn_g", 2), ("kvn_g", 1), ("pool_scale", 4), ("ln1_g", 8), ("ln1_b", 8), ("ln2_g", 8),
               ("ln2_b", 8), ("ffn_dw", NUP * 3)]:
    VEC[_n] = _o
    _o += _w
NV = _o


class Buf:
    __slots__ = ("name", "last_w", "readers")

    def __init__(self, name):
        self.name = name
        self.last_w = None
        self.readers = []


class Op:
    __slots__ = ("eng", "fn", "deps", "dma", "needs_inc", "sem", "val", "prev_same_sem")

    def __init__(self, eng, fn, deps, dma):
        self.eng, self.fn, self.deps, self.dma = eng, fn, deps, dma
        self.needs_inc = False
        self.sem = None
        self.val = 0
        self.prev_same_sem = None


ENGS = ("pe", "act", "dve", "pool", "sp")
EPOCH = 4000
NDMASEM = 12


class Sched:
    def __init__(self, same_engine_sync=False):
        self.ops = []
        self.same = same_engine_sync
        self.pending_bar = {}

    def add(self, eng, fn, reads=(), writes=(), dma=False):
        idx = len(self.ops)
        deps = set()
        for b in reads:
            if b.last_w is not None:
                deps.add(b.last_w)
        for b in writes:
            if b.last_w is not None:
                deps.add(b.last_w)
            deps.update(b.readers)
        for b in reads:
            b.readers.append(idx)
        for b in writes:
            b.last_w = idx
            b.readers = []
        keep = []
        for d in deps:
            od = self.ops[d]
            if (not od.dma) and od.eng == eng and not dma:
                if eng == "pe" or not self.same:
                    continue
            keep.append(d)
        if eng in self.pending_bar:
            keep = sorted(set(keep) | set(self.pending_bar.pop(eng)))
        self.ops.append(Op(eng, fn, sorted(keep), dma))
        return idx

    def barrier(self):
        deps = []
        for e in ENGS:
            comp = [i for i in range(len(self.ops) - 1, -1, -1) if self.ops[i].eng == e and not self.ops[i].dma][:1]
            deps += comp
            dm = [i for i in range(len(self.ops) - 1, max(-1, len(self.ops) - 4000), -1)
                  if self.ops[i].eng == e and self.ops[i].dma][:NDMASEM]
            deps += dm
        self.pending_bar = {e: sorted(set(deps)) for e in ENGS}

    def emit(self, nc, es):
        ops = self.ops
        for op in ops:
            for d in op.deps:
                ops[d].needs_inc = True
        cnt = {e: 0 for e in ENGS}
        dcnt = {e: 0 for e in ENGS}
        sems = {}

        def getsem(key):
            if key not in sems:
                sems[key] = es.enter_context(nc.semaphore("s_%s_%s_%d" % key))
            return sems[key]

        last_on_dsem = {}
        for i, op in enumerate(ops):
            if op.dma:
                k = dcnt[op.eng]
                dcnt[op.eng] += 1
                key = ("d", op.eng, k % NDMASEM)
                op.sem = key
                op.val = 16 * (k // NDMASEM + 1)
                op.prev_same_sem = last_on_dsem.get(key)
                last_on_dsem[key] = i
            elif op.needs_inc:
                c = cnt[op.eng]
                cnt[op.eng] += 1
                op.sem = ("c", op.eng, c // EPOCH)
                op.val = c % EPOCH + 1
        for op in ops:
            if op.sem is not None:
                getsem(op.sem)
        import os as _os
        if _os.environ.get("KSTATS"):
            print("KSTATS ops", {e: sum(1 for o in ops if o.eng == e) for e in ENGS}, "incs", cnt, "dmas", dcnt,
                  "nsems", len(sems), flush=True)
        block = es.enter_context(nc.Block())
        streams = {e: [i for i, op in enumerate(ops) if op.eng == e] for e in ENGS}

        def run(eng_name, eng):
            known = {}
            for i in streams[eng_name]:
                op = ops[i]
                waits = [(ops[d].sem, ops[d].val) for d in op.deps]
                if op.dma and op.prev_same_sem is not None:
                    p = ops[op.prev_same_sem]
                    waits.append((p.sem, p.val))
                for key, val in waits:
                    if known.get(key, 0) < val:
                        eng.wait_ge(sems[key], val)
                        known[key] = val
                ins = op.fn(eng)
                if op.dma:
                    ins.then_inc(sems[op.sem], 16)
                elif op.needs_inc:
                    ins.then_inc(sems[op.sem], 1)
            for key, s in sems.items():
                if key[0] == "d" and key[1] == eng_name:
                    li = last_on_dsem[key]
                    if known.get(key, 0) < ops[li].val:
                        eng.wait_ge(s, ops[li].val)

        @block.tensor
        def _(e):
            run("pe", e)

        @block.scalar
        def _(e):
            run("act", e)

        @block.vector
        def _(e):
            run("dve", e)

        @block.gpsimd
        def _(e):
            run("pool", e)

        @block.sync
        def _(e):
            run("sp", e)


class Arena:
    def __init__(self, nc, es, nbytes, name="arena"):
        self.t = es.enter_context(nc.sbuf_tensor(name, [128, nbytes // 4], F32))
        self.off = 0
        self.cap = nbytes
        self.peak = 0

    def reset(self):
        self.off = 0

    def alloc(self, shape, dt):
        free = 1
        for s in shape[1:]:
            free *= s
        esz = 4 if dt in (F32, I32) else 2
        nb = (free * esz + 31) // 32 * 32
        assert self.off + nb <= self.cap, ("arena overflow", self.off, nb, self.cap)
        ap = self.t[:, self.off // 4:(self.off + nb) // 4]
        if dt != F32:
            ap = ap.bitcast(dt)
        ap = ap[:, 0:free]
        self.off += nb
        self.peak = max(self.peak, self.off)
        if len(shape) == 3:
            ap = ap.rearrange("p (a b) -> p a b", b=shape[2])
        elif len(shape) == 4:
            ap = ap.rearrange("p (a b c) -> p a b c", b=shape[2], c=shape[3])
        return ap


class Rot:
    def __init__(self, nc, es, name, shape, dt, n):
        self.t = [es.alloc(shape, dt) for i in range(n)]
        self.b = [Buf("%s%d" % (name, i)) for i in range(n)]
        self.i = 0

    def next(self):
        k = self.i % len(self.t)
        self.i += 1
        return self.t[k], self.b[k]


def build_program(CPS, TPC, DEPTH, n_cores, alpha_depth=None):
    T = 512
    NT = TPC // T
    NKC = CPS
    NK = NKC * TPC
    NKB = NK // 128
    groups = [list(range(g * CPS, (g + 1) * CPS)) for g in range(n_cores // CPS)]
    nc = bass.Bass("TRN2", target_bir_lowering=False)
    S = Sched()
    es = ExitStack()

    def din(name, shape, dt=F32):
        return nc.dram_tensor(name, list(shape), dt, kind="ExternalInput").ap()

    def dint(name, shape, dt=F32):
        if DEBUG and name in DEBUG_NAMES:
            return nc.dram_tensor(name, list(shape), dt, kind="ExternalOutput").ap()
        return nc.dram_tensor(name, list(shape), dt, kind="Internal").ap()

    xT_in = din("xT", [D, TPC])
    pos_in = din("posr", [128, TPC], I32)
    cT_in = din("cT", [128, KD])
    cst_in = din("cst", [128, 16])
    rcnt_in = din("rcnt", [128, 4 * 16])
    tri_in = din("tri", [128, 128])
    shift_in = din("shift", [128, 64])
    RT = wrows(n_cores)
    RSH = RT // n_cores
    vecs_in = din("vecs", [DEPTH, 128, NV])
    wfull_in = din("wfull", [DEPTH, RT, WCOLS])
    wfull = [dint("wbf%d" % i, [RT, WCOLS], BF16) for i in range(DEPTH)]
    bwsrc = [Buf("wsrc%d" % i) for i in range(DEPTH)]
    bwfull = [Buf("wfull%d" % i) for i in range(DEPTH)]
    curl = [0]

    def Wl(name):
        off, r, c = WOFF[name]
        return wfull[curl[0]].rearrange("r c -> (r c)")[off:off + r * c].rearrange("(r c) -> r c", c=c)

    def wslot(name, c0, layer=None):
        o_, kp_, w_ = WSLOT[(name, c0)]
        li = curl[0] if layer is None else layer
        return wfull[li].rearrange("r c -> (r c)")[o_:o_ + 128 * kp_ * w_].rearrange(
            "(p k c) -> p k c", p=128, k=kp_, c=w_)

    def gather_weights(i, part=0, nparts=1):
        keys = list(WSLOT.keys())
        per = (len(keys) + nparts - 1) // nparts
        for (name, c0) in keys[part * per:(part + 1) * per]:
            o_, kp_, w_ = WSLOT[(name, c0)]
            mo_, r_, c_ = WOFF[name]
            srcm = wfull_in[i].rearrange("r c -> (r c)")[mo_:mo_ + r_ * c_].rearrange("(r c) -> r c", c=c_)
            dma("pool", wslot(name, c0, i), srcm[:, c0:c0 + w_].rearrange("(k p) c -> p k c", p=128),
                [], [bwfull[i]])

    out_T = nc.dram_tensor("outT", [D, TPC], F32, kind="ExternalOutput").ap()

    xA = dint("xA", [D, TPC]); bxA = Buf("xA")
    xM = dint("xM", [D, TPC]); bxM = Buf("xM")
    tail_src = dint("tail_src", [D, HALO]); btail_src = Buf("tail_src")
    tail_all = dint("tail_all", [CPS * D, HALO]); btail_all = Buf("tail_all")
    kv_src = dint("kv_src", [NT * 160, T], BF16); bkv_src = [Buf("kv_src%d" % i) for i in range(NT)]
    kv_all = dint("kv_all", [NT * CPS * 160, T], BF16); bkv_all = [Buf("kv_all%d" % i) for i in range(NT)]
    mg_d = dint("mg_d", [D, TPC]); bmg = Buf("mg_d")
    g2_d = dint("g2_d", [D, TPC], BF16); bg2 = Buf("g2_d")
    q_d = dint("q_d", [NH * 96, TPC], BF16); bq = Buf("q_d")
    at_d = dint("at_d", [NH * 64, TPC], BF16); bat = Buf("at_d")
    cc_d = dint("cc_d", [32, TPC]); bcc = Buf("cc_d")
    ss_d = dint("ss_d", [32, TPC]); bss = Buf("ss_d")

    def sb(name, shape, dt=F32):
        return es.enter_context(nc.sbuf_tensor("sb_" + name, list(shape), dt))

    cst = sb("cst", [128, 16]); bcst = Buf("cst")
    rcnt = sb("rcnt", [128, 4, 16])
    tri = sb("tri", [128, 128], BF16)
    shiftm = sb("shiftm", [128, 64])
    epsc = sb("epsc", [128, 4]); bconst = Buf("const")
    ones = sb("ones", [128, 3, 128], BF16)
    ones1k = sb("ones1k", [128, 128], BF16)
    cact = sb("cact", [128, KD], BF16); bcact = Buf("cact")
    mod = sb("mod", [128, 48]); bmod = Buf("mod")
    vecs = sb("vecs", [128, NV]); bvecs = Buf("vecs")
    ps = es.enter_context(nc.psum_tensor("ps", [128, 8, 512], F32))
    bps = [Buf("ps%d" % i) for i in range(8)]
    psi = [0]

    def nps():
        k = psi[0] % 8
        psi[0] += 1
        return k

    AR = Arena(nc, es, 168 * 1024)
    ST = Rot(nc, Arena(nc, es, 8 * 1024, "arena_st"), "st", [128, T], F32, 4)
    rc2 = sb("rc2", [128, 64]); brc2 = Buf("rc2")
    uph = sb("uph", [128, NUP, 2]); buph = Buf("uph")
    tailsb = sb("tailsb", [128, CPS, KD, HALO]); btailsb = Buf("tailsb")
    xhalo = sb("xhalo", [128, KD, HALO]); bxhalo = Buf("xhalo")
    wukv = sb("wukv", [128, 1024], BF16); bwukv = Buf("wukv")
    ALPHA = (2.0 * (alpha_depth or DEPTH)) ** 0.25
    XT = HT = WS = F1 = B1 = F4 = B4 = B8 = F8 = tabs = upb = PW = qh = PT = None
    bufA = bufB = bufP = merged = ffin = kvnT = kT = vaug = None
    bbufA = [Buf("bufA%d" % j) for j in range(4)]
    bbufB = [Buf("bufB%d" % j) for j in range(4)]
    bbufP = [Buf("bufP%d" % j) for j in range(4)]
    bmerged = [Buf("mg%d" % m) for m in range(KD)]
    bffin = [Buf("ffin%d" % i) for i in range(NFF)]
    bkvnT = [Buf("kvnT%d" % r) for r in range(NKC)]
    bkTn = Buf("kTn"); bkTr = Buf("kTr"); bvv = Buf("vaug_v"); bvo = Buf("vaug_o")

    P = lambda a: a

    def dma(q, out, in_, reads, writes):
        S.add(q, lambda e, o=out, i=in_: e.dma_start(out=o, in_=i), reads, writes, dma=True)

    def act(out, in_, func, reads, writes, scale=None, bias=None):
        kw = {}
        if scale is not None:
            kw["scale"] = scale
        if bias is not None:
            kw["bias"] = bias
        S.add("act", lambda e, o=out, i=in_, f=func, kw=kw: e.activation(out=o, in_=i, func=f, **kw), reads, writes)

    def tt(out, in0, in1, op, reads, writes, eng="dve"):
        S.add(eng, lambda e, o=out, a=in0, b=in1, p=op: e.tensor_tensor(out=o, in0=a, in1=b, op=p), reads, writes)

    def ts(out, in0, s1, op0, reads, writes, s2=None, op1=None, eng="dve"):
        if op1 is None:
            S.add(eng, lambda e, o=out, a=in0, s=s1, p=op0: e.tensor_scalar(out=o, in0=a, scalar1=s, scalar2=None, op0=p),
                  reads, writes)
        else:
            S.add(eng, lambda e, o=out, a=in0, s=s1, p=op0, s_2=s2, p1=op1: e.tensor_scalar(
                out=o, in0=a, scalar1=s, scalar2=s_2, op0=p, op1=p1), reads, writes)

    def stt(out, in0, scalar, in1, op0, op1, reads, writes):
        S.add("dve", lambda e, o=out, a=in0, s=scalar, b=in1, p0=op0, p1=op1: e.scalar_tensor_tensor(
            out=o, in0=a, scalar=s, in1=b, op0=p0, op1=p1), reads, writes)

    def mm(out, pairs, reads, writes):
        def fn(e, out=out, pairs=pairs):
            n = len(pairs)
            ins = None
            for i, (l, r) in enumerate(pairs):
                ins = e.matmul(out, lhsT=l, rhs=r, start=(i == 0), stop=(i == n - 1))
            return ins
        S.add("pe", fn, reads, writes)

    def vcol(name, i=0, n=1):
        o = VEC[name] + i
        return vecs[:, o:o + n]

    def wload(name, c0, ncols, k0=0, kn=None):
        o_, kp_, w_ = WSLOT[(name, c0)]
        assert w_ == ncols, (name, c0, ncols, w_)
        kn = kp_ if kn is None else kn
        t, b = WS.next()
        dma("sp", t[:, 0:kn, 0:ncols], wslot(name, c0)[:, k0:k0 + kn, :], [bwfull[curl[0]]], [b])
        return t, b

    S.add("sp", lambda e: e.dma_start(out=cst[:], in_=cst_in), [], [bcst], dma=True)
    S.add("sp", lambda e: e.dma_start(out=rcnt[:].rearrange("p a b -> p (a b)"), in_=rcnt_in), [], [bconst], dma=True)
    S.add("pool", lambda e: e.dma_start(out=tri[:], in_=tri_in), [], [bconst], dma=True)
    S.add("sp", lambda e: e.dma_start(out=shiftm[:], in_=shift_in), [], [bconst], dma=True)
    for col, v in enumerate([LN_EPS, RMS_EPS, 0.0, 1.0]):
        S.add("dve", lambda e, c=col, v=v: e.memset(epsc[:, c:c + 1], v), [], [bconst])
    for k, v in enumerate([1.0 / 128, 1.0 / 256, 1.0 / 512]):
        S.add("dve", lambda e, k=k, v=v: e.memset(ones[:, k, :], v), [], [bconst])
    S.add("dve", lambda e: e.memset(ones1k[:], 1.0 / 1024), [], [bconst])
    for g_, w_ in enumerate((2, 4, 8, 16)):
        for t_ in range(w_ - 1):
            A_ = 1.0 / (t_ + 1)
            B_ = 1.0 / w_ - A_
            ts(rc2[:, g_ * 16 + t_:g_ * 16 + t_ + 1], cst[:, 2:3], B_, ALU.mult, [bcst], [brc2], s2=A_, op1=ALU.add)
    eps_ln = epsc[:, 0:1]
    eps_rms = epsc[:, 1:2]

    ctmp = sb("ctmp", [128, KD])
    S.add("sp", lambda e: e.dma_start(out=ctmp[:], in_=cT_in), [], [bcact], dma=True)
    act(cact[:], ctmp[:], AF.Silu, [bcact], [bcact])

    R = slice(64, 96)
    TWO_PI = 2.0 * math.pi
    PI_HI = 6.28125
    PI_LO = TWO_PI - PI_HI
    PI_SAFE = 3.14159
    AR.reset()
    posi = AR.alloc([128, T], I32)
    bro = Buf("ropetmp")
    ang = AR.alloc([128, T], F32); kf = AR.alloc([128, T], F32); ki = AR.alloc([128, T], I32)
    rr = AR.alloc([128, T], F32); mk = AR.alloc([128, T], F32); sn = AR.alloc([128, T], F32)
    for n in range(NT):
        cs = slice(n * T, (n + 1) * T)
        dma("sp", posi[R, :], pos_in[64:96, cs], [], [bro])
        S.add("dve", lambda e: e.tensor_copy(out=ang[R, :], in_=posi[R, :]), [bro], [bro])
        ts(ang[R, :], ang[R, :], cst[R, 0:1], ALU.mult, [bro, bcst], [bro])
        ts(kf[R, :], ang[R, :], 1.0 / TWO_PI, ALU.mult, [bro], [bro])
        S.add("dve", lambda e: e.tensor_copy(out=ki[R, :], in_=kf[R, :]), [bro], [bro])
        S.add("dve", lambda e: e.tensor_copy(out=kf[R, :], in_=ki[R, :]), [bro], [bro])
        stt(rr[R, :], kf[R, :], -PI_HI, ang[R, :], ALU.mult, ALU.add, [bro], [bro])
        stt(rr[R, :], kf[R, :], -PI_LO, rr[R, :], ALU.mult, ALU.add, [bro], [bro])

        def wrap(dst, src):
            ts(mk[R, :], src, math.pi, ALU.is_gt, [bro], [bro])
            stt(dst, mk[R, :], -TWO_PI, src, ALU.mult, ALU.add, [bro], [bro])
            ts(mk[R, :], dst, -math.pi, ALU.is_lt, [bro], [bro])
            stt(dst, mk[R, :], TWO_PI, dst, ALU.mult, ALU.add, [bro], [bro])
            ts(dst, dst, -PI_SAFE, ALU.max, [bro], [bro], s2=PI_SAFE, op1=ALU.min)
        wrap(rr[R, :], rr[R, :])
        act(sn[R, :], rr[R, :], AF.Sin, [bro], [bro])
        ts(sn[R, :], sn[R, :], cst[R, 1:2], ALU.mult, [bro, bcst], [bro])
        dma("sp", ss_d[:, cs], sn[R, :], [bro], [bss])
        ts(ang[R, :], rr[R, :], math.pi / 2, ALU.add, [bro], [bro])
        wrap(ang[R, :], ang[R, :])
        act(kf[R, :], ang[R, :], AF.Sin, [bro], [bro])
        dma("sp", cc_d[:, cs], kf[R, :], [bro], [bcc])

    for k in range(KD):
        dma("sp", xA[k * 128:(k + 1) * 128, :], xT_in[k * 128:(k + 1) * 128, :], [], [bxA])

    def exchange_tail(xsrc, bxsrc):
        dma("pool", tail_src, xsrc[:, TPC - HALO:TPC], [bxsrc], [btail_src])
        S.add("pool", lambda e: e.collective_compute("AllGather", ALU.bypass, replica_groups=groups,
                                                     ins=[tail_src], outs=[tail_all]),
              [btail_src], [btail_all])
        dma("pool", tailsb[:].rearrange("p r k h -> p (r k) h"),
            tail_all.rearrange("(rk p) h -> p rk h", p=128), [btail_all], [btailsb])
        xh = xhalo[:].rearrange("p k h -> p (k h)")
        for r in range(CPS):
            src = tailsb[:, r].rearrange("p k h -> p (k h)")
            if r == 0:
                ts(xh, src, cst[:, 6:7], ALU.mult, [btailsb, bcst], [bxhalo])
            else:
                stt(xh, src, cst[:, 6 + r:7 + r], xh, ALU.mult, ALU.add, [btailsb, bcst], [bxhalo])

    def modulate(xt, bxt, N, off):
        ht, bht = HT.next()
        for k in range(KD):
            act(ht[:, k, 0:N], xt[:, k, 0:N], AF.Identity, [bxt, bmod], [bht],
                scale=mod[:, off + 8 + k:off + 9 + k], bias=mod[:, off + k:off + k + 1])
        return ht, bht

    def proj_group(wsrc, c0, ncols, ht, bht, N, kparts=KD):
        wt, bw = wload(wsrc, c0, ncols)
        banks = []
        nchunks = (ncols + 127) // 128
        for ci in range(nchunks):
            mcols = min(128, ncols - ci * 128)
            b = nps()
            mm(ps[0:mcols, b, 0:N], [(wt[:, k, ci * 128:ci * 128 + mcols], ht[:, k, 0:N]) for k in range(kparts)],
               [bw, bht], [bps[b]])
            banks.append(b)
        return banks

    def stats(src_bf, bsrc, sq_bf, bsq, nchunk, ones_ap, N):
        bm = nps()
        mm(ps[:, bm, 0:N], [(ones_ap, src_bf[:, j, 0:N]) for j in range(nchunk)], [bsrc, bconst], [bps[bm]])
        be = nps()
        mm(ps[:, be, 0:N], [(ones_ap, sq_bf[:, j, 0:N]) for j in range(nchunk)], [bsq, bconst], [bps[be]])
        return bm, be

    def ln_finish(bm, be, N, eps_ap):
        mean, bmean = ST.next()
        act(mean[:, 0:N], ps[:, bm, 0:N], AF.Copy, [bps[bm]], [bmean])
        var, bvar = ST.next()
        tt(var[:, 0:N], mean[:, 0:N], mean[:, 0:N], ALU.mult, [bmean], [bvar])
        tt(var[:, 0:N], ps[:, be, 0:N], var[:, 0:N], ALU.subtract, [bps[be], bvar], [bvar])
        ts(var[:, 0:N], var[:, 0:N], 0.0, ALU.max, [bvar], [bvar])
        act(var[:, 0:N], var[:, 0:N], AF.Sqrt, [bvar, bconst], [bvar], bias=eps_ap)
        S.add("dve", lambda e, v=var, N=N: e.reciprocal(out=v[:, 0:N], in_=v[:, 0:N]), [bvar], [bvar])
        return mean, bmean, var, bvar

    def layer_norm_full(xin, bxin, gname, bname, N, dst_ap_fn, bdst):
        xb, bxb = B8.next()
        sq, bsq = B8.next()
        for k in range(KD):
            act(xb[:, k, 0:N], xin[:, k, 0:N], AF.Copy, [bxin], [bxb])
            act(sq[:, k, 0:N], xin[:, k, 0:N], AF.Square, [bxin], [bsq])
        bm, be = stats(xb, bxb, sq, bsq, KD, ones1k[:], N)
        mean, bmean, rstd, brstd = ln_finish(bm, be, N, eps_ln)
        for k in range(KD):
            d, bd = F1.next()
            tt(d[:, 0:N], xin[:, k, 0:N], mean[:, 0:N], ALU.subtract, [bxin, bmean], [bd])
            tt(d[:, 0:N], d[:, 0:N], rstd[:, 0:N], ALU.mult, [bd, brstd], [bd])
            act(dst_ap_fn(k), d[:, 0:N], AF.Identity, [bd, bvecs], [bdst], scale=vcol(gname, k), bias=vcol(bname, k))

    for l in range(DEPTH):
        S.barrier()
        AR.reset()
        XT = Rot(nc, AR, "xt", [128, KD, T], F32, 1)
        HT = Rot(nc, AR, "ht", [128, KD, T], BF16, 1)
        WS = Rot(nc, AR, "ws", [128, KD, 512], BF16, 3)
        F1 = Rot(nc, AR, "f1", [128, T], F32, 6)
        B1 = Rot(nc, AR, "b1", [128, T], BF16, 3)
        F4 = Rot(nc, AR, "f4", [128, 4, T], F32, 2)
        B4 = Rot(nc, AR, "b4", [128, 4, T], BF16, 5)
        B8 = Rot(nc, AR, "b8", [128, KD, T], BF16, 1)
        tabs = Rot(nc, AR, "tabs", [128, 2, T], F32, 1)
        PW = Rot(nc, AR, "pw", [128, 16 + T], F32, 3)
        bufA = AR.alloc([128, 4, 30 + T], F32)
        bufB = AR.alloc([128, 4, 2 + T], F32)
        bufP = AR.alloc([128, 4, 16 + T], F32)
        merged = AR.alloc([128, KD, T], F32)
        curl[0] = l
        if l == 0:
            gather_weights(0)
        dma("sp", vecs[:], vecs_in[l], [], [bvecs])
        for g6 in range(6):
            for half in range(2):
                wt, bw = wload("w_ada", g6 * D + half * 512, 512)
                for ci in range(4):
                    mcol = g6 * 8 + half * 4 + ci
                    b = nps()
                    mm(ps[:, b, 0:1], [(wt[:, k, ci * 128:(ci + 1) * 128], cact[:, k:k + 1]) for k in range(KD)],
                       [bw, bcact], [bps[b]])
                    act(mod[:, mcol:mcol + 1], ps[:, b, 0:1], AF.Identity, [bps[b], bvecs], [bmod],
                        bias=vcol("b_ada", mcol))
        for g6 in (1, 2, 4, 5):
            ts(mod[:, g6 * 8:(g6 + 1) * 8], mod[:, g6 * 8:(g6 + 1) * 8], 1.0, ALU.add, [bmod], [bmod])

        exchange_tail(xA, bxA)

        wi = "w_in"

        def phase1_tile(n):
            halo = n < 0
            N = HALO if halo else T
            if halo:
                xt, bxt = xhalo, bxhalo
            else:
                xt, bxt = XT.next()
                dma("pool", xt[:], xA[:, n * T:(n + 1) * T].rearrange("(k p) t -> p k t", p=128), [bxA], [bxt])
            ht, bht = modulate(xt, bxt, N, 0)

            bA = proj_group(wi, 0, 512, ht, bht, N)
            bB = proj_group(wi, 512, 512, ht, bht, N)
            for j in range(4):
                sg, bsg = F1.next()
                act(sg[:, 0:N], ps[:, bB[j], 0:N], AF.Sigmoid, [bps[bB[j]], bvecs], [bsg], bias=vcol("b_in", 4 + j))
                if halo:
                    tmp, btmp = F1.next()
                    stt(tmp[:, 0:N], ps[:, bA[j], 0:N], vcol("b_in", j), sg[:, 0:N], ALU.add, ALU.mult,
                        [bps[bA[j]], bsg, bvecs], [btmp])
                    ts(bufA[:, j, 0:30], tmp[:, 2:32], cst[:, 2:3], ALU.mult, [btmp, bcst], [bbufA[j]])
                else:
                    stt(bufA[:, j, 30:30 + T], ps[:, bA[j], 0:T], vcol("b_in", j), sg[:, 0:T], ALU.add, ALU.mult,
                        [bps[bA[j]], bsg, bvecs], [bbufA[j]])
            bC = proj_group(wi, 12 * 128, 512, ht, bht, N)
            bX = proj_group(wi, 16 * 128, 512, ht, bht, N)
            for j in range(4):
                cs_, bcs = F1.next()
                act(cs_[:, 0:N], ps[:, bC[j], 0:N], AF.Identity, [bps[bC[j]], bvecs], [bcs], bias=vcol("b_in", 12 + j))
                if halo:
                    tmp, btmp = F1.next()
                    stt(tmp[:, 0:N], ps[:, bX[j], 0:N], vcol("b_in", 16 + j), cs_[:, 0:N], ALU.add, ALU.mult,
                        [bps[bX[j]], bcs, bvecs], [btmp])
                    ts(bufB[:, j, 0:2], tmp[:, 30:32], cst[:, 2:3], ALU.mult, [btmp, bcst], [bbufB[j]])
                else:
                    stt(bufB[:, j, 2:2 + T], ps[:, bX[j], 0:T], vcol("b_in", 16 + j), cs_[:, 0:T], ALU.add, ALU.mult,
                        [bps[bX[j]], bcs, bvecs], [bbufB[j]])
            bP = proj_group(wi, 25 * 128, 512, ht, bht, N)
            for j in range(4):
                if halo:
                    tmp, btmp = F1.next()
                    act(tmp[:, 0:N], ps[:, bP[j], 0:N], AF.Identity, [bps[bP[j]], bvecs], [btmp], bias=vcol("b_in", 25 + j))
                    ts(bufP[:, j, 0:16], tmp[:, 16:32], cst[:, 2:3], ALU.mult, [btmp, bcst], [bbufP[j]])
                else:
                    act(bufP[:, j, 16:16 + T], ps[:, bP[j], 0:T], AF.Identity, [bps[bP[j]], bvecs], [bbufP[j]],
                        bias=vcol("b_in", 25 + j))
            if halo:
                return

            g2t, bg2t = B8.next()
            for half in range(2):
                bg_ = proj_group(wi, (29 + 2 * 8 + half * 4) * 128, 512, ht, bht, T)
                for ci in range(4):
                    m = half * 4 + ci
                    act(g2t[:, m, :], ps[:, bg_[ci], :], AF.Sigmoid, [bps[bg_[ci]], bvecs], [bg2t],
                        bias=vcol("b_in", 29 + 16 + m))
            dma("pool", g2_d[:, n * T:(n + 1) * T].rearrange("(k p) t -> p k t", p=128), g2t[:], [bg2t], [bg2])

            tb, btb = tabs.next()
            dma("pool", tb[64:96, 0, :], cc_d[:, n * T:(n + 1) * T], [bcc], [btb])
            dma("pool", tb[64:96, 1, :], ss_d[:, n * T:(n + 1) * T], [bss], [btb])
            bQ = proj_group(wi, 20 * 128, 512, ht, bht, T)
            bK2 = proj_group(wi, 24 * 128, 128, ht, bht, T)
            ql, bql = F4.next()
            qsq, bqsq = B4.next()
            for j in range(2):
                act(ql[:, j, :], ps[:, bQ[j], :], AF.Identity, [bps[bQ[j]], bvecs], [bql], bias=vcol("b_in", 20 + j))
                act(qsq[:, j, :], ql[:, j, :], AF.Square, [bql], [bqsq])
            act(ql[:, 2, :], ps[:, bQ[2], :], AF.Identity, [bps[bQ[2]], bvecs], [bql], bias=vcol("b_in", 22))
            act(qsq[:, 2, :], ql[:, 2, :], AF.Square, [bql], [bqsq])
            bqs = nps()
            mm(ps[:, bqs, :], [(ones[:, 1, :], qsq[:, j, :]) for j in range(2)], [bqsq, bconst], [bps[bqs]])
            bks = nps()
            mm(ps[:, bks, :], [(ones[:, 0, :], qsq[:, 2, :])], [bqsq, bconst], [bps[bks]])
            qn, bqn = B4.next()
            for (bb, js, gname) in ((bqs, (0, 1), "qn_g"), (bks, (2,), "kvn_g")):
                rs, brs = F1.next()
                act(rs[:], ps[:, bb, :], AF.Sqrt, [bps[bb], bconst], [brs], bias=eps_rms)
                S.add("dve", lambda e, v=rs: e.reciprocal(out=v[:], in_=v[:]), [brs], [brs])
                for j in js:
                    stt(qn[:, j, :], ql[:, j, :], vcol(gname, j if gname == "qn_g" else 0), rs[:], ALU.mult, ALU.mult,
                        [bql, brs, bvecs], [bqn])
            dma("pool", kv_src[n * 160:n * 160 + 128, :], qn[:, 2, :], [bqn], [bkv_src[n]])
            kr, bkr = F1.next()
            kr2, bkr2 = F1.next()
            stt(kr[R, :], ps[R, bQ[3], :], vecs[R, VEC["b_in"] + 23:VEC["b_in"] + 24], tb[R, 0, :], ALU.add, ALU.mult,
                [bps[bQ[3]], btb, bvecs], [bkr])
            stt(kr2[R, :], ps[R, bK2[0], :], vecs[R, VEC["b_in"] + 24:VEC["b_in"] + 25], tb[R, 1, :], ALU.add, ALU.mult,
                [bps[bK2[0]], btb, bvecs], [bkr2])
            krb, bkrb = B1.next()
            tt(krb[R, :], kr[R, :], kr2[R, :], ALU.add, [bkr, bkr2], [bkrb])
            dma("pool", kv_src[n * 160 + 128:(n + 1) * 160, :], krb[R, :], [bkrb], [bkv_src[n]])
            S.add("pool", lambda e, n=n: e.collective_compute(
                "AllGather", ALU.bypass, replica_groups=groups, ins=[kv_src[n * 160:(n + 1) * 160, :]],
                outs=[kv_all[n * CPS * 160:(n + 1) * CPS * 160, :]]), [bkv_src[n]], [bkv_all[n]])
            wqs = []
            for s4 in range(4):
                wq_, bwq_ = WS.next()
                dma("sp", wq_[:, 0:2, 0:512], wslot("w_uq", s4 * 512), [bwfull[l]], [bwq_])
                outs4 = []
                for h4 in range(4):
                    b = nps()
                    mm(ps[0:96, b, :], [(wq_[:, k, h4 * 128:h4 * 128 + 96], qn[:, k, :]) for k in range(2)],
                       [bwq_, bqn], [bps[b]])
                    outs4.append(b)
                wqs.append(outs4)
                if s4 % 2 == 0:
                    continue
                for h4 in range(4):
                    h = (s4 // 2) * 4 + h4
                    outs = [wqs[s4 - 1][h4], wqs[s4][h4]]
                    qo, bqo = B1.next()
                    act(qo[0:64, :], ps[0:64, outs[0], :], AF.Copy, [bps[outs[0]]], [bqo])
                    t1, bt1 = F1.next()
                    t2, bt2 = F1.next()
                    tt(t1[R, :], ps[R, outs[0], :], tb[R, 0, :], ALU.mult, [bps[outs[0]], btb], [bt1])
                    tt(t2[R, :], ps[R, outs[1], :], tb[R, 1, :], ALU.mult, [bps[outs[1]], btb], [bt2])
                    tt(qo[R, :], t1[R, :], t2[R, :], ALU.add, [bt1, bt2], [bqo])
                    dma("pool", q_d[h * 96:(h + 1) * 96, n * T:(n + 1) * T], qo[0:96, :], [bqo], [bq])

            accA, baccA = F4.next()
            for j in range(4):
                for k in range(31):
                    wk = vcol("conv_dw", j * 31 + k)
                    if k == 0:
                        ts(accA[:, j, :], bufA[:, j, 0:T], wk, ALU.mult, [bbufA[j], bvecs], [baccA])
                    else:
                        stt(accA[:, j, :], bufA[:, j, k:k + T], wk, accA[:, j, :], ALU.mult, ALU.add,
                            [bbufA[j], bvecs], [baccA])
                ts(bufA[:, j, 0:30], bufA[:, j, T:T + 30], 1.0, ALU.mult, [bbufA[j]], [bbufA[j]])
            ab, bab = B4.next()
            sq, bsq = B4.next()
            for j in range(4):
                act(ab[:, j, :], accA[:, j, :], AF.Copy, [baccA], [bab])
                act(sq[:, j, :], accA[:, j, :], AF.Square, [baccA], [bsq])
            bm, be = stats(ab, bab, sq, bsq, 4, ones[:, 2, :], T)
            mean, bmean, rstd, brstd = ln_finish(bm, be, T, eps_ln)
            yA, byA = B4.next()
            for j in range(4):
                d, bd = F1.next()
                tt(d[:], accA[:, j, :], mean[:], ALU.subtract, [baccA, bmean], [bd])
                tt(d[:], d[:], rstd[:], ALU.mult, [bd, brstd], [bd])
                act(yA[:, j, :], d[:], AF.Silu, [bd, bvecs], [byA], scale=vcol("cln_g", j), bias=vcol("cln_b", j))

            bG = proj_group(wi, 8 * 128, 512, ht, bht, T)
            yB, byB = B4.next()
            for j in range(4):
                acc, bacc = F1.next()
                for k in range(3):
                    wk = vcol("sc_dw", j * 3 + k)
                    if k == 0:
                        ts(acc[:], bufB[:, j, 0:T], wk, ALU.mult, [bbufB[j], bvecs], [bacc])
                    else:
                        stt(acc[:], bufB[:, j, k:k + T], wk, acc[:], ALU.mult, ALU.add, [bbufB[j], bvecs], [bacc])
                ts(bufB[:, j, 0:2], bufB[:, j, T:T + 2], 1.0, ALU.mult, [bbufB[j]], [bbufB[j]])
                stt(yB[:, j, :], ps[:, bG[j], :], vcol("b_in", 8 + j), acc[:], ALU.add, ALU.mult,
                    [bps[bG[j]], bacc, bvecs], [byB])

            pd, bpd = B4.next()
            for g in range(4):
                w = 2 << g
                cur = bufP[:, g, :]
                bcur = bbufP[g]
                span = 16 + T
                sh = 1
                srcs = (cur, bcur)
                while sh < w:
                    tmpP, btmpP = PW.next()
                    tt(tmpP[:, sh:span], srcs[0][:, sh:span], srcs[0][:, 0:span - sh], ALU.add, [srcs[1]], [btmpP])
                    srcs = (tmpP, btmpP)
                    sh *= 2
                o, bo = F1.next()
                stt(o[:], srcs[0][:, 16:16 + T], 1.0 / w, bufP[:, g, 16:16 + T], ALU.mult, ALU.subtract,
                    [srcs[1], bbufP[g]], [bo])
                if n == 0:
                    for t_ in range(w - 1):
                        stt(o[:, t_:t_ + 1], srcs[0][:, 16 + t_:17 + t_], rc2[:, g * 16 + t_:g * 16 + t_ + 1],
                            bufP[:, g, 16 + t_:17 + t_], ALU.mult, ALU.subtract, [srcs[1], bbufP[g], brc2], [bo])
                act(pd[:, g, :], o[:], AF.Copy, [bo], [bpd])
                ts(bufP[:, g, 0:16], bufP[:, g, T:T + 16], 1.0, ALU.mult, [bbufP[g]], [bbufP[g]])
            wpt, bwp = WS.next()
            dma("sp", wpt[:, 0:4, 0:128], wslot("w_pool", 0), [bwfull[l]], [bwp])
            yD, byD = B4.next()
            for g in range(4):
                b = nps()
                mm(ps[:, b, :], [(wpt[:, g, 0:128], pd[:, g, :])], [bwp, bpd], [bps[b]])
                act(yD[:, g, :], ps[:, b, :], AF.Copy, [bps[b], bvecs], [byD], scale=vcol("pool_scale", g))

            for bi, (wname, ysrc, bysrc) in enumerate([("w_conv_out", yA, byA), ("w_sc_out", yB, byB),
                                                      ("w_pool_out", yD, byD)]):
                gate_b = (0, 1, 3)[bi]
                for half in range(2):
                    bo_ = proj_group(wname, half * 512, 512, ysrc, bysrc, T, kparts=4)
                    bg_ = proj_group(wi, (29 + gate_b * 8 + half * 4) * 128, 512, ht, bht, T)
                    for ci in range(4):
                        m = half * 4 + ci
                        gt, bgt = F1.next()
                        act(gt[:], ps[:, bg_[ci], :], AF.Sigmoid, [bps[bg_[ci]], bvecs], [bgt],
                            bias=vcol("b_in", 29 + gate_b * 8 + m))
                        if bi == 0:
                            tt(merged[:, m, :], gt[:], ps[:, bo_[ci], :], ALU.mult, [bgt, bps[bo_[ci]]], [bmerged[m]])
                        else:
                            tt(gt[:], gt[:], ps[:, bo_[ci], :], ALU.mult, [bgt, bps[bo_[ci]]], [bgt])
                            tt(merged[:, m, :], merged[:, m, :], gt[:], ALU.add, [bgt, bmerged[m]], [bmerged[m]],
                               eng="pool")
            dma("pool", mg_d[:, n * T:(n + 1) * T].rearrange("(k p) t -> p k t", p=128), merged[:], bmerged, [bmg])
        phase1_tile(-1)
        for n in range(NT):
            phase1_tile(n)
            if l + 1 < DEPTH:
                gather_weights(l + 1, n, NT)

        S.barrier()
        AR.reset()
        kvnT = AR.alloc([128, NK], BF16)
        kT = AR.alloc([128, NK], BF16)
        vaug = AR.alloc([128, NKB, 128], BF16)
        qh = Rot(nc, AR, "qh", [128, TPC], BF16, 2)
        PT = Rot(nc, AR, "pt", [128, 2, T], BF16, 3)
        F1 = Rot(nc, AR, "f1", [128, T], F32, 6)
        B1 = Rot(nc, AR, "b1", [128, T], BF16, 3)
        for r in range(NKC):
            kb0 = r * (TPC // 128)
            kb1 = (r + 1) * (TPC // 128)
            S.add("pool", lambda e, kb0=kb0, kb1=kb1: e.memset(vaug[:, kb0:kb1, 64:128], 1.0), [], [bvo])
            if r < NKC - 1:
                S.add("pool", lambda e, kb0=kb0, kb1=kb1, r=r: e.tensor_scalar(
                    out=vaug[:, kb0:kb1, 64:128], in0=vaug[:, kb0:kb1, 64:128], scalar1=cst[:, 3 + r:4 + r],
                    scalar2=None, op0=ALU.mult), [bcst, bvo], [bvo])
        for r in range(NKC):
            for n in range(NT):
                cs = slice(r * TPC + n * T, r * TPC + (n + 1) * T)
                if r < NKC - 1:
                    r0 = (n * CPS + r) * 160
                    dma("pool", kvnT[:, cs], kv_all[r0:r0 + 128, :], [bkv_all[n]], [bkvnT[r]])
                    dma("pool", kT[64:96, cs], kv_all[r0 + 128:r0 + 160, :], [bkv_all[n]], [bkTr])
                else:
                    dma("pool", kvnT[:, cs], kv_src[n * 160:n * 160 + 128, :], [bkv_src[n]], [bkvnT[r]])
                    dma("pool", kT[64:96, cs], kv_src[n * 160 + 128:(n + 1) * 160, :], [bkv_src[n]], [bkTr])
        dma("sp", wukv[:], wslot("w_ukv", 0)[:, 0, :], [bwfull[l]], [bwukv])
        OB = (4, 5)
        oi = 0
        pending_epi = []
        for h in range(NH):
            for kb5 in range(NK // 512):
                b = 6 + (kb5 % 2)
                mm(ps[0:64, b, :], [(wukv[:, h * 128:h * 128 + 64], kvnT[:, kb5 * 512:(kb5 + 1) * 512])],
                   [bwukv] + bkvnT, [bps[b]])
                act(kT[0:64, kb5 * 512:(kb5 + 1) * 512], ps[0:64, b, :], AF.Copy, [bps[b]], [bkTn])
            for kb8 in range(NKB // 8):
                b = 6 + (kb8 % 2)
                for i8 in range(8):
                    kb = kb8 * 8 + i8
                    mm(ps[:, b, i8 * 64:(i8 + 1) * 64], [(kvnT[:, kb * 128:(kb + 1) * 128],
                                                          wukv[:, h * 128 + 64:(h + 1) * 128])],
                       [bwukv] + bkvnT, [bps[b]])
                r = (kb8 * 8 * 128) // TPC
                src = ps[:, b, :].rearrange("p (a d) -> p a d", d=64)
                if r < NKC - 1:
                    act(vaug[:, kb8 * 8:(kb8 + 1) * 8, 0:64], src, AF.Copy, [bps[b], bcst], [bvv],
                        scale=cst[:, 3 + r:4 + r])
                else:
                    act(vaug[:, kb8 * 8:(kb8 + 1) * 8, 0:64], src, AF.Copy, [bps[b]], [bvv])
            qt, bqt = qh.next()
            dma("pool", qt[0:96, :], q_d[h * 96:(h + 1) * 96, :], [bq], [bqt])
            for n in range(NT):
                kbl = [(kb, 0, False) for kb in range((NKC - 1) * (TPC // 128))]
                own0 = (NKC - 1) * (TPC // 128)
                for kbo in range(4 * n + 4):
                    i = kbo - 4 * n
                    if i < 0:
                        kbl.append((own0 + kbo, 0, False))
                    else:
                        kbl.append((own0 + kbo, i * 128, True))
                ob = OB[oi % 2]
                oi += 1
                nblk = len(kbl)
                pairs = [kbl[i:i + 2] for i in range(0, nblk, 2)]
                npairs = len(pairs)

                def emit_S(pi):
                    sb0 = (pi % 2) * 2
                    for e_, (kb, c0, dg) in enumerate(pairs[pi]):
                        mm(ps[:, sb0 + e_, c0:T], [(kT[0:96, kb * 128:(kb + 1) * 128], qt[0:96, n * T + c0:(n + 1) * T])],
                           [bkTn, bkTr, bqt], [bps[sb0 + e_]])

                emit_S(0)
                if pending_epi:
                    pending_epi.pop(0)()
                for pi in range(npairs):
                    pair = pairs[pi]
                    sb0 = (pi % 2) * 2
                    if pi + 1 < npairs:
                        emit_S(pi + 1)
                    pt, bpt = PT.next()
                    if len(pair) == 2 and pair[0][1] == 0 and pair[1][1] == 0:
                        act(pt[:, 0:2, :], ps[:, sb0:sb0 + 2, :], AF.Exp, [bps[sb0], bps[sb0 + 1]], [bpt], scale=SCALE)
                    else:
                        for e_, (kb, c0, dg) in enumerate(pair):
                            act(pt[:, e_, c0:T], ps[:, sb0 + e_, c0:T], AF.Exp, [bps[sb0 + e_]], [bpt], scale=SCALE)
                    for e_, (kb, c0, dg) in enumerate(pair):
                        if dg:
                            tt(pt[:, e_, c0:c0 + 128], pt[:, e_, c0:c0 + 128], tri[:], ALU.mult, [bpt, bconst], [bpt],
                               eng="pool")
                    for e_, (kb, c0, dg) in enumerate(pair):
                        gi = pi * 2 + e_
                        S.add("pe", lambda e, ob=ob, kb=kb, c0=c0, pt=pt, e_=e_, st=(gi == 0), sp_=(gi == nblk - 1):
                              e.matmul(ps[:, ob, c0:T], lhsT=vaug[:, kb, :], rhs=pt[:, e_, c0:T], start=st, stop=sp_),
                              [bvv, bvo, bpt], [bps[ob]])
                def epilogue(ob=ob, oi=oi, h=h, n=n):
                    osb, bosb = F1.next()
                    act(osb[:], ps[:, ob, :], AF.Copy, [bps[ob]], [bosb])
                    bd_ = 6 + (oi % 2)
                    mm(ps[0:64, bd_, :], [(shiftm[:], osb[:])], [bosb, bconst], [bps[bd_]])
                    rd, brd = F1.next()
                    S.add("dve", lambda e, rd=rd, bd_=bd_: e.reciprocal(out=rd[0:64, :], in_=ps[0:64, bd_, :]),
                          [bps[bd_]], [brd])
                    ao, bao = B1.next()
                    tt(ao[0:64, :], osb[0:64, :], rd[0:64, :], ALU.mult, [bosb, brd], [bao])
                    dma("pool", at_d[h * 64:(h + 1) * 64, n * T:(n + 1) * T], ao[0:64, :], [bao], [bat])
                pending_epi.append(epilogue)
        while pending_epi:
            pending_epi.pop(0)()

        S.barrier()
        AR.reset()
        XT = Rot(nc, AR, "xt", [128, KD, T], F32, 2)
        F8 = Rot(nc, AR, "f8", [128, KD, T], F32, 2)
        F1 = Rot(nc, AR, "f1", [128, T], F32, 6)
        B4 = Rot(nc, AR, "b4", [128, 4, T], BF16, 2)
        B8 = Rot(nc, AR, "b8", [128, KD, T], BF16, 4)
        WS = Rot(nc, AR, "ws", [128, KD, 512], BF16, 3)
        for n in range(NT):
            cs = slice(n * T, (n + 1) * T)
            at, bat_t = B4.next()
            dma("pool", at[:], at_d[:, cs].rearrange("(k p) t -> p k t", p=128), [bat], [bat_t])
            mgt, bmgt = F8.next()
            dma("pool", mgt[:], mg_d[:, cs].rearrange("(k p) t -> p k t", p=128), [bmg], [bmgt])
            g2t, bg2t = B8.next()
            dma("pool", g2t[:], g2_d[:, cs].rearrange("(k p) t -> p k t", p=128), [bg2], [bg2t])
            xt, bxt = XT.next()
            dma("pool", xt[:], xA[:, cs].rearrange("(k p) t -> p k t", p=128), [bxA], [bxt])
            mb, bmb = B8.next()
            for half in range(2):
                bo_ = proj_group("w_mla_out", half * 512, 512, at, bat_t, T, kparts=4)
                for ci in range(4):
                    m = half * 4 + ci
                    tmp, btmp = F1.next()
                    tt(tmp[:], g2t[:, m, :], ps[:, bo_[ci], :], ALU.mult, [bg2t, bps[bo_[ci]]], [btmp])
                    tt(mb[:, m, :], tmp[:], mgt[:, m, :], ALU.add, [btmp, bmgt], [bmb], eng="pool")
            x1, bx1 = F8.next()
            for half in range(2):
                bo_ = proj_group("w_o", half * 512, 512, mb, bmb, T)
                for ci in range(4):
                    m = half * 4 + ci
                    tmp, btmp = F1.next()
                    act(tmp[:], ps[:, bo_[ci], :], AF.Copy, [bps[bo_[ci]], bmod], [btmp], scale=mod[:, 16 + m:17 + m])
                    stt(x1[:, m, :], xt[:, m, :], ALPHA, tmp[:], ALU.mult, ALU.add, [bxt, btmp], [bx1])
            xo, bxo = XT.next()
            layer_norm_full(x1, bx1, "ln1_g", "ln1_b", T, lambda k, xo=xo: xo[:, k, :], bxo)
            dma("pool", xM[:, cs].rearrange("(k p) t -> p k t", p=128), xo[:], [bxo], [bxM])

        exchange_tail(xM, bxM)

        def ffn_tile(n):
            halo = n < 0
            N = HALO if halo else T
            if halo:
                xt, bxt = xhalo, bxhalo
            else:
                xt, bxt = XT.next()
                dma("pool", xt[:], xM[:, n * T:(n + 1) * T].rearrange("(k p) t -> p k t", p=128), [bxM], [bxt])
            ht, bht = modulate(xt, bxt, N, 24)
            vals = [None, None]
            for g in range(NUP // 4):
                banks = proj_group("w_up", g * 512, 512, ht, bht, N)
                for ci in range(4):
                    c = g * 4 + ci
                    if halo:
                        ts(uph[:, c, :], ps[:, banks[ci], 30:32], cst[:, 2:3], ALU.mult, [bps[banks[ci]], bcst], [buph])
                        continue
                    ub, bub = upb.next()
                    act(ub[:, 2:2 + T], ps[:, banks[ci], :], AF.Copy, [bps[banks[ci]]], [bub])
                    ts(ub[:, 0:2], uph[:, c, :], 1.0, ALU.mult, [buph], [bub], eng="pool")
                    ts(uph[:, c, :], ub[:, T:T + 2], 1.0, ALU.mult, [bub], [buph], eng="pool")
                    acc, bacc = F1.next()
                    for k in range(3):
                        wk = vcol("ffn_dw", c * 3 + k)
                        if k == 0:
                            ts(acc[:], ub[:, 0:T], wk, ALU.mult, [bub, bvecs], [bacc])
                        else:
                            stt(acc[:], ub[:, k:k + T], wk, acc[:], ALU.mult, ALU.add, [bub, bvecs], [bacc])
                    if ci < 2:
                        vals[ci] = (acc, bacc)
                    else:
                        i = g * 2 + ci - 2
                        sg, bsg = F1.next()
                        act(sg[:], acc[:], AF.Silu, [bacc], [bsg])
                        tt(ffin[:, i, :], sg[:], vals[ci - 2][0][:], ALU.mult, [bsg, vals[ci - 2][1]], [bffin[i]])
            if halo:
                return
            x2, bx2 = F8.next()
            for half in range(2):
                pb = []
                for ci in range(4):
                    pb.append(nps())
                kgs = [(0, 8), (8, 8), (16, 6)]
                wts = []
                for (k0, kn) in kgs:
                    wt, bw = wload("w_down", half * 512, 512, k0, kn)
                    wts.append((wt, bw, k0, kn))
                for ci in range(4):
                    pairs = []
                    rd_ = []
                    for (wt, bw, k0, kn) in wts:
                        for k in range(kn):
                            pairs.append((wt[:, k, ci * 128:(ci + 1) * 128], ffin[:, k0 + k, :]))
                        rd_.append(bw)
                    mm(ps[:, pb[ci], :], pairs, rd_ + bffin, [bps[pb[ci]]])
                    m = half * 4 + ci
                    tmp, btmp = F1.next()
                    act(tmp[:], ps[:, pb[ci], :], AF.Copy, [bps[pb[ci]], bmod], [btmp], scale=mod[:, 40 + m:41 + m])
                    stt(x2[:, m, :], xt[:, m, :], ALPHA, tmp[:], ALU.mult, ALU.add, [bxt, btmp], [bx2])
            xo, bxo = XT.next()
            layer_norm_full(x2, bx2, "ln2_g", "ln2_b", T, lambda k, xo=xo: xo[:, k, :], bxo)
            if l == DEPTH - 1:
                dma("pool", out_T[:, n * T:(n + 1) * T].rearrange("(k p) t -> p k t", p=128), xo[:], [bxo], [Buf("o")])
            else:
                dma("pool", xA[:, n * T:(n + 1) * T].rearrange("(k p) t -> p k t", p=128), xo[:], [bxo], [bxA])

        S.barrier()
        AR.reset()
        XT = Rot(nc, AR, "xt", [128, KD, T], F32, 2)
        HT = Rot(nc, AR, "ht", [128, KD, T], BF16, 1)
        WS = Rot(nc, AR, "ws", [128, KD, 512], BF16, 4)
        upb = Rot(nc, AR, "upb", [128, 2 + T], F32, 4)
        F1 = Rot(nc, AR, "f1", [128, T], F32, 8)
        F8 = Rot(nc, AR, "f8", [128, KD, T], F32, 1)
        B8 = Rot(nc, AR, "b8", [128, KD, T], BF16, 2)
        ffin = AR.alloc([128, NFF, T], BF16)
        ffn_tile(-1)
        for n in range(NT):
            ffn_tile(n)

    S.emit(nc, es)
    es.close()
    return nc


DEPTH_FULL = 4
WCOLS = 2048
WLAY = [("w_ada", D, 6 * D), ("w_in", D, NCH_IN * 128), ("w_conv_out", 512, D), ("w_sc_out", 512, D),
        ("w_uq", 256, 2048), ("w_ukv", 128, 1024), ("w_mla_out", 512, D), ("w_pool", 512, 128),
        ("w_pool_out", 512, D), ("w_o", D, D), ("w_up", D, 2 * DFF), ("w_down", DFF, D)]
WOFF = {}
_o = 0
for _n, _r, _c in WLAY:
    WOFF[_n] = (_o, _r, _c)
    _o += _r * _c
WTOT = _o


WCH = 256
WGROUPS = {}
for _n, _r, _c in WLAY:
    if _n == "w_in":
        _g = [(i * 512, 512) for i in range(6)] + [(3072, 128), (3200, 512)] + [(3712 + 512 * i, 512) for i in range(8)]
    elif _n == "w_ukv":
        _g = [(0, 1024)]
    elif _n == "w_pool":
        _g = [(0, 128)]
    else:
        _g = [(i * 512, 512) for i in range(_c // 512)]
    assert sum(w for _, w in _g) == _c, _n
    WGROUPS[_n] = _g
WSLOT = {}
_o = 0
for _n, _r, _c in WLAY:
    for (_c0, _w) in WGROUPS[_n]:
        WSLOT[(_n, _c0)] = (_o, _r // 128, _w)
        _o += _r * _w
assert _o == WTOT


def wrows(n_cores):
    per = n_cores * WCH * WCOLS
    return (WTOT + per - 1) // per * n_cores * WCH
DEBUG = False
DEBUG_NAMES = ("xM", "mg_d", "g2_d", "q_d", "at_d", "cc_d", "ss_d", "kv_dbg")
LAST = {}


def _pt(v, nchunk):
    return np.ascontiguousarray(np.asarray(v, np.float32).reshape(nchunk, 128).T)


def prep_weights(inp, DEPTH, n_cores):
    f = lambda a: np.ascontiguousarray(np.asarray(a, dtype=np.float32))
    w_in = f(inp["w_in"]); b_in = f(inp["b_in"])
    idx = np.zeros(NCH_IN * 128, np.int64)
    valid = np.zeros(NCH_IN * 128, bool)

    def put(chunk, cols, off=0):
        idx[chunk * 128 + off: chunk * 128 + off + len(cols)] = cols
        valid[chunk * 128 + off: chunk * 128 + off + len(cols)] = True
    put(0, np.arange(0, 512)); put(4, np.arange(512, 1024)); put(8, np.arange(1024, 1536))
    put(12, np.arange(1536, 2048)); put(16, np.arange(2048, 2560)); put(20, np.arange(2560, 2816))
    put(22, np.arange(2816, 2944))
    kr = np.arange(2944, 2976)
    put(23, kr, 64)
    put(24, np.concatenate([kr[16:], kr[:16]]), 64)
    put(25, np.arange(2976, 3488))
    put(29, np.arange(3488, 7584))
    w_inP = np.where(valid[None, None, :], w_in[:, :, idx], 0.0).astype(np.float32)
    b_inP = np.where(valid[None, :], b_in[:, idx], 0.0).astype(np.float32)
    w_uq = f(inp["w_uq"])
    hid = np.arange(768).reshape(NH, 96)
    sw = np.concatenate([hid[:, :64], hid[:, 80:96], hid[:, 64:80]], axis=1).reshape(-1)
    w_uq_sw = w_uq[:, :, sw]
    w_uqP = np.zeros((w_uq.shape[0], 256, 2048), np.float32)
    for h in range(NH):
        for v, src_ in enumerate((w_uq, w_uq_sw)):
            s4 = (h // 4) * 2 + v
            c0 = s4 * 512 + (h % 4) * 128
            w_uqP[:, :, c0:c0 + 96] = src_[:, :, h * 96:(h + 1) * 96]
    upidx = []
    for g in range(NUP // 4):
        for ci in range(4):
            ch = (2 * g + ci) if ci < 2 else (NFF + 2 * g + ci - 2)
            upidx.append(np.arange(ch * 128, (ch + 1) * 128))
    upidx = np.concatenate(upidx)
    w_upP = f(inp["w_up"])[:, :, upidx]
    ffn_dwP = f(inp["ffn_dw"])[:, :, upidx]
    vecs = np.zeros((DEPTH, 128, NV), np.float32)

    def setv(name, l, arr):
        vecs[l, :, VEC[name]:VEC[name] + arr.shape[1]] = arr
    for l in range(DEPTH):
        setv("b_ada", l, _pt(inp["b_ada"][l], 48))
        setv("b_in", l, _pt(b_inP[l], NCH_IN))
        setv("conv_dw", l, f(inp["conv_dw"][l]).T.reshape(4, 128, 31).transpose(1, 0, 2).reshape(128, 124))
        setv("cln_g", l, _pt(inp["conv_ln_g"][l], 4)); setv("cln_b", l, _pt(inp["conv_ln_b"][l], 4))
        setv("sc_dw", l, f(inp["sc_dw"][l]).T.reshape(4, 128, 3).transpose(1, 0, 2).reshape(128, 12))
        setv("qn_g", l, _pt(inp["q_norm_g"][l], 2)); setv("kvn_g", l, _pt(inp["kv_norm_g"][l], 1))
        setv("pool_scale", l, _pt(inp["pool_scale"][l], 4))
        setv("ln1_g", l, _pt(inp["ln1_g"][l], 8)); setv("ln1_b", l, _pt(inp["ln1_b"][l], 8))
        setv("ln2_g", l, _pt(inp["ln2_g"][l], 8)); setv("ln2_b", l, _pt(inp["ln2_b"][l], 8))
        setv("ffn_dw", l, ffn_dwP[l].T.reshape(NUP, 128, 3).transpose(1, 0, 2).reshape(128, NUP * 3))
    Wd = {"w_ada": f(inp["w_ada"]), "w_in": w_inP, "w_conv_out": f(inp["w_conv_out"]),
          "w_sc_out": f(inp["w_sc_out"]), "w_uq": w_uqP, "w_ukv": f(inp["w_ukv"]), "w_mla_out": f(inp["w_mla_out"]),
          "w_pool": f(inp["w_pool"]), "w_pool_out": f(inp["w_pool_out"]), "w_o": f(inp["w_o"]),
          "w_up": w_upP, "w_down": f(inp["w_down"])}
    RT = wrows(n_cores)
    blob = np.zeros((DEPTH, RT * WCOLS), np.float32)
    for n_, r_, c_ in WLAY:
        o_ = WOFF[n_][0]
        blob[:, o_:o_ + r_ * c_] = Wd[n_][:DEPTH].reshape(DEPTH, r_ * c_)
    return vecs, blob.reshape(DEPTH, RT, WCOLS)


def run_model(inp, NB, CPS, TPC, DEPTH, layers_per_launch=None):
    n_cores = NB * CPS
    lpl = layers_per_launch or DEPTH
    x = np.asarray(inp["x"], np.float32)
    c = np.asarray(inp["c"], np.float32)
    pos = np.asarray(inp["positions"], np.int32)
    vecs_h, blob_full = prep_weights(inp, DEPTH, n_cores)
    inv = (1.0 / (10000.0 ** (np.arange(0, 32, 2, dtype=np.float32) / np.float32(32)))).astype(np.float32)
    tri = (np.arange(128)[:, None] <= np.arange(128)[None, :]).astype(np.float32)
    shift = np.zeros((128, 64), np.float32)
    shift[64 + np.arange(64), np.arange(64)] = 1.0
    base_maps = []
    for core in range(n_cores):
        b, j = core // CPS, core % CPS
        sl = slice(j * TPC, (j + 1) * TPC)
        cst = np.zeros((128, 16), np.float32)
        cst[64:80, 0] = inv; cst[80:96, 0] = inv
        cst[64:80, 1] = -1.0; cst[80:96, 1] = 1.0
        cst[:, 2] = 1.0 if j > 0 else 0.0
        for r in range(3):
            cst[:, 3 + r] = 1.0 if r < j else 0.0
        for r in range(CPS):
            cst[:, 6 + r] = 1.0 if r == j - 1 else 0.0
        rc = np.zeros((128, 4, 16), np.float32)
        m = {"xT": np.ascontiguousarray(x[b, sl, :].T),
             "posr": np.ascontiguousarray(np.broadcast_to(pos[b, sl][None, :], (128, TPC))).astype(np.int32),
             "cT": _pt(c[b], KD), "cst": cst, "rcnt": rc.reshape(128, 64), "tri": tri, "shift": shift}
        base_maps.append(m)
    nc = build_program(CPS, TPC, lpl, n_cores, alpha_depth=DEPTH)
    res = None
    for l0 in range(0, DEPTH, lpl):
        in_maps = []
        for core in range(n_cores):
            m = dict(base_maps[core])
            if res is not None:
                m["xT"] = np.ascontiguousarray(res.results[core]["outT"])
            m["vecs"] = np.ascontiguousarray(vecs_h[l0:l0 + lpl])
            m["wfull"] = blob_full[l0:l0 + lpl]
            in_maps.append(m)
        res = run_bass_kernel_spmd(nc, in_maps, core_ids=list(range(n_cores)))
    if DEBUG:
        LAST["res"] = res.results
    out = np.zeros((NB, CPS * TPC, D), np.float32)
    for core in range(n_cores):
        b, j = core // CPS, core % CPS
        out[b, j * TPC:(j + 1) * TPC, :] = res.results[core]["outT"].T
    return out


LAYERS_PER_LAUNCH = 4


def kernel(**inputs):
    return run_model(inputs, NB=2, CPS=4, TPC=4096, DEPTH=4, layers_per_launch=LAYERS_PER_LAUNCH)
```
